# Optimizing a Trainium2 kernel written in Bass

```python
import math
import jax
import jax.numpy as jnp
from jax import lax
import numpy as np

D_MODEL = 1024
BATCH = 4
SEQ = 4096
DEPTH = 2

CHUNK = 64
QBLK = 128
HEAD_DIM = 64
NORM_EPS = 1e-6

A_HEADS = 4
A_QK_WIDTH = A_HEADS * 2 * HEAD_DIM
A_VDIM = 2 * HEAD_DIM
A_WIDTH = A_HEADS * A_VDIM

B_HEADS = 8
B_WIDTH = B_HEADS * HEAD_DIM

C_HEADS = 8
C_KV_HEADS = 2
C_GROUP = C_HEADS // C_KV_HEADS
C_WIDTH = C_HEADS * HEAD_DIM
C_KV_WIDTH = C_KV_HEADS * HEAD_DIM
WINDOW = 128
WIN_CHUNKS = WINDOW // CHUNK

D_HEADS = 8
D_WIDTH = D_HEADS * HEAD_DIM
D_LEFT_CHUNKS = 8
D_BAND = (D_LEFT_CHUNKS + 1) * CHUNK
REL_MAX = 256
REL_SIZE = REL_MAX + CHUNK

EVEN_SPLITS = (A_QK_WIDTH, A_QK_WIDTH, A_WIDTH, A_WIDTH, B_WIDTH, B_WIDTH, B_WIDTH, B_WIDTH, B_HEADS)
ODD_SPLITS = (C_WIDTH, C_KV_WIDTH, C_KV_WIDTH, C_WIDTH, D_WIDTH, D_WIDTH, D_WIDTH, D_WIDTH)
P_EVEN = sum(EVEN_SPLITS)
P_ODD = sum(ODD_SPLITS)
MIX_EVEN = A_WIDTH + B_WIDTH
MIX_ODD = C_WIDTH + D_WIDTH

kernel_name = "hybrid_chunk_causal_attn_trunk"


def rms_norm(x, g):
    xf = x.astype(jnp.float32)
    y = xf * lax.rsqrt(jnp.mean(xf * xf, axis=-1, keepdims=True) + NORM_EPS)
    return (y * g.astype(jnp.float32)).astype(x.dtype)


def split_cols(z, sizes):
    idx = np.cumsum(np.array(sizes))[:-1].tolist()
    return jnp.split(z, idx, axis=-1)


def alibi_slopes(n_heads):
    return 2.0 ** (-8.0 * jnp.arange(1, n_heads + 1, dtype=jnp.float32) / n_heads)


def sweep_query_blocks(fn, n_blocks):
    out = lax.map(fn, jnp.arange(n_blocks))
    out = jnp.moveaxis(out, 0, 1)
    return out.reshape((out.shape[0], -1) + out.shape[3:])


def diff_attention(q, k, v, lam, lam_init, subln_g):
    bsz, seq = q.shape[:2]
    scale = HEAD_DIM ** -0.5
    slopes = alibi_slopes(A_HEADS)
    tk = jnp.arange(seq)

    def block(b):
        start = b * QBLK
        tq = start + jnp.arange(QBLK)
        qb = lax.dynamic_slice_in_dim(q, start, QBLK, axis=1)
        s = jnp.einsum("bqhcd,bkhcd->bhcqk", qb, k).astype(jnp.float32) * scale
        dist = jnp.abs(tq[:, None] - tk[None, :]).astype(jnp.float32)
        s = s - (slopes[:, None, None] * dist)[None, :, None]
        allowed = (tk[None, :] // CHUNK) <= (tq[:, None] // CHUNK)
        s = jnp.where(allowed, s, -jnp.inf)
        p = jax.nn.softmax(s, axis=-1)
        w = p[:, :, 0] - lam * p[:, :, 1]
        return jnp.einsum("bhqk,bkhe->bqhe", w.astype(v.dtype), v)

    o = sweep_query_blocks(block, seq // QBLK)
    o = rms_norm(o, subln_g) * (1.0 - lam_init)
    return o.reshape(bsz, seq, A_WIDTH)


def forgetting_attention(q, k, v, log_f):
    bsz, seq = q.shape[:2]
    scale = HEAD_DIM ** -0.5
    cum = jnp.transpose(jnp.cumsum(log_f, axis=1), (0, 2, 1))
    tk = jnp.arange(seq)

    def block(b):
        start = b * QBLK
        tq = start + jnp.arange(QBLK)
        qb = lax.dynamic_slice_in_dim(q, start, QBLK, axis=1)
        cq = lax.dynamic_slice_in_dim(cum, start, QBLK, axis=2)
        s = jnp.einsum("bqhd,bkhd->bhqk", qb, k).astype(jnp.float32) * scale
        s = s + cq[..., :, None] - cum[:, :, None, :]
        s = jnp.where(tk[None, :] <= tq[:, None], s, -jnp.inf)
        p = jax.nn.softmax(s, axis=-1)
        return jnp.einsum("bhqk,bkhd->bqhd", p.astype(v.dtype), v)

    o = sweep_query_blocks(block, seq // QBLK)
    return o.reshape(bsz, seq, B_WIDTH)


def sliding_window_sink_attention(q, k, v, sinks):
    bsz, seq = q.shape[:2]
    nb = seq // QBLK
    scale = HEAD_DIM ** -0.5

    def band(t):
        tp = jnp.pad(t, ((0, 0), (QBLK, 0), (0, 0), (0, 0)))
        tb = tp.reshape(bsz, nb + 1, QBLK, C_KV_HEADS, HEAD_DIM)
        return jnp.concatenate([tb[:, :-1], tb[:, 1:]], axis=2)

    kb, vb = band(k), band(v)
    qb = q.reshape(bsz, nb, QBLK, C_KV_HEADS, C_GROUP, HEAD_DIM)
    s = jnp.einsum("bnqkgd,bnskd->bnkgqs", qb, kb).astype(jnp.float32) * scale
    iq = jnp.arange(QBLK)
    ik = jnp.arange(2 * QBLK) - QBLK
    dist = jnp.abs(iq[:, None] - ik[None, :]).astype(jnp.float32)
    slopes = alibi_slopes(C_HEADS).reshape(C_KV_HEADS, C_GROUP, 1, 1)
    s = s - (slopes * dist)[None, None]
    chunk_diff = (iq[:, None] // CHUNK + QBLK // CHUNK) - ((ik[None, :] + QBLK) // CHUNK)
    in_window = (chunk_diff >= 0) & (chunk_diff <= WIN_CHUNKS)
    valid = (jnp.arange(nb)[:, None] * QBLK + ik[None, :]) >= 0
    mask = in_window[None] & valid[:, None, :]
    s = jnp.where(mask[None, :, None, None], s, -jnp.inf)
    sink = jnp.broadcast_to(sinks.astype(jnp.float32).reshape(1, 1, C_KV_HEADS, C_GROUP, 1, 1),
                            s.shape[:-1] + (1,))
    p = jax.nn.softmax(jnp.concatenate([s, sink], axis=-1), axis=-1)[..., :-1]
    o = jnp.einsum("bnkgqs,bnskd->bnqkgd", p.astype(v.dtype), vb)
    return o.reshape(bsz, seq, C_WIDTH)


def chunk_relpos_attention(q, k, v, rel_table):
    bsz, seq = q.shape[:2]
    nc = seq // CHUNK
    scale = HEAD_DIM ** -0.5
    idx = jnp.arange(nc)[:, None] + jnp.arange(D_LEFT_CHUNKS + 1)[None, :]

    def band(t):
        tp = jnp.pad(t, ((0, 0), (D_LEFT_CHUNKS * CHUNK, 0), (0, 0), (0, 0)))
        tp = tp.reshape(bsz, nc + D_LEFT_CHUNKS, CHUNK, D_HEADS, HEAD_DIM)
        return tp[:, idx].reshape(bsz, nc, D_BAND, D_HEADS, HEAD_DIM)

    kb, vb = band(k), band(v)
    qc = q.reshape(bsz, nc, CHUNK, D_HEADS, HEAD_DIM)
    s = jnp.einsum("bnqhd,bnkhd->bnhqk", qc, kb).astype(jnp.float32) * scale
    iq = jnp.arange(CHUNK)
    ik = jnp.arange(D_BAND) - D_LEFT_CHUNKS * CHUNK
    rel = iq[:, None] - ik[None, :]
    ridx = jnp.clip(rel, -(CHUNK - 1), REL_MAX) + (CHUNK - 1)
    bias = rel_table[:, ridx].astype(jnp.float32)
    s = s + bias[None, None]
    valid = (jnp.arange(nc)[:, None] * CHUNK + ik[None, :]) >= 0
    s = jnp.where(valid[None, :, None, None, :], s, -jnp.inf)
    p = jax.nn.softmax(s, axis=-1)
    o = jnp.einsum("bnhqk,bnkhd->bnqhd", p.astype(v.dtype), vb)
    return o.reshape(bsz, seq, D_WIDTH)


def even_layer(x, ln_g, w_in, w_out, a_qn_g, a_kn_g, a_lq1, a_lk1, a_lq2, a_lk2, a_subln_g,
               b_qn_g, b_kn_g, b_f_bias, layer_idx):
    bsz, seq, _ = x.shape
    z = rms_norm(x, ln_g) @ w_in
    aq, ak, av, ag, bq, bk, bv, bg, bf = split_cols(z, EVEN_SPLITS)
    aq = rms_norm(aq.reshape(bsz, seq, A_HEADS, 2, HEAD_DIM), a_qn_g)
    ak = rms_norm(ak.reshape(bsz, seq, A_HEADS, 2, HEAD_DIM), a_kn_g)
    av = av.reshape(bsz, seq, A_HEADS, A_VDIM)
    lam_init = 0.8 - 0.6 * math.exp(-0.3 * layer_idx)
    f32 = jnp.float32
    lam = (jnp.exp(jnp.sum(a_lq1.astype(f32) * a_lk1.astype(f32)))
           - jnp.exp(jnp.sum(a_lq2.astype(f32) * a_lk2.astype(f32))) + lam_init)
    a_out = diff_attention(aq, ak, av, lam, lam_init, a_subln_g)
    bq = rms_norm(bq.reshape(bsz, seq, B_HEADS, HEAD_DIM), b_qn_g)
    bk = rms_norm(bk.reshape(bsz, seq, B_HEADS, HEAD_DIM), b_kn_g)
    bv = bv.reshape(bsz, seq, B_HEADS, HEAD_DIM)
    log_f = jax.nn.log_sigmoid(bf.astype(f32) + b_f_bias.astype(f32))
    b_out = forgetting_attention(bq, bk, bv, log_f)
    mixed = jnp.concatenate([a_out * jax.nn.silu(ag), b_out * jax.nn.silu(bg)], axis=-1)
    return x + mixed @ w_out


def odd_layer(x, ln_g, w_in, w_out, c_qn_g, c_kn_g, c_sinks, d_qn_g, d_kn_g, d_rel_bias):
    bsz, seq, _ = x.shape
    z = rms_norm(x, ln_g) @ w_in
    cq, ck, cv, cg, dq, dk, dv, dg = split_cols(z, ODD_SPLITS)
    cq = rms_norm(cq.reshape(bsz, seq, C_HEADS, HEAD_DIM), c_qn_g)
    ck = rms_norm(ck.reshape(bsz, seq, C_KV_HEADS, HEAD_DIM), c_kn_g)
    cv = cv.reshape(bsz, seq, C_KV_HEADS, HEAD_DIM)
    c_out = sliding_window_sink_attention(cq, ck, cv, c_sinks)
    dq = rms_norm(dq.reshape(bsz, seq, D_HEADS, HEAD_DIM), d_qn_g)
    dk = rms_norm(dk.reshape(bsz, seq, D_HEADS, HEAD_DIM), d_kn_g)
    dv = dv.reshape(bsz, seq, D_HEADS, HEAD_DIM)
    d_out = chunk_relpos_attention(dq, dk, dv, d_rel_bias)
    mixed = jnp.concatenate([c_out * jax.nn.silu(cg), d_out * jax.nn.silu(dg)], axis=-1)
    return x + mixed @ w_out


def setup_inputs(seed: int = 0) -> dict:
    key = jax.random.key(seed)
    ne = (DEPTH + 1) // 2
    no = DEPTH // 2
    ks = jax.random.split(key, 23)

    def nrm(k, shape, s):
        return s * jax.random.normal(k, shape, jnp.float32)

    def gain(k, shape):
        return 1.0 + 0.1 * jax.random.normal(k, shape, jnp.float32)

    return {
        "x": jax.random.normal(ks[0], (BATCH, SEQ, D_MODEL), jnp.float32),
        "even_ln_g": gain(ks[1], (ne, D_MODEL)),
        "even_w_in": nrm(ks[2], (ne, D_MODEL, P_EVEN), D_MODEL ** -0.5),
        "even_w_out": nrm(ks[3], (ne, MIX_EVEN, D_MODEL), MIX_EVEN ** -0.5),
        "a_q_norm_g": gain(ks[4], (ne, HEAD_DIM)),
        "a_k_norm_g": gain(ks[5], (ne, HEAD_DIM)),
        "a_lambda_q1": nrm(ks[6], (ne, HEAD_DIM), 0.1),
        "a_lambda_k1": nrm(ks[7], (ne, HEAD_DIM), 0.1),
        "a_lambda_q2": nrm(ks[8], (ne, HEAD_DIM), 0.1),
        "a_lambda_k2": nrm(ks[9], (ne, HEAD_DIM), 0.1),
        "a_subln_g": gain(ks[10], (ne, A_VDIM)),
        "b_q_norm_g": gain(ks[11], (ne, HEAD_DIM)),
        "b_k_norm_g": gain(ks[12], (ne, HEAD_DIM)),
        "b_forget_bias": 3.0 + nrm(ks[13], (ne, B_HEADS), 0.5),
        "odd_ln_g": gain(ks[14], (no, D_MODEL)),
        "odd_w_in": nrm(ks[15], (no, D_MODEL, P_ODD), D_MODEL ** -0.5),
        "odd_w_out": nrm(ks[16], (no, MIX_ODD, D_MODEL), MIX_ODD ** -0.5),
        "c_q_norm_g": gain(ks[17], (no, HEAD_DIM)),
        "c_k_norm_g": gain(ks[18], (no, HEAD_DIM)),
        "c_sinks": nrm(ks[19], (no, C_HEADS), 0.5),
        "d_q_norm_g": gain(ks[20], (no, HEAD_DIM)),
        "d_k_norm_g": gain(ks[21], (no, HEAD_DIM)),
        "d_rel_bias": nrm(ks[22], (no, D_HEADS, REL_SIZE), 0.5),
    }


def reference(x, even_ln_g, even_w_in, even_w_out, a_q_norm_g, a_k_norm_g, a_lambda_q1,
              a_lambda_k1, a_lambda_q2, a_lambda_k2, a_subln_g, b_q_norm_g, b_k_norm_g,
              b_forget_bias, odd_ln_g, odd_w_in, odd_w_out, c_q_norm_g, c_k_norm_g, c_sinks,
              d_q_norm_g, d_k_norm_g, d_rel_bias):
    for i in range(DEPTH):
        j = i // 2
        if i % 2 == 0:
            x = even_layer(x, even_ln_g[j], even_w_in[j], even_w_out[j], a_q_norm_g[j],
                           a_k_norm_g[j], a_lambda_q1[j], a_lambda_k1[j], a_lambda_q2[j],
                           a_lambda_k2[j], a_subln_g[j], b_q_norm_g[j], b_k_norm_g[j],
                           b_forget_bias[j], i)
        else:
            x = odd_layer(x, odd_ln_g[j], odd_w_in[j], odd_w_out[j], c_q_norm_g[j],
                          c_k_norm_g[j], c_sinks[j], d_q_norm_g[j], d_k_norm_g[j],
                          d_rel_bias[j])
    return x
```

```python
import contextlib
import numpy as np
import ml_dtypes
import concourse.bass as bass
import concourse.mybir as mybir
from concourse.bass_utils import run_bass_kernel_spmd

F32 = mybir.dt.float32
BF16 = mybir.dt.bfloat16
ALU = mybir.AluOpType
AF = mybir.ActivationFunctionType
AX = mybir.AxisListType

S = 4096
DM = 1024
NT = 32
NBLK = 8
EPS = 1e-6
NEG = -30000.0
P_EVEN = 4104
P_ODD = 3328
NPAR = 290
PC_LNG0, PC_LNG1 = 0, 8
PC_AQ, PC_AK, PC_SUB, PC_BQ, PC_BK, PC_CQ, PC_CK, PC_DQ, PC_DK, PC_FB = 16, 17, 18, 19, 20, 21, 22, 23, 24, 25
PC_SINK = 26
PC_LQ1, PC_LK1, PC_LQ2, PC_LK2 = 34, 98, 162, 226


class Buf:
    __slots__ = ("w", "r", "dsem", "dcnt", "name")

    def __init__(self, name=""):
        self.w = {}
        self.r = {}
        self.dsem = None
        self.dcnt = 0
        self.name = name


def _merge(d, s):
    for k, (sem, v) in s.items():
        if k not in d or d[k][1] < v:
            d[k] = (sem, v)


class KB:
    def __init__(self, nc):
        self.nc = nc
        self.stack = contextlib.ExitStack()
        self.eng = {}
        for name, e in (("pe", nc.tensor), ("act", nc.scalar), ("dve", nc.vector),
                        ("pool", nc.gpsimd), ("sp", nc.sync)):
            sem = self.stack.enter_context(nc.semaphore("s_" + name))
            self.eng[name] = dict(e=e, sem=sem, cnt=0, waited={}, name=name)
        self.dbufs = []
        self.nsem = 5

    def _deps(self, reads, writes):
        d = {}
        for b in reads:
            _merge(d, b.w)
        for b in writes:
            _merge(d, b.w)
            _merge(d, b.r)
        return d

    def _wait(self, E, deps, skip_key=None):
        for key, (sem, val) in deps.items():
            if key == skip_key:
                continue
            if E["name"] == "pe" and key == id(self.eng["pe"]["sem"]):
                continue
            if E["waited"].get(key, 0) < val:
                E["e"].wait_ge(sem, val)
                E["waited"][key] = val

    def op(self, en, fn, reads=(), writes=()):
        E = self.eng[en]
        self._wait(E, self._deps(reads, writes))
        ins = fn(E["e"])
        E["cnt"] += 1
        ins.then_inc(E["sem"], 1)
        key = id(E["sem"])
        tok = (E["sem"], E["cnt"])
        for b in reads:
            b.r[key] = tok
        for b in writes:
            b.w[key] = tok
            b.r = {}
        return tok

    def dma(self, qn, out, in_, reads=(), writes=(), par=False):
        E = self.eng[qn]
        b = writes[0]
        if b.dsem is None:
            b.dsem = self.stack.enter_context(self.nc.semaphore("d%d" % self.nsem))
            self.nsem += 1
            self.dbufs.append(b)
        key = id(b.dsem)
        self._wait(E, self._deps(reads, writes), skip_key=key if par else None)
        ins = E["e"].dma_start(out=out, in_=in_)
        b.dcnt += 16
        ins.then_inc(b.dsem, 16)
        tok = (b.dsem, b.dcnt)
        for rb in reads:
            rb.r[key] = tok
        for wb in writes:
            wb.w[key] = tok
            wb.r = {}
        return tok

    def barrier(self):
        deps = {}
        for F in self.eng.values():
            if F["cnt"] > 0:
                deps[id(F["sem"])] = (F["sem"], F["cnt"])
        for b in self.dbufs:
            deps[id(b.dsem)] = (b.dsem, b.dcnt)
        for E in self.eng.values():
            for key, (sem, val) in deps.items():
                if key == id(E["sem"]):
                    continue
                if E["waited"].get(key, 0) < val:
                    E["e"].wait_ge(sem, val)
                    E["waited"][key] = val


def mm(pe, out, lhsT, rhs, start, stop, skip=False):
    return pe.matmul(out, lhsT=lhsT, rhs=rhs, start=start, stop=stop, skip_group_check=skip)


def units_l0():
    us = []
    for h in range(4):
        us.append(dict(kind="A", qcol=h * 128, kcol=512 + h * 128, kw=128, vcol=1024 + h * 128, vw=128,
                       gcol=1536 + h * 128, mix0=h * 128, heads=(h, h), pq=PC_AQ, pk=PC_AK))
    for p in range(4):
        us.append(dict(kind="B", qcol=2048 + p * 128, kcol=2560 + p * 128, kw=128, vcol=3072 + p * 128, vw=128,
                       gcol=3584 + p * 128, mix0=512 + p * 128, heads=(2 * p, 2 * p + 1), pq=PC_BQ, pk=PC_BK))
    return us


def units_l1():
    us = []
    for j in range(2):
        for p in range(2):
            us.append(dict(kind="C", qcol=j * 256 + p * 128, kcol=512 + j * 64, kw=64, vcol=640 + j * 64, vw=64,
                           gcol=768 + j * 256 + p * 128, mix0=j * 256 + p * 128,
                           heads=(4 * j + 2 * p, 4 * j + 2 * p + 1), pq=PC_CQ, pk=PC_CK))
    for p in range(4):
        us.append(dict(kind="D", qcol=1280 + p * 128, kcol=1792 + p * 128, kw=128, vcol=2304 + p * 128, vw=128,
                       gcol=2816 + p * 128, mix0=512 + p * 128, heads=(2 * p, 2 * p + 1), pq=PC_DQ, pk=PC_DK))
    return us


def tiles_for(kind, qb):
    out = []
    if kind in ("A", "B"):
        for kt in range(4 * qb + 4):
            if kt < 4 * qb:
                out.append((kt, 0, 512, None))
            else:
                j = kt - 4 * qb
                out.append((kt, 128 * j, 512, 0))
        return out
    span = 256 if kind == "C" else 640
    back = 1 if kind == "C" else 4
    kts = [kt for kt in range(4 * qb - back, 4 * qb + 4) if kt >= 0]
    for kt in kts:
        lo = max(128 * kt, 512 * qb)
        hi = min(128 * kt + span, 512 * qb + 512)
        if hi > lo:
            out.append((kt, lo - 512 * qb, hi - 512 * qb, lo - 128 * kt))
    return out


def build(debug=False, nlayers=2, max_units=None):
    nc = bass.Bass("TRN2", target_bir_lowering=False)
    kb = KB(nc)
    st = kb.stack

    def dram(name, shape, dt, kind="ExternalInput"):
        return nc.dram_tensor(name, shape, dt, kind=kind).ap()

    x_d = dram("x", [S, DM], F32)
    win_d = [dram("w_in0", [DM, P_EVEN], F32), dram("w_in1", [DM, P_ODD], F32)]
    wout_d = [dram("w_out0", [DM, DM], F32), dram("w_out1", [DM, DM], F32)]
    par_d = dram("params", [128, NPAR], F32)
    ident_d = dram("ident", [128, 128], BF16)
    bones_d = dram("bones", [128, 128], BF16)
    aones_d = dram("aones", [128, 128], BF16)
    augA_d = dram("augA", [4, 12, S], BF16)
    onesrow_d = dram("onesrow", [8, S], BF16)
    corrA_d = dram("corrA", [4, 128, 128], F32)
    corrB_d = dram("corrB", [128, 128], F32)
    corrC_d = dram("corrC", [8, 128, 256], F32)
    relD_d = dram("relD", [8, 128, 640], F32)
    maskD_d = dram("maskD", [128, 640], F32)
    y_d = dram("y", [S, DM], F32, kind="ExternalOutput")
    skind = "ExternalOutput" if debug else "Internal"
    mixT_d = dram("mixT_scr", [DM, S], BF16, kind=skind)
    x1_d = dram("x1_scr", [S, DM], F32, kind=skind)
    augB_d = dram("augB_scr", [8, 12, S], BF16, kind=skind)

    uniq = [0]

    def sb(stack, name, shape, dt):
        uniq[0] += 1
        return stack.enter_context(nc.sbuf_tensor("%s_%d" % (name, uniq[0]), shape, dt))

    xnT = sb(st, "xnT", [128, 8, S], BF16)
    xnT_b = [Buf("xnT%d" % i) for i in range(NBLK)]
    ident = sb(st, "ident_sb", [128, 128], BF16)
    bones = sb(st, "bones_sb", [128, 128], BF16)
    aones = sb(st, "aones_sb", [128, 128], BF16)
    par = sb(st, "par_sb", [128, NPAR], F32)
    der = sb(st, "der_sb", [128, 32], F32)
    esink = sb(st, "esink_sb", [128, 8], F32)
    cst_b = Buf("consts")
    par_b = Buf("par")
    der_b = Buf("der")
    ps = [st.enter_context(nc.psum_tensor("ps%d" % i, [128, 512], F32)) for i in range(8)]
    ps_b = [Buf("ps%d" % i) for i in range(8)]
    psrr = [0]

    def psum_next(allowed=range(8)):
        allowed = list(allowed)
        i = allowed[psrr[0] % len(allowed)]
        psrr[0] += 1
        return ps[i], ps_b[i]

    mixblk_b = [Buf("mixblk%d" % i) for i in range(NBLK)]
    x1_b = Buf("x1scr")
    y_b = Buf("y")
    augB_b = Buf("augB")

    kb.dma("sp", ident[:], ident_d[:, :], writes=[cst_b])
    kb.dma("sp", bones[:], bones_d[:, :], writes=[cst_b], par=True)
    kb.dma("sp", aones[:], aones_d[:, :], writes=[cst_b], par=True)
    kb.dma("sp", par[:], par_d[:, :], writes=[par_b])

    DQ = {PC_AQ: 0, PC_BQ: 1, PC_CQ: 2, PC_DQ: 3}
    for pc, dc_ in DQ.items():
        kb.op("dve", lambda e, pc=pc, dc_=dc_: e.tensor_scalar(out=der[:, dc_:dc_ + 1], in0=par[:, pc:pc + 1], scalar1=0.125,
                                                             scalar2=None, op0=ALU.mult), reads=[par_b], writes=[der_b])
    kb.op("dve", lambda e: e.tensor_scalar(out=der[:, 4:5], in0=par[:, PC_SUB:PC_SUB + 1], scalar1=0.8, scalar2=None,
                                           op0=ALU.mult), reads=[par_b], writes=[der_b])
    kb.op("dve", lambda e: e.tensor_scalar(out=der[:, 6:7], in0=par[:, PC_FB:PC_FB + 1], scalar1=-1.0, scalar2=None,
                                           op0=ALU.mult), reads=[par_b], writes=[der_b])
    with contextlib.ExitStack() as s0:
        lt = sb(s0, "lamtmp", [128, 128], F32)
        lt_b = Buf("lt")
        kb.op("dve", lambda e: e.tensor_tensor(out=lt[:, 0:64], in0=par[:, PC_LQ1:PC_LQ1 + 64], in1=par[:, PC_LK1:PC_LK1 + 64],
                                               op=ALU.mult), reads=[par_b], writes=[lt_b])
        kb.op("dve", lambda e: e.tensor_tensor(out=lt[:, 64:128], in0=par[:, PC_LQ2:PC_LQ2 + 64], in1=par[:, PC_LK2:PC_LK2 + 64],
                                               op=ALU.mult), reads=[par_b], writes=[lt_b])
        kb.op("dve", lambda e: e.reduce_sum(out=der[:, 8:9], in_=lt[:, 0:64], axis=AX.X), reads=[lt_b], writes=[der_b])
        kb.op("dve", lambda e: e.reduce_sum(out=der[:, 9:10], in_=lt[:, 64:128], axis=AX.X), reads=[lt_b], writes=[der_b])
        kb.op("act", lambda e: e.activation(out=der[:, 10:12], in_=der[:, 8:10], func=AF.Exp), reads=[der_b], writes=[der_b])
        kb.op("dve", lambda e: e.tensor_tensor(out=der[:, 12:13], in0=der[:, 11:12], in1=der[:, 10:11], op=ALU.subtract),
              reads=[der_b], writes=[der_b])
        kb.op("dve", lambda e: e.tensor_scalar(out=der[:, 5:6], in0=der[:, 12:13], scalar1=-0.2, scalar2=None, op0=ALU.add),
              reads=[der_b], writes=[der_b])
        kb.op("act", lambda e: e.activation(out=esink[:, :], in_=par[:, PC_SINK:PC_SINK + 8], func=AF.Exp), reads=[par_b],
              writes=[der_b])
        kb.barrier()

    def norm_tile(nb, xs, xs_b, tt):
        i = tt % 2
        sq, sq_b = nb["sq"][i], nb["sq_b"][i]
        sm, sm_b = nb["sm"][i], nb["sm_b"][i]
        xb, xb_b = nb["xb"][i], nb["xb_b"][i]
        kb.op("act", lambda e: e.activation(out=sq[:, :], in_=xs, func=AF.Square), reads=[xs_b], writes=[sq_b])
        kb.op("dve", lambda e: e.reduce_sum(out=sm[:, 0:1], in_=sq[:, :], axis=AX.X), reads=[sq_b], writes=[sm_b])
        kb.op("act", lambda e: e.activation(out=sm[:, 1:2], in_=sm[:, 0:1], func=AF.Ln, scale=1.0 / DM, bias=EPS),
              reads=[sm_b], writes=[sm_b])
        kb.op("act", lambda e: e.activation(out=sm[:, 2:3], in_=sm[:, 1:2], func=AF.Exp, scale=-0.5), reads=[sm_b], writes=[sm_b])
        kb.op("dve", lambda e: e.tensor_scalar(out=xb[:, :], in0=xs, scalar1=sm[:, 2:3], scalar2=None, op0=ALU.mult),
              reads=[xs_b, sm_b], writes=[xb_b])
        pt, pt_b = psum_next()
        ptv = pt[:, :].bitcast(BF16)

        def tr(pe):
            ins = None
            for dc in range(8):
                ins = pe.transpose(ptv[:, dc * 128:(dc + 1) * 128], xb[:, dc * 128:(dc + 1) * 128], ident[:, :])
            return ins
        kb.op("pe", tr, reads=[xb_b, cst_b], writes=[pt_b])
        tb = tt // 4
        kb.op("act", lambda e: e.copy(out=xnT[:, :, tt * 128:(tt + 1) * 128],
                                      in_=ptv.rearrange("p (c t) -> p c t", c=8)),
              reads=[pt_b], writes=[xnT_b[tb]])

    def norm_bufs(stack, pfx):
        nb = dict(sq=[], sq_b=[], sm=[], sm_b=[], xb=[], xb_b=[])
        for i in range(2):
            nb["sq"].append(sb(stack, pfx + "sq%d" % i, [128, DM], F32))
            nb["sq_b"].append(Buf())
            nb["sm"].append(sb(stack, pfx + "sm%d" % i, [128, 4], F32))
            nb["sm_b"].append(Buf())
            nb["xb"].append(sb(stack, pfx + "xb%d" % i, [128, DM], BF16))
            nb["xb_b"].append(Buf())
        return nb

    with contextlib.ExitStack() as s1:
        nb = norm_bufs(s1, "n0")
        xst = [sb(s1, "xst%d" % i, [128, DM], F32) for i in range(2)]
        xst_b = [Buf() for _ in range(2)]
        for tt in range(NT):
            i = tt % 2
            kb.dma("sp", xst[i][:, :], x_d[tt * 128:(tt + 1) * 128, :], writes=[xst_b[i]])
            norm_tile(nb, xst[i][:, :], xst_b[i], tt)
        kb.barrier()

    def units_phase(layer):
        units = units_l0() if layer == 0 else units_l1()
        if max_units is not None:
            units = units[:max_units] if isinstance(max_units, int) else [units[i] for i in max_units]
        lng = PC_LNG0 if layer == 0 else PC_LNG1
        win = win_d[layer].rearrange("(c p) n -> p c n", p=128)
        KK = 70 if layer == 0 else 64
        with contextlib.ExitStack() as s2:
            qT = [sb(s2, "qT%d" % m, [128, S], BF16) for m in range(2)]
            kT = [sb(s2, "kT%d" % m, [128, S], BF16) for m in range(2)]
            qT_b = [Buf("qT%d" % m) for m in range(2)]
            kT_b = [Buf("kT%d" % m) for m in range(2)]
            Vaug = sb(s2, "Vaug", [128, NT, 2, 128], BF16)
            Vaug_b = Buf("Vaug")
            sg = sb(s2, "sg", [128, S], F32)
            sg_b = Buf("sg")
            wbf = sb(s2, "wbf", [128, 8, 512], BF16)
            wbf_b = Buf("wbf")
            wst = [sb(s2, "wst%d" % i, [128, 8, 128], F32) for i in range(2)]
            wst_b = [Buf() for _ in range(2)]
            Pt = [sb(s2, "P%d" % i, [128, 512], BF16) for i in range(4)]
            Pt_b = [Buf() for _ in range(4)]
            ncorr = 640 if layer == 1 else 128
            corr = [sb(s2, "corr%d" % m, [128, ncorr], F32) for m in range(2)]
            corr_b = [Buf() for _ in range(2)]
            if layer == 1:
                maskD = sb(s2, "maskD", [128, 640], F32)
                maskD_b = Buf()
                kb.dma("sp", maskD[:, :], maskD_d[:, :], writes=[maskD_b])
            accS = [sb(s2, "accS%d" % i, [128, 512], F32) for i in range(4)]
            accS_b = [Buf() for _ in range(4)]
            rl = [sb(s2, "rl%d" % i, [128, 512], F32) for i in range(2)]
            rl_b = [Buf() for _ in range(2)]
            on = [sb(s2, "on%d" % i, [128, 512], F32) for i in range(2)]
            on_b = [Buf() for _ in range(2)]
            sqt = sb(s2, "sqt", [128, 512], BF16)
            sqt_b = Buf()
            lnv = sb(s2, "lnv", [128, 512], F32)
            lnv_b = Buf()
            rstd = sb(s2, "rstd", [128, 512], F32)
            rstd_b = Buf()
            mx = [sb(s2, "mx%d" % i, [128, 512], BF16) for i in range(2)]
            mx_b = [Buf() for _ in range(2)]

            kb.op("pool", lambda e: e.memset(Vaug[:, :, 0, 64:128], 1.0), writes=[Vaug_b])
            kb.op("pool", lambda e: e.memset(Vaug[:, :, 1, 0:64], 1.0), writes=[Vaug_b])

            def prep_weights(u):
                groups = [(u["qcol"], 128), (u["kcol"], u["kw"]), (u["vcol"], u["vw"]), (u["gcol"], 128)]
                for gi, (col, wd) in enumerate(groups):
                    i = gi % 2
                    kb.dma("sp", wst[i][:, :, 0:wd], win[:, :, col:col + wd], writes=[wst_b[i]])
                    for dc in range(8):
                        kb.op("pool", lambda e, i=i, dc=dc, gi=gi, wd=wd: e.tensor_scalar(
                            out=wbf[:, dc, gi * 128:gi * 128 + wd], in0=wst[i][:, dc, 0:wd],
                            scalar1=par[:, lng + dc:lng + dc + 1], scalar2=None, op0=ALU.mult),
                            reads=[wst_b[i], par_b], writes=[wbf_b])

            if units:
                prep_weights(units[0])
            for ui, u in enumerate(units):
                kind = u["kind"]
                if layer == 0:
                    for m in range(2):
                        if kind == "A":
                            srcq = augA_d[u["heads"][0], 0:6, :]
                            srck = augA_d[u["heads"][0], 6:12, :]
                            rd = []
                        else:
                            srcq = augB_d[u["heads"][m], 0:6, :]
                            srck = augB_d[u["heads"][m], 6:12, :]
                            rd = [augB_b]
                        kb.dma("sp", qT[m][64:70, :], srcq, reads=rd, writes=[qT_b[m]])
                        kb.dma("sp", kT[m][64:70, :], srck, reads=rd, writes=[kT_b[m]])
                    if kind == "A":
                        kb.dma("sp", corr[0][:, 0:128], corrA_d[u["heads"][0], :, :], writes=[corr_b[0]])
                    else:
                        kb.dma("sp", corr[0][:, 0:128], corrB_d[:, :], writes=[corr_b[0]])
                else:
                    for m in range(2):
                        h = u["heads"][m]
                        if kind == "C":
                            kb.dma("sp", corr[m][:, 0:256], corrC_d[h, :, :], writes=[corr_b[m]])
                        else:
                            kb.dma("sp", corr[m][:, 0:640], relD_d[h, :, :], writes=[corr_b[m]])
                            kb.op("pool", lambda e, m=m: e.tensor_tensor(out=corr[m][:, :], in0=corr[m][:, :], in1=maskD[:, :],
                                                                         op=ALU.add), reads=[maskD_b], writes=[corr_b[m]])
                dq = DQ[u["pq"]]
                for tb in range(NBLK):
                    t0 = tb * 512
                    xb_ = xnT_b[tb]

                    def proj(pdst, c0, wd, t0=t0):
                        def f(pe):
                            ins = None
                            for dc in range(8):
                                ins = mm(pe, pdst[0:wd, :], wbf[:, dc, c0:c0 + wd], xnT[:, dc, t0:t0 + 512], dc == 0, dc == 7)
                            return ins
                        return f

                    def qk_norm(pq_, pq_b_, R, gcolap, dst, dst_b, kmap, t0=t0):
                        kb.op("act", lambda e: e.activation(out=sqt[0:R, :], in_=pq_[0:R, :], func=AF.Square),
                              reads=[pq_b_], writes=[sqt_b])
                        pss, pss_b = psum_next()
                        kb.op("pe", lambda pe: mm(pe, pss[0:R, :], bones[0:R, 0:R], sqt[0:R, :], True, True),
                              reads=[sqt_b, cst_b], writes=[pss_b])
                        kb.op("act", lambda e: e.activation(out=lnv[0:R, :], in_=pss[0:R, :], func=AF.Ln, scale=1.0 / 64, bias=EPS),
                              reads=[pss_b], writes=[lnv_b])
                        kb.op("act", lambda e: e.activation(out=rstd[0:R, :], in_=lnv[0:R, :], func=AF.Exp, scale=-0.5),
                              reads=[lnv_b], writes=[rstd_b])
                        for m in range(R // 64):
                            rows = slice(64 * m, 64 * m + 64)
                            kb.op("dve", lambda e, m=m, rows=rows: e.scalar_tensor_tensor(
                                out=dst[m][0:64, t0:t0 + 512], in0=pq_[rows, :], scalar=gcolap(rows), in1=rstd[rows, :],
                                op0=ALU.mult, op1=ALU.mult), reads=[pq_b_, rstd_b, par_b, der_b], writes=[dst_b[m]])

                    pq_, pq_b_ = psum_next()
                    kb.op("pe", proj(pq_, 0, 128), reads=[wbf_b, xb_], writes=[pq_b_])
                    qk_norm(pq_, pq_b_, 128, lambda rows: der[rows, dq:dq + 1], qT, qT_b, None)
                    pk_, pk_b_ = psum_next()
                    kb.op("pe", proj(pk_, 128, u["kw"]), reads=[wbf_b, xb_], writes=[pk_b_])
                    qk_norm(pk_, pk_b_, u["kw"], lambda rows: par[rows, u["pk"]:u["pk"] + 1], kT, kT_b, None)
                    pg_, pg_b_ = psum_next()
                    kb.op("pe", proj(pg_, 384, 128), reads=[wbf_b, xb_], writes=[pg_b_])
                    kb.op("act", lambda e: e.activation(out=sg[:, t0:t0 + 512], in_=pg_[:, :], func=AF.Silu),
                          reads=[pg_b_], writes=[sg_b])
                    pv_, pv_b_ = psum_next()
                    vw = u["vw"]

                    def vproj(pe, t0=t0, vw=vw):
                        ins = None
                        for j in range(4):
                            for dc in range(8):
                                ins = mm(pe, pv_[:, j * vw:(j + 1) * vw], xnT[:, dc, t0 + j * 128:t0 + (j + 1) * 128],
                                         wbf[:, dc, 256:256 + vw], dc == 0, dc == 7)
                        return ins
                    kb.op("pe", vproj, reads=[wbf_b, xb_], writes=[pv_b_])
                    pvv = pv_[:, 0:4 * vw].rearrange("p (j c) -> p j c", j=4)
                    if vw == 128:
                        kb.op("dve", lambda e: e.tensor_copy(out=Vaug[:, 4 * tb:4 * tb + 4, 0, 0:64], in_=pvv[:, :, 0:64]),
                              reads=[pv_b_], writes=[Vaug_b])
                        kb.op("dve", lambda e: e.tensor_copy(out=Vaug[:, 4 * tb:4 * tb + 4, 1, 64:128], in_=pvv[:, :, 64:128]),
                              reads=[pv_b_], writes=[Vaug_b])
                    else:
                        kb.op("dve", lambda e: e.tensor_copy(out=Vaug[:, 4 * tb:4 * tb + 4, 0, 0:64], in_=pvv[:, :, 0:64]),
                              reads=[pv_b_], writes=[Vaug_b])
                        kb.op("dve", lambda e: e.tensor_copy(out=Vaug[:, 4 * tb:4 * tb + 4, 1, 64:128], in_=pvv[:, :, 0:64]),
                              reads=[pv_b_], writes=[Vaug_b])

                if ui + 1 < len(units):
                    prep_weights(units[ui + 1])
                if kind == "A":
                    maps = [dict(q=0, k=0, pv=[(0, 0), (1, 1)], corr=0), dict(q=1, k=1, pv=[(0, 2), (1, 3)], corr=0)]
                    nacc = 4
                elif kind == "C":
                    maps = [dict(q=0, k=0, pv=[(0, 0)], corr=0), dict(q=1, k=0, pv=[(1, 1)], corr=1)]
                    nacc = 2
                else:
                    maps = [dict(q=0, k=0, pv=[(0, 0)], corr=0 if layer == 0 else 0), dict(q=1, k=1, pv=[(1, 1)], corr=0 if layer == 0 else 1)]
                    nacc = 2
                sbanks = list(range(nacc, 8))
                nS = len(sbanks)
                skew = 2
                for qb in range(NBLK):
                    Q0 = qb * 512
                    items = []
                    for (kt, q0, q1, c0) in tiles_for(kind, qb):
                        for mi, mp in enumerate(maps):
                            items.append((kt, q0, q1, c0, mi))
                    started = set()

                    def emit_qk(w):
                        kt, q0, q1, c0, mi = items[w]
                        mp = maps[mi]
                        bi = sbanks[w % nS]
                        Sx, Sb = ps[bi], ps_b[bi]
                        kb.op("pe", lambda pe: mm(pe, Sx[:, q0:q1], kT[mp["k"]][0:KK, kt * 128:(kt + 1) * 128],
                                                  qT[mp["q"]][0:KK, Q0 + q0:Q0 + q1], True, True),
                              reads=[kT_b[mp["k"]], qT_b[mp["q"]]], writes=[Sb])
                        if c0 is not None:
                            cm = mp["corr"]
                            if layer == 0:
                                a0, a1 = q0, q0 + 128
                                cap = corr[cm][:, 0:128]
                            else:
                                a0, a1 = q0, q1
                                cap = corr[cm][:, c0:c0 + (q1 - q0)]
                            kb.op("dve", lambda e: e.tensor_tensor(out=Sx[:, a0:a1], in0=Sx[:, a0:a1], in1=cap, op=ALU.add),
                                  reads=[corr_b[cm]], writes=[Sb])
                        pi = w % 4
                        kb.op("act", lambda e: e.activation(out=Pt[pi][:, q0:q1], in_=Sx[:, q0:q1], func=AF.Exp),
                              reads=[Sb], writes=[Pt_b[pi]])

                    def emit_pv(w):
                        kt, q0, q1, c0, mi = items[w]
                        mp = maps[mi]
                        pi = w % 4
                        for (slot, ai) in mp["pv"]:
                            first = ai not in started
                            started.add(ai)
                            kb.op("pe", lambda pe, slot=slot, ai=ai, first=first: mm(
                                pe, ps[ai][:, q0:q1], Vaug[:, kt, slot, :], Pt[pi][:, q0:q1], first, True, skip=True),
                                reads=[Pt_b[pi], Vaug_b], writes=[ps_b[ai]])

                    n = len(items)
                    for w in range(n + skew):
                        if w < n:
                            emit_qk(w)
                        if w - skew >= 0:
                            emit_pv(w - skew)

                    for a in range(nacc):
                        kb.op("dve", lambda e, a=a: e.tensor_copy(out=accS[a][:, :], in_=ps[a][:, :]), reads=[ps_b[a]],
                              writes=[accS_b[a]])
                    mxi = qb % 2
                    if kind == "A":
                        for a in range(4):
                            c, half = a // 2, a % 2
                            orow = slice(0, 64) if half == 0 else slice(64, 128)
                            lrow = slice(64, 128) if half == 0 else slice(0, 64)
                            kb.op("dve", lambda e, a=a, c=c, orow=orow, lrow=lrow: e.reciprocal(out=rl[c][orow, :], in_=accS[a][lrow, :]),
                                  reads=[accS_b[a]], writes=[rl_b[c]])
                            kb.op("dve", lambda e, a=a, c=c, orow=orow: e.tensor_tensor(out=on[c][orow, :], in0=accS[a][orow, :],
                                                                                       in1=rl[c][orow, :], op=ALU.mult),
                                  reads=[accS_b[a], rl_b[c]], writes=[on_b[c]])
                        kb.op("dve", lambda e: e.scalar_tensor_tensor(out=rl[0][:, :], in0=on[1][:, :], scalar=der[:, 5:6], in1=on[0][:, :],
                                                                      op0=ALU.mult, op1=ALU.add),
                              reads=[on_b[0], on_b[1], der_b], writes=[rl_b[0]])
                        kb.op("act", lambda e: e.activation(out=sqt[:, :], in_=rl[0][:, :], func=AF.Square), reads=[rl_b[0]], writes=[sqt_b])
                        pss, pss_b = psum_next(sbanks)
                        kb.op("pe", lambda pe: mm(pe, pss[:, :], aones[:, :], sqt[:, :], True, True), reads=[sqt_b, cst_b], writes=[pss_b])
                        kb.op("act", lambda e: e.activation(out=lnv[:, :], in_=pss[:, :], func=AF.Ln, scale=1.0 / 128, bias=EPS),
                              reads=[pss_b], writes=[lnv_b])
                        kb.op("act", lambda e: e.activation(out=rstd[:, :], in_=lnv[:, :], func=AF.Exp, scale=-0.5),
                              reads=[lnv_b], writes=[rstd_b])
                        kb.op("dve", lambda e: e.scalar_tensor_tensor(out=on[0][:, :], in0=rl[0][:, :], scalar=der[:, 4:5], in1=rstd[:, :],
                                                                      op0=ALU.mult, op1=ALU.mult),
                              reads=[rl_b[0], rstd_b, der_b], writes=[on_b[0]])
                        kb.op("dve", lambda e: e.tensor_tensor(out=mx[mxi][:, :], in0=on[0][:, :], in1=sg[:, Q0:Q0 + 512], op=ALU.mult),
                              reads=[on_b[0], sg_b], writes=[mx_b[mxi]])
                    else:
                        for m in range(2):
                            orow = slice(0, 64) if m == 0 else slice(64, 128)
                            lrow = slice(64, 128) if m == 0 else slice(0, 64)
                            if kind == "C":
                                h = u["heads"][m]
                                kb.op("dve", lambda e, m=m, orow=orow, lrow=lrow, h=h: e.tensor_scalar(
                                    out=rl[0][orow, :], in0=accS[m][lrow, :], scalar1=esink[lrow, h:h + 1], scalar2=None, op0=ALU.add),
                                    reads=[accS_b[m], der_b], writes=[rl_b[0]])
                                kb.op("dve", lambda e, orow=orow: e.reciprocal(out=rl[0][orow, :], in_=rl[0][orow, :]),
                                      reads=[], writes=[rl_b[0]])
                            else:
                                kb.op("dve", lambda e, m=m, orow=orow, lrow=lrow: e.reciprocal(out=rl[0][orow, :], in_=accS[m][lrow, :]),
                                      reads=[accS_b[m]], writes=[rl_b[0]])
                            kb.op("dve", lambda e, m=m, orow=orow: e.tensor_tensor(out=on[0][orow, :], in0=accS[m][orow, :],
                                                                                  in1=rl[0][orow, :], op=ALU.mult),
                                  reads=[accS_b[m], rl_b[0]], writes=[on_b[0]])
                        kb.op("dve", lambda e: e.tensor_tensor(out=mx[mxi][:, :], in0=on[0][:, :], in1=sg[:, Q0:Q0 + 512], op=ALU.mult),
                              reads=[on_b[0], sg_b], writes=[mx_b[mxi]])
                    kb.dma("pool", mixT_d[u["mix0"]:u["mix0"] + 128, Q0:Q0 + 512], mx[mxi][:, :], reads=[mx_b[mxi]],
                           writes=[mixblk_b[qb]])
            kb.barrier()

    def bprep():
        win = win_d[0].rearrange("(c p) n -> p c n", p=128)
        with contextlib.ExitStack() as s3:
            wst = [sb(s3, "wstb", [128, 8, 8], F32)]
            wst_b = [Buf()]
            wfb = sb(s3, "wfb", [128, 8, 8], BF16)
            wfb_b = Buf()
            l1p = sb(s3, "l1p", [8, S], F32)
            l1p_b = Buf()
            ones8 = sb(s3, "ones8", [8, S], F32)
            ones8_b = Buf()
            cum = sb(s3, "cum", [8, S], F32)
            cum_b = Buf()
            parts = [sb(s3, "part%d" % i, [8, S], BF16) for i in range(3)]
            nparts = [sb(s3, "npart%d" % i, [8, S], BF16) for i in range(3)]
            parts_b = Buf()
            res = sb(s3, "resid", [8, S], F32)
            res_b = Buf()
            kb.dma("sp", wst[0][:, :, 0:8], win[:, :, 4096:4104], writes=[wst_b[0]])
            for dc in range(8):
                kb.op("pool", lambda e, dc=dc: e.tensor_scalar(out=wfb[:, dc, :], in0=wst[0][:, dc, 0:8],
                                                               scalar1=par[:, PC_LNG0 + dc:PC_LNG0 + dc + 1], scalar2=None, op0=ALU.mult),
                      reads=[wst_b[0], par_b], writes=[wfb_b])
            kb.op("pool", lambda e: e.memset(ones8[:, :], 1.0), writes=[ones8_b])
            ones8h = sb(s3, "ones8h", [8, S], BF16)
            kb.op("pool", lambda e: e.memset(ones8h[:, :], 1.0), writes=[ones8_b])
            for tb in range(NBLK):
                t0 = tb * 512
                pf, pf_b = psum_next()

                def f(pe, t0=t0):
                    ins = None
                    for dc in range(8):
                        ins = mm(pe, pf[0:8, :], wfb[:, dc, :], xnT[:, dc, t0:t0 + 512], dc == 0, dc == 7)
                    return ins
                kb.op("pe", f, reads=[wfb_b, xnT_b[tb]], writes=[pf_b])
                kb.op("act", lambda e, t0=t0: e.activation(out=l1p[:, t0:t0 + 512], in_=pf[0:8, :], func=AF.Exp, scale=-1.0,
                                                          bias=der[0:8, 6:7]), reads=[pf_b, der_b], writes=[l1p_b])
                kb.op("act", lambda e, t0=t0: e.activation(out=l1p[:, t0:t0 + 512], in_=l1p[:, t0:t0 + 512], func=AF.Ln, scale=1.0,
                                                          bias=1.0), reads=[], writes=[l1p_b])
            kb.op("dve", lambda e: e.tensor_tensor_scan(out=cum[:, :], data0=ones8[:, :], data1=l1p[:, :], initial=0.0,
                                                        op0=ALU.mult, op1=ALU.add), reads=[ones8_b, l1p_b], writes=[cum_b])
            kb.op("dve", lambda e: e.tensor_copy(out=parts[0][:, :], in_=cum[:, :]), reads=[cum_b], writes=[parts_b])
            kb.op("dve", lambda e: e.tensor_tensor(out=res[:, :], in0=cum[:, :], in1=parts[0][:, :], op=ALU.subtract),
                  reads=[cum_b], writes=[res_b, parts_b])
            kb.op("dve", lambda e: e.tensor_copy(out=parts[1][:, :], in_=res[:, :]), reads=[res_b], writes=[parts_b])
            kb.op("dve", lambda e: e.tensor_tensor(out=res[:, :], in0=res[:, :], in1=parts[1][:, :], op=ALU.subtract),
                  reads=[], writes=[res_b, parts_b])
            kb.op("dve", lambda e: e.tensor_copy(out=parts[2][:, :], in_=res[:, :]), reads=[res_b], writes=[parts_b])
            for i in range(3):
                kb.op("dve", lambda e, i=i: e.tensor_scalar(out=nparts[i][:, :], in0=parts[i][:, :], scalar1=-1.0, scalar2=None,
                                                            op0=ALU.mult), reads=[], writes=[parts_b])
            first = True
            for r in range(3):
                kb.dma("pool", augB_d[:, r, :], ones8h[:, :], reads=[ones8_b], writes=[augB_b], par=not first)
                first = False
                kb.dma("pool", augB_d[:, 9 + r, :], ones8h[:, :], reads=[ones8_b], writes=[augB_b], par=True)
                kb.dma("pool", augB_d[:, 3 + r, :], nparts[r][:, :], reads=[parts_b], writes=[augB_b], par=True)
                kb.dma("pool", augB_d[:, 6 + r, :], parts[r][:, :], reads=[parts_b], writes=[augB_b], par=True)
            kb.barrier()

    def outproj_phase(layer):
        res_src = x_d if layer == 0 else x1_d
        last = (layer == nlayers - 1)
        with contextlib.ExitStack() as s4:
            wo = sb(s4, "wo", [128, 8, DM], BF16)
            wo_b = Buf()
            wos = [sb(s4, "wos%d" % i, [128, DM], F32) for i in range(2)]
            wos_b = [Buf() for _ in range(2)]
            mt = [sb(s4, "mt%d" % i, [128, 8, 512], BF16) for i in range(2)]
            mt_b = [Buf() for _ in range(2)]
            xr = [sb(s4, "xr%d" % i, [128, DM], F32) for i in range(2)]
            xr_b = [Buf() for _ in range(2)]
            xo = [sb(s4, "xo%d" % i, [128, DM], F32) for i in range(2)]
            xo_b = [Buf() for _ in range(2)]
            nb = norm_bufs(s4, "n1") if not last else None
            wod = wout_d[layer].rearrange("(c p) n -> p c n", p=128)
            for mc in range(8):
                i = mc % 2
                kb.dma("sp", wos[i][:, :], wod[:, mc, :], writes=[wos_b[i]])
                kb.op("pool", lambda e, i=i, mc=mc: e.tensor_copy(out=wo[:, mc, :], in_=wos[i][:, :]), reads=[wos_b[i]], writes=[wo_b])
            mixv = mixT_d.rearrange("(c p) t -> p c t", p=128)
            for tb in range(NBLK):
                bi = tb % 2
                kb.dma("sp", mt[bi][:, :, :], mixv[:, :, tb * 512:(tb + 1) * 512], reads=[mixblk_b[tb]], writes=[mt_b[bi]])
                for j in range(4):
                    tt = 4 * tb + j
                    i = tt % 2
                    rd = [x1_b] if layer == 1 else []
                    kb.dma("sp", xr[i][:, :], res_src[tt * 128:(tt + 1) * 128, :], reads=rd, writes=[xr_b[i]])
                    for half in range(2):
                        po, po_b = psum_next()

                        def f(pe, po=po, half=half, j=j, bi=bi):
                            ins = None
                            for mc in range(8):
                                ins = mm(pe, po[:, :], mt[bi][:, mc, j * 128:(j + 1) * 128], wo[:, mc, half * 512:(half + 1) * 512],
                                         mc == 0, mc == 7)
                            return ins
                        kb.op("pe", f, reads=[mt_b[bi], wo_b], writes=[po_b])
                        kb.op("dve", lambda e, po=po, half=half, i=i: e.tensor_tensor(
                            out=xo[i][:, half * 512:(half + 1) * 512], in0=po[:, :], in1=xr[i][:, half * 512:(half + 1) * 512], op=ALU.add),
                            reads=[po_b, xr_b[i]], writes=[xo_b[i]])
                    if last:
                        kb.dma("pool", y_d[tt * 128:(tt + 1) * 128, :], xo[i][:, :], reads=[xo_b[i]], writes=[y_b])
                    else:
                        kb.dma("pool", x1_d[tt * 128:(tt + 1) * 128, :], xo[i][:, :], reads=[xo_b[i]], writes=[x1_b])
                        norm_tile(nb, xo[i][:, :], xo_b[i], tt)
            kb.barrier()

    for layer in range(nlayers):
        if layer == 0:
            bprep()
        units_phase(layer)
        outproj_phase(layer)
    kb.barrier()
    return nc


def _consts():
    bf = ml_dtypes.bfloat16
    c = {}
    c["ident"] = np.eye(128, dtype=np.float32).astype(bf)
    bo = np.zeros((128, 128), np.float32)
    bo[0:64, 0:64] = 1.0
    bo[64:128, 64:128] = 1.0
    c["bones"] = bo.astype(bf)
    c["aones"] = np.ones((128, 128), np.float32).astype(bf)
    c["onesrow"] = np.ones((8, S), np.float32).astype(bf)
    t = np.arange(S)
    hi = (t // 256) * 256
    lo = t % 256
    augA = np.zeros((4, 12, S), np.float32)
    for h in range(4):
        s = 2.0 ** (-8.0 * (h + 1) / 4)
        augA[h, 0] = 1.0
        augA[h, 1] = 1.0
        augA[h, 3] = -s * hi
        augA[h, 4] = -s * lo
        augA[h, 6] = s * hi
        augA[h, 7] = s * lo
        augA[h, 9] = 1.0
        augA[h, 10] = 1.0
    c["augA"] = augA.astype(bf)
    k = np.arange(128)[:, None]
    i = np.arange(128)[None, :]
    corrA = np.zeros((4, 128, 128), np.float32)
    for h in range(4):
        s = 2.0 ** (-8.0 * (h + 1) / 4)
        m = np.where(k <= i, 0.0, np.where((k // 64) == (i // 64), -2.0 * s * (k - i), NEG))
        corrA[h] = m
    c["corrA"] = corrA
    c["corrB"] = np.where(k <= i, 0.0, NEG).astype(np.float32)
    i2 = np.arange(256)[None, :]
    corrC = np.zeros((8, 128, 256), np.float32)
    for h in range(8):
        s = 2.0 ** (-8.0 * (h + 1) / 8)
        cd = (i2 // 64) - (k // 64)
        corrC[h] = np.where((cd >= 0) & (cd <= 2), -s * np.abs(i2 - k), NEG)
    c["corrC"] = corrC
    i3 = np.arange(640)[None, :]
    cd = (i3 // 64) - (k // 64)
    c["maskD"] = np.where((cd >= 0) & (cd <= 8), 0.0, NEG).astype(np.float32)
    c["ridxD"] = np.clip(i3 - k, -63, 256) + 63
    return c


_C = None


def _host_inputs(inputs):
    global _C
    if _C is None:
        _C = _consts()
    c = _C
    f = lambda a: np.ascontiguousarray(np.asarray(a), dtype=np.float32)
    par = np.zeros((128, NPAR), np.float32)
    par[:, PC_LNG0:PC_LNG0 + 8] = f(inputs["even_ln_g"])[0].reshape(8, 128).T
    par[:, PC_LNG1:PC_LNG1 + 8] = f(inputs["odd_ln_g"])[0].reshape(8, 128).T
    for col, name in ((PC_AQ, "a_q_norm_g"), (PC_AK, "a_k_norm_g"), (PC_BQ, "b_q_norm_g"), (PC_BK, "b_k_norm_g"),
                      (PC_CQ, "c_q_norm_g"), (PC_CK, "c_k_norm_g"), (PC_DQ, "d_q_norm_g"), (PC_DK, "d_k_norm_g")):
        par[:, col] = np.tile(f(inputs[name])[0], 2)
    par[:, PC_SUB] = f(inputs["a_subln_g"])[0]
    par[0:8, PC_FB] = f(inputs["b_forget_bias"])[0]
    par[:, PC_SINK:PC_SINK + 8] = np.broadcast_to(f(inputs["c_sinks"])[0], (128, 8))
    for col, name in ((PC_LQ1, "a_lambda_q1"), (PC_LK1, "a_lambda_k1"), (PC_LQ2, "a_lambda_q2"), (PC_LK2, "a_lambda_k2")):
        par[:, col:col + 64] = np.broadcast_to(f(inputs[name])[0], (128, 64))
    relD = np.ascontiguousarray(f(inputs["d_rel_bias"])[0][:, c["ridxD"]])
    shared = {
        "w_in0": f(inputs["even_w_in"])[0], "w_in1": f(inputs["odd_w_in"])[0],
        "w_out0": f(inputs["even_w_out"])[0], "w_out1": f(inputs["odd_w_out"])[0],
        "params": par, "ident": c["ident"], "bones": c["bones"], "aones": c["aones"], "augA": c["augA"],
        "onesrow": c["onesrow"], "corrA": c["corrA"], "corrB": c["corrB"], "corrC": c["corrC"], "relD": relD,
        "maskD": c["maskD"],
    }
    return shared


def kernel(**inputs):
    x = np.ascontiguousarray(np.asarray(inputs["x"]), dtype=np.float32)
    shared = _host_inputs(inputs)
    nb = x.shape[0]
    in_maps = []
    for b in range(nb):
        m = dict(shared)
        m["x"] = x[b]
        in_maps.append(m)
    nc = build()
    res = run_bass_kernel_spmd(nc, in_maps, core_ids=list(range(nb)))
    return np.stack([np.asarray(r["y"]) for r in res.results], axis=0).astype(np.float32)
```

```python
import contextlib
import numpy as np
import ml_dtypes
import concourse.bass as bass
import concourse.mybir as mybir
from concourse.bass_utils import run_bass_kernel_spmd

F32 = mybir.dt.float32
BF16 = mybir.dt.bfloat16
ALU = mybir.AluOpType
AF = mybir.ActivationFunctionType
AX = mybir.AxisListType

S = 4096
DM = 1024
NT = 32
NBLK = 8
EPS = 1e-6
NEG = -30000.0
P_EVEN = 4104
P_ODD = 3328
NPAR = 290
PC_LNG0, PC_LNG1 = 0, 8
PC_AQ, PC_AK, PC_SUB, PC_BQ, PC_BK, PC_CQ, PC_CK, PC_DQ, PC_DK, PC_FB = 16, 17, 18, 19, 20, 21, 22, 23, 24, 25
PC_SINK = 26
PC_LQ1, PC_LK1, PC_LQ2, PC_LK2 = 34, 98, 162, 226


class Buf:
    __slots__ = ("w", "r", "dsem", "dcnt", "name")

    def __init__(self, name=""):
        self.w = {}
        self.r = {}
        self.dsem = None
        self.dcnt = 0
        self.name = name


def _merge(d, s):
    for k, (sem, v) in s.items():
        if k not in d or d[k][1] < v:
            d[k] = (sem, v)


class KB:
    def __init__(self, nc):
        self.nc = nc
        self.stack = contextlib.ExitStack()
        self.eng = {}
        for name, e in (("pe", nc.tensor), ("act", nc.scalar), ("dve", nc.vector),
                        ("pool", nc.gpsimd), ("sp", nc.sync)):
            sem = self.stack.enter_context(nc.semaphore("s_" + name))
            self.eng[name] = dict(e=e, sem=sem, cnt=0, waited={}, name=name)
        self.dbufs = []
        self.nsem = 5

    def _deps(self, reads, writes):
        d = {}
        for b in reads:
            _merge(d, b.w)
        for b in writes:
            _merge(d, b.w)
            _merge(d, b.r)
        return d

    def _wait(self, E, deps, skip_key=None):
        for key, (sem, val) in deps.items():
            if key == skip_key:
                continue
            if E["name"] == "pe" and key == id(self.eng["pe"]["sem"]):
                continue
            if E["waited"].get(key, 0) < val:
                E["e"].wait_ge(sem, val)
                E["waited"][key] = val

    def op(self, en, fn, reads=(), writes=()):
        E = self.eng[en]
        self._wait(E, self._deps(reads, writes))
        ins = fn(E["e"])
        E["cnt"] += 1
        ins.then_inc(E["sem"], 1)
        key = id(E["sem"])
        tok = (E["sem"], E["cnt"])
        for b in reads:
            b.r[key] = tok
        for b in writes:
            b.w[key] = tok
            b.r = {}
        return tok

    def dma(self, qn, out, in_, reads=(), writes=(), par=False):
        E = self.eng[qn]
        b = writes[0]
        if b.dsem is None:
            b.dsem = self.stack.enter_context(self.nc.semaphore("d%d" % self.nsem))
            self.nsem += 1
            self.dbufs.append(b)
        key = id(b.dsem)
        self._wait(E, self._deps(reads, writes), skip_key=key if par else None)
        ins = E["e"].dma_start(out=out, in_=in_)
        b.dcnt += 16
        ins.then_inc(b.dsem, 16)
        tok = (b.dsem, b.dcnt)
        for rb in reads:
            rb.r[key] = tok
        for wb in writes:
            wb.w[key] = tok
            wb.r = {}
        return tok

    def barrier(self):
        deps = {}
        for F in self.eng.values():
            if F["cnt"] > 0:
                deps[id(F["sem"])] = (F["sem"], F["cnt"])
        for b in self.dbufs:
            deps[id(b.dsem)] = (b.dsem, b.dcnt)
        for E in self.eng.values():
            for key, (sem, val) in deps.items():
                if key == id(E["sem"]):
                    continue
                if E["waited"].get(key, 0) < val:
                    E["e"].wait_ge(sem, val)
                    E["waited"][key] = val


def mm(pe, out, lhsT, rhs, start, stop, skip=False):
    return pe.matmul(out, lhsT=lhsT, rhs=rhs, start=start, stop=stop, skip_group_check=skip)


def units_l0():
    us = []
    for h in range(4):
        us.append(dict(kind="A", qcol=h * 128, kcol=512 + h * 128, kw=128, vcol=1024 + h * 128, vw=128,
                       gcol=1536 + h * 128, mix0=h * 128, heads=(h, h), pq=PC_AQ, pk=PC_AK))
    for p in range(4):
        us.append(dict(kind="B", qcol=2048 + p * 128, kcol=2560 + p * 128, kw=128, vcol=3072 + p * 128, vw=128,
                       gcol=3584 + p * 128, mix0=512 + p * 128, heads=(2 * p, 2 * p + 1), pq=PC_BQ, pk=PC_BK))
    return us


def units_l1():
    us = []
    for j in range(2):
        for p in range(2):
            us.append(dict(kind="C", qcol=j * 256 + p * 128, kcol=512 + j * 64, kw=64, vcol=640 + j * 64, vw=64,
                           gcol=768 + j * 256 + p * 128, mix0=j * 256 + p * 128,
                           heads=(4 * j + 2 * p, 4 * j + 2 * p + 1), pq=PC_CQ, pk=PC_CK))
    for p in range(4):
        us.append(dict(kind="D", qcol=1280 + p * 128, kcol=1792 + p * 128, kw=128, vcol=2304 + p * 128, vw=128,
                       gcol=2816 + p * 128, mix0=512 + p * 128, heads=(2 * p, 2 * p + 1), pq=PC_DQ, pk=PC_DK))
    return us


def tiles_for(kind, qb):
    out = []
    if kind in ("A", "B"):
        for kt in range(4 * qb + 4):
            if kt < 4 * qb:
                out.append((kt, 0, 512, None))
            else:
                j = kt - 4 * qb
                out.append((kt, 128 * j, 512, 0))
        return out
    span = 256 if kind == "C" else 640
    back = 1 if kind == "C" else 4
    kts = [kt for kt in range(4 * qb - back, 4 * qb + 4) if kt >= 0]
    for kt in kts:
        lo = max(128 * kt, 512 * qb)
        hi = min(128 * kt + span, 512 * qb + 512)
        if hi > lo:
            out.append((kt, lo - 512 * qb, hi - 512 * qb, lo - 128 * kt))
    return out


def build(debug=False, nlayers=2, max_units=None):
    nc = bass.Bass("TRN2", target_bir_lowering=False)
    kb = KB(nc)
    st = kb.stack

    def dram(name, shape, dt, kind="ExternalInput"):
        return nc.dram_tensor(name, shape, dt, kind=kind).ap()

    x_d = dram("x", [S, DM], F32)
    win_d = [dram("w_in0", [DM, P_EVEN], F32), dram("w_in1", [DM, P_ODD], F32)]
    wout_d = [dram("w_out0", [DM, DM], F32), dram("w_out1", [DM, DM], F32)]
    par_d = dram("params", [128, NPAR], F32)
    ident_d = dram("ident", [128, 128], BF16)
    bones_d = dram("bones", [128, 128], BF16)
    aones_d = dram("aones", [128, 128], BF16)
    augA_d = dram("augA", [4, 12, S], BF16)
    onesrow_d = dram("onesrow", [8, S], BF16)
    corrA_d = dram("corrA", [4, 128, 128], F32)
    corrB_d = dram("corrB", [128, 128], F32)
    corrC_d = dram("corrC", [8, 128, 256], F32)
    relD_d = dram("relD", [8, 128, 640], F32)
    maskD_d = dram("maskD", [128, 640], F32)
    y_d = dram("y", [S, DM], F32, kind="ExternalOutput")
    skind = "ExternalOutput" if debug else "Internal"
    mixT_d = dram("mixT_scr", [DM, S], BF16, kind=skind)
    x1_d = dram("x1_scr", [S, DM], F32, kind=skind)
    augB_d = dram("augB_scr", [8, 12, S], BF16, kind=skind)

    uniq = [0]

    def sb(stack, name, shape, dt):
        uniq[0] += 1
        return stack.enter_context(nc.sbuf_tensor("%s_%d" % (name, uniq[0]), shape, dt))

    xnT = sb(st, "xnT", [128, 8, S], BF16)
    xnT_b = [Buf("xnT%d" % i) for i in range(NBLK)]
    ident = sb(st, "ident_sb", [128, 128], BF16)
    bones = sb(st, "bones_sb", [128, 128], BF16)
    aones = sb(st, "aones_sb", [128, 128], BF16)
    par = sb(st, "par_sb", [128, NPAR], F32)
    der = sb(st, "der_sb", [128, 32], F32)
    esink = sb(st, "esink_sb", [128, 8], F32)
    cst_b = Buf("consts")
    par_b = Buf("par")
    der_b = Buf("der")
    ps = [st.enter_context(nc.psum_tensor("ps%d" % i, [128, 512], F32)) for i in range(8)]
    ps_b = [Buf("ps%d" % i) for i in range(8)]
    psrr = [0]

    def psum_next(allowed=range(8)):
        allowed = list(allowed)
        i = allowed[psrr[0] % len(allowed)]
        psrr[0] += 1
        return ps[i], ps_b[i]

    mixblk_b = [Buf("mixblk%d" % i) for i in range(NBLK)]
    x1_b = Buf("x1scr")
    y_b = Buf("y")
    augB_b = Buf("augB")

    kb.dma("sp", ident[:], ident_d[:, :], writes=[cst_b])
    kb.dma("sp", bones[:], bones_d[:, :], writes=[cst_b], par=True)
    kb.dma("sp", aones[:], aones_d[:, :], writes=[cst_b], par=True)
    kb.dma("sp", par[:], par_d[:, :], writes=[par_b])

    DQ = {PC_AQ: 0, PC_BQ: 1, PC_CQ: 2, PC_DQ: 3}
    for pc, dc_ in DQ.items():
        kb.op("dve", lambda e, pc=pc, dc_=dc_: e.tensor_scalar(out=der[:, dc_:dc_ + 1], in0=par[:, pc:pc + 1], scalar1=0.125,
                                                             scalar2=None, op0=ALU.mult), reads=[par_b], writes=[der_b])
    kb.op("dve", lambda e: e.tensor_scalar(out=der[:, 4:5], in0=par[:, PC_SUB:PC_SUB + 1], scalar1=0.8, scalar2=None,
                                           op0=ALU.mult), reads=[par_b], writes=[der_b])
    kb.op("dve", lambda e: e.tensor_scalar(out=der[:, 6:7], in0=par[:, PC_FB:PC_FB + 1], scalar1=-1.0, scalar2=None,
                                           op0=ALU.mult), reads=[par_b], writes=[der_b])
    with contextlib.ExitStack() as s0:
        lt = sb(s0, "lamtmp", [128, 128], F32)
        lt_b = Buf("lt")
        kb.op("dve", lambda e: e.tensor_tensor(out=lt[:, 0:64], in0=par[:, PC_LQ1:PC_LQ1 + 64], in1=par[:, PC_LK1:PC_LK1 + 64],
                                               op=ALU.mult), reads=[par_b], writes=[lt_b])
        kb.op("dve", lambda e: e.tensor_tensor(out=lt[:, 64:128], in0=par[:, PC_LQ2:PC_LQ2 + 64], in1=par[:, PC_LK2:PC_LK2 + 64],
                                               op=ALU.mult), reads=[par_b], writes=[lt_b])
        kb.op("dve", lambda e: e.reduce_sum(out=der[:, 8:9], in_=lt[:, 0:64], axis=AX.X), reads=[lt_b], writes=[der_b])
        kb.op("dve", lambda e: e.reduce_sum(out=der[:, 9:10], in_=lt[:, 64:128], axis=AX.X), reads=[lt_b], writes=[der_b])
        kb.op("act", lambda e: e.activation(out=der[:, 10:12], in_=der[:, 8:10], func=AF.Exp), reads=[der_b], writes=[der_b])
        kb.op("dve", lambda e: e.tensor_tensor(out=der[:, 12:13], in0=der[:, 11:12], in1=der[:, 10:11], op=ALU.subtract),
              reads=[der_b], writes=[der_b])
        kb.op("dve", lambda e: e.tensor_scalar(out=der[:, 5:6], in0=der[:, 12:13], scalar1=-0.2, scalar2=None, op0=ALU.add),
              reads=[der_b], writes=[der_b])
        kb.op("act", lambda e: e.activation(out=esink[:, :], in_=par[:, PC_SINK:PC_SINK + 8], func=AF.Exp), reads=[par_b],
              writes=[der_b])
        kb.barrier()

    def norm_tile(nb, xs, xs_b, tt):
        i = tt % 2
        sq, sq_b = nb["sq"][i], nb["sq_b"][i]
        sm, sm_b = nb["sm"][i], nb["sm_b"][i]
        xb, xb_b = nb["xb"][i], nb["xb_b"][i]
        kb.op("act", lambda e: e.activation(out=sq[:, :], in_=xs, func=AF.Square), reads=[xs_b], writes=[sq_b])
        kb.op("dve", lambda e: e.reduce_sum(out=sm[:, 0:1], in_=sq[:, :], axis=AX.X), reads=[sq_b], writes=[sm_b])
        kb.op("act", lambda e: e.activation(out=sm[:, 1:2], in_=sm[:, 0:1], func=AF.Ln, scale=1.0 / DM, bias=EPS),
              reads=[sm_b], writes=[sm_b])
        kb.op("act", lambda e: e.activation(out=sm[:, 2:3], in_=sm[:, 1:2], func=AF.Exp, scale=-0.5), reads=[sm_b], writes=[sm_b])
        kb.op("dve", lambda e: e.tensor_scalar(out=xb[:, :], in0=xs, scalar1=sm[:, 2:3], scalar2=None, op0=ALU.mult),
              reads=[xs_b, sm_b], writes=[xb_b])
        pt, pt_b = psum_next()
        ptv = pt[:, :].bitcast(BF16)

        def tr(pe):
            ins = None
            for dc in range(8):
                ins = pe.transpose(ptv[:, dc * 128:(dc + 1) * 128], xb[:, dc * 128:(dc + 1) * 128], ident[:, :])
            return ins
        kb.op("pe", tr, reads=[xb_b, cst_b], writes=[pt_b])
        tb = tt // 4
        kb.op("act", lambda e: e.copy(out=xnT[:, :, tt * 128:(tt + 1) * 128],
                                      in_=ptv.rearrange("p (c t) -> p c t", c=8)),
              reads=[pt_b], writes=[xnT_b[tb]])

    def norm_bufs(stack, pfx):
        nb = dict(sq=[], sq_b=[], sm=[], sm_b=[], xb=[], xb_b=[])
        for i in range(2):
            nb["sq"].append(sb(stack, pfx + "sq%d" % i, [128, DM], F32))
            nb["sq_b"].append(Buf())
            nb["sm"].append(sb(stack, pfx + "sm%d" % i, [128, 4], F32))
            nb["sm_b"].append(Buf())
            nb["xb"].append(sb(stack, pfx + "xb%d" % i, [128, DM], BF16))
            nb["xb_b"].append(Buf())
        return nb

    with contextlib.ExitStack() as s1:
        nb = norm_bufs(s1, "n0")
        xst = [sb(s1, "xst%d" % i, [128, DM], F32) for i in range(2)]
        xst_b = [Buf() for _ in range(2)]
        for tt in range(NT):
            i = tt % 2
            kb.dma("sp", xst[i][:, :], x_d[tt * 128:(tt + 1) * 128, :], writes=[xst_b[i]])
            norm_tile(nb, xst[i][:, :], xst_b[i], tt)
        kb.barrier()

    def units_phase(layer):
        units = units_l0() if layer == 0 else units_l1()
        if max_units is not None:
            units = units[:max_units] if isinstance(max_units, int) else [units[i] for i in max_units]
        lng = PC_LNG0 if layer == 0 else PC_LNG1
        win = win_d[layer].rearrange("(c p) n -> p c n", p=128)
        KK = 70 if layer == 0 else 64
        with contextlib.ExitStack() as s2:
            qT = [sb(s2, "qT%d" % m, [128, S], BF16) for m in range(2)]
            kT = [sb(s2, "kT%d" % m, [128, S], BF16) for m in range(2)]
            qT_b = [Buf("qT%d" % m) for m in range(2)]
            kT_b = [Buf("kT%d" % m) for m in range(2)]
            Vaug = sb(s2, "Vaug", [128, NT, 2, 128], BF16)
            Vaug_b = Buf("Vaug")
            sg = sb(s2, "sg", [128, S], F32)
            sg_b = Buf("sg")
            wbf = sb(s2, "wbf", [128, 8, 512], BF16)
            wbf_b = Buf("wbf")
            wst = [sb(s2, "wst%d" % i, [128, 8, 128], F32) for i in range(2)]
            wst_b = [Buf() for _ in range(2)]
            Pt = [sb(s2, "P%d" % i, [128, 512], BF16) for i in range(4)]
            Pt_b = [Buf() for _ in range(4)]
            ncorr = 640 if layer == 1 else 128
            corr = [sb(s2, "corr%d" % m, [128, ncorr], F32) for m in range(2)]
            corr_b = [Buf() for _ in range(2)]
            if layer == 1:
                maskD = sb(s2, "maskD", [128, 640], F32)
                maskD_b = Buf()
                kb.dma("sp", maskD[:, :], maskD_d[:, :], writes=[maskD_b])
            rl = [sb(s2, "rl%d" % i, [128, 512], F32) for i in range(2)]
            rl_b = [Buf() for _ in range(2)]
            on = [sb(s2, "on%d" % i, [128, 512], F32) for i in range(2)]
            on_b = [Buf() for _ in range(2)]
            lnt = [sb(s2, "lnt%d" % i, [128, 512], F32) for i in range(2)]
            lnt_b = [Buf() for _ in range(2)]
            sqt = [sb(s2, "sqt%d" % i, [128, 512], BF16) for i in range(2)]
            sqt_b = [Buf() for _ in range(2)]
            lnv = [sb(s2, "lnv%d" % i, [128, 512], F32) for i in range(2)]
            lnv_b = [Buf() for _ in range(2)]
            rstd = [sb(s2, "rstd%d" % i, [128, 512], F32) for i in range(2)]
            rstd_b = [Buf() for _ in range(2)]
            nrm_rr = [0]
            mx = [sb(s2, "mx%d" % i, [128, 512], BF16) for i in range(2)]
            mx_b = [Buf() for _ in range(2)]

            kb.op("pool", lambda e: e.memset(Vaug[:, :, 0, 64:128], 1.0), writes=[Vaug_b])
            kb.op("pool", lambda e: e.memset(Vaug[:, :, 1, 0:64], 1.0), writes=[Vaug_b])

            def prep_weights(u):
                groups = [(u["qcol"], 128), (u["kcol"], u["kw"]), (u["vcol"], u["vw"]), (u["gcol"], 128)]
                for gi, (col, wd) in enumerate(groups):
                    i = gi % 2
                    kb.dma("sp", wst[i][:, :, 0:wd], win[:, :, col:col + wd], writes=[wst_b[i]])
                    for dc in range(8):
                        kb.op("pool", lambda e, i=i, dc=dc, gi=gi, wd=wd: e.tensor_scalar(
                            out=wbf[:, dc, gi * 128:gi * 128 + wd], in0=wst[i][:, dc, 0:wd],
                            scalar1=par[:, lng + dc:lng + dc + 1], scalar2=None, op0=ALU.mult),
                            reads=[wst_b[i], par_b], writes=[wbf_b])

            if units:
                prep_weights(units[0])
            for ui, u in enumerate(units):
                kind = u["kind"]
                if layer == 0:
                    for m in range(2):
                        if kind == "A":
                            srcq = augA_d[u["heads"][0], 0:6, :]
                            srck = augA_d[u["heads"][0], 6:12, :]
                            rd = []
                        else:
                            srcq = augB_d[u["heads"][m], 0:6, :]
                            srck = augB_d[u["heads"][m], 6:12, :]
                            rd = [augB_b]
                        kb.dma("sp", qT[m][64:70, :], srcq, reads=rd, writes=[qT_b[m]])
                        kb.dma("sp", kT[m][64:70, :], srck, reads=rd, writes=[kT_b[m]])
                    if kind == "A":
                        kb.dma("sp", corr[0][:, 0:128], corrA_d[u["heads"][0], :, :], writes=[corr_b[0]])
                    else:
                        kb.dma("sp", corr[0][:, 0:128], corrB_d[:, :], writes=[corr_b[0]])
                else:
                    for m in range(2):
                        h = u["heads"][m]
                        if kind == "C":
                            kb.dma("sp", corr[m][:, 0:256], corrC_d[h, :, :], writes=[corr_b[m]])
                        else:
                            kb.dma("sp", corr[m][:, 0:640], relD_d[h, :, :], writes=[corr_b[m]])
                            kb.op("pool", lambda e, m=m: e.tensor_tensor(out=corr[m][:, :], in0=corr[m][:, :], in1=maskD[:, :],
                                                                         op=ALU.add), reads=[maskD_b], writes=[corr_b[m]])
                dq = DQ[u["pq"]]
                for tb in range(NBLK):
                    t0 = tb * 512
                    xb_ = xnT_b[tb]

                    def proj(pdst, c0, wd, t0=t0):
                        def f(pe):
                            ins = None
                            for dc in range(8):
                                ins = mm(pe, pdst[0:wd, :], wbf[:, dc, c0:c0 + wd], xnT[:, dc, t0:t0 + 512], dc == 0, dc == 7)
                            return ins
                        return f

                    def qk_norm(pq_, pq_b_, R, gcolap, dst, dst_b, kmap, t0=t0):
                        ri = nrm_rr[0] % 2
                        nrm_rr[0] += 1
                        sq_, sq_b_, ln_, ln_b_, rs_, rs_b_ = sqt[ri], sqt_b[ri], lnv[ri], lnv_b[ri], rstd[ri], rstd_b[ri]
                        kb.op("act", lambda e: e.activation(out=sq_[0:R, :], in_=pq_[0:R, :], func=AF.Square),
                              reads=[pq_b_], writes=[sq_b_])
                        pss, pss_b = psum_next()
                        kb.op("pe", lambda pe: mm(pe, pss[0:R, :], bones[0:R, 0:R], sq_[0:R, :], True, True),
                              reads=[sq_b_, cst_b], writes=[pss_b])
                        kb.op("act", lambda e: e.activation(out=ln_[0:R, :], in_=pss[0:R, :], func=AF.Ln, scale=1.0 / 64, bias=EPS),
                              reads=[pss_b], writes=[ln_b_])
                        kb.op("act", lambda e: e.activation(out=rs_[0:R, :], in_=ln_[0:R, :], func=AF.Exp, scale=-0.5),
                              reads=[ln_b_], writes=[rs_b_])
                        for m in range(R // 64):
                            rows = slice(64 * m, 64 * m + 64)
                            kb.op("dve", lambda e, m=m, rows=rows: e.scalar_tensor_tensor(
                                out=dst[m][0:64, t0:t0 + 512], in0=pq_[rows, :], scalar=gcolap(rows), in1=rs_[rows, :],
                                op0=ALU.mult, op1=ALU.mult), reads=[pq_b_, rs_b_, par_b, der_b], writes=[dst_b[m]])

                    pq_, pq_b_ = psum_next()
                    kb.op("pe", proj(pq_, 0, 128), reads=[wbf_b, xb_], writes=[pq_b_])
                    qk_norm(pq_, pq_b_, 128, lambda rows: der[rows, dq:dq + 1], qT, qT_b, None)
                    pk_, pk_b_ = psum_next()
                    kb.op("pe", proj(pk_, 128, u["kw"]), reads=[wbf_b, xb_], writes=[pk_b_])
                    qk_norm(pk_, pk_b_, u["kw"], lambda rows: par[rows, u["pk"]:u["pk"] + 1], kT, kT_b, None)
                    pv_, pv_b_ = psum_next()
                    vw = u["vw"]

                    def vproj(pe, t0=t0, vw=vw):
                        ins = None
                        for j in range(4):
                            for dc in range(8):
                                ins = mm(pe, pv_[:, j * vw:(j + 1) * vw], xnT[:, dc, t0 + j * 128:t0 + (j + 1) * 128],
                                         wbf[:, dc, 256:256 + vw], dc == 0, dc == 7)
                        return ins
                    kb.op("pe", vproj, reads=[wbf_b, xb_], writes=[pv_b_])
                    pvv = pv_[:, 0:4 * vw].rearrange("p (j c) -> p j c", j=4)
                    if vw == 128:
                        kb.op("dve", lambda e: e.tensor_copy(out=Vaug[:, 4 * tb:4 * tb + 4, 0, 0:64], in_=pvv[:, :, 0:64]),
                              reads=[pv_b_], writes=[Vaug_b])
                        kb.op("dve", lambda e: e.tensor_copy(out=Vaug[:, 4 * tb:4 * tb + 4, 1, 64:128], in_=pvv[:, :, 64:128]),
                              reads=[pv_b_], writes=[Vaug_b])
                    else:
                        kb.op("dve", lambda e: e.tensor_copy(out=Vaug[:, 4 * tb:4 * tb + 4, 0, 0:64], in_=pvv[:, :, 0:64]),
                              reads=[pv_b_], writes=[Vaug_b])
                        kb.op("dve", lambda e: e.tensor_copy(out=Vaug[:, 4 * tb:4 * tb + 4, 1, 64:128], in_=pvv[:, :, 0:64]),
                              reads=[pv_b_], writes=[Vaug_b])

                for tb in range(NBLK):
                    t0 = tb * 512
                    pg_, pg_b_ = psum_next()

                    def gproj(pe, pg_=pg_, t0=t0):
                        ins = None
                        for dc in range(8):
                            ins = mm(pe, pg_[:, :], wbf[:, dc, 384:512], xnT[:, dc, t0:t0 + 512], dc == 0, dc == 7)
                        return ins
                    kb.op("pe", gproj, reads=[wbf_b, xnT_b[tb]], writes=[pg_b_])
                    kb.op("act", lambda e, pg_=pg_, t0=t0: e.activation(out=sg[:, t0:t0 + 512], in_=pg_[:, :], func=AF.Silu),
                          reads=[pg_b_], writes=[sg_b])
                if ui + 1 < len(units):
                    prep_weights(units[ui + 1])
                if kind == "A":
                    maps = [dict(q=0, k=0, pv=[(0, 0), (1, 1)], corr=0), dict(q=1, k=1, pv=[(0, 2), (1, 3)], corr=0)]
                    nacc = 4
                elif kind == "C":
                    maps = [dict(q=0, k=0, pv=[(0, 0)], corr=0), dict(q=1, k=0, pv=[(1, 1)], corr=1)]
                    nacc = 2
                else:
                    maps = [dict(q=0, k=0, pv=[(0, 0)], corr=0 if layer == 0 else 0), dict(q=1, k=1, pv=[(1, 1)], corr=0 if layer == 0 else 1)]
                    nacc = 2
                dbl = (nacc == 2)
                sbanks = list(range(4, 8))
                nS = len(sbanks)
                skew = 2
                for qb in range(NBLK):
                    Q0 = qb * 512
                    abase = 2 * (qb % 2) if dbl else 0
                    items = []
                    for (kt, q0, q1, c0) in tiles_for(kind, qb):
                        for mi, mp in enumerate(maps):
                            items.append((kt, q0, q1, c0, mi))
                    started = set()

                    def emit_qk(w):
                        kt, q0, q1, c0, mi = items[w]
                        mp = maps[mi]
                        bi = sbanks[w % nS]
                        Sx, Sb = ps[bi], ps_b[bi]
                        kb.op("pe", lambda pe: mm(pe, Sx[:, q0:q1], kT[mp["k"]][0:KK, kt * 128:(kt + 1) * 128],
                                                  qT[mp["q"]][0:KK, Q0 + q0:Q0 + q1], True, True),
                              reads=[kT_b[mp["k"]], qT_b[mp["q"]]], writes=[Sb])
                        if c0 is not None:
                            cm = mp["corr"]
                            if layer == 0:
                                a0, a1 = q0, q0 + 128
                                cap = corr[cm][:, 0:128]
                            else:
                                a0, a1 = q0, q1
                                cap = corr[cm][:, c0:c0 + (q1 - q0)]
                            kb.op("dve", lambda e: e.tensor_tensor(out=Sx[:, a0:a1], in0=Sx[:, a0:a1], in1=cap, op=ALU.add),
                                  reads=[corr_b[cm]], writes=[Sb])
                        pi = w % 4
                        kb.op("act", lambda e: e.activation(out=Pt[pi][:, q0:q1], in_=Sx[:, q0:q1], func=AF.Exp),
                              reads=[Sb], writes=[Pt_b[pi]])

                    def emit_pv(w):
                        kt, q0, q1, c0, mi = items[w]
                        mp = maps[mi]
                        pi = w % 4
                        for (slot, ai) in mp["pv"]:
                            first = ai not in started
                            started.add(ai)
                            kb.op("pe", lambda pe, slot=slot, ai=ai, first=first: mm(
                                pe, ps[abase + ai][:, q0:q1], Vaug[:, kt, slot, :], Pt[pi][:, q0:q1], first, True, skip=True),
                                reads=[Pt_b[pi], Vaug_b], writes=[ps_b[abase + ai]])

                    n = len(items)
                    for w in range(n + skew):
                        if w < n:
                            emit_qk(w)
                        if w - skew >= 0:
                            emit_pv(w - skew)

                    mxi = qb % 2
                    if kind == "A":
                        for a in range(4):
                            c, half = a // 2, a % 2
                            orow = slice(0, 64) if half == 0 else slice(64, 128)
                            lrow = slice(64, 128) if half == 0 else slice(0, 64)
                            kb.op("act", lambda e, a=a, c=c, orow=orow, lrow=lrow: e.activation(
                                out=lnt[c][orow, :], in_=ps[a][lrow, :], func=AF.Ln), reads=[ps_b[a]], writes=[lnt_b[c]])
                        for c in range(2):
                            kb.op("act", lambda e, c=c: e.activation(out=rl[c][:, :], in_=lnt[c][:, :], func=AF.Exp, scale=-1.0),
                                  reads=[lnt_b[c]], writes=[rl_b[c]])
                        for a in range(4):
                            c, half = a // 2, a % 2
                            orow = slice(0, 64) if half == 0 else slice(64, 128)
                            kb.op("dve", lambda e, a=a, c=c, orow=orow: e.tensor_tensor(out=on[c][orow, :], in0=ps[a][orow, :],
                                                                                       in1=rl[c][orow, :], op=ALU.mult),
                                  reads=[ps_b[a], rl_b[c]], writes=[on_b[c]])
                        kb.op("dve", lambda e: e.scalar_tensor_tensor(out=lnt[0][:, :], in0=on[1][:, :], scalar=der[:, 5:6], in1=on[0][:, :],
                                                                       op0=ALU.mult, op1=ALU.add),
                              reads=[on_b[0], on_b[1], der_b], writes=[lnt_b[0]])
                        ri = nrm_rr[0] % 2
                        nrm_rr[0] += 1
                        kb.op("act", lambda e: e.activation(out=sqt[ri][:, :], in_=lnt[0][:, :], func=AF.Square), reads=[lnt_b[0]],
                              writes=[sqt_b[ri]])
                        pss, pss_b = psum_next(sbanks)
                        kb.op("pe", lambda pe: mm(pe, pss[:, :], aones[:, :], sqt[ri][:, :], True, True), reads=[sqt_b[ri], cst_b],
                              writes=[pss_b])
                        kb.op("act", lambda e: e.activation(out=lnv[ri][:, :], in_=pss[:, :], func=AF.Ln, scale=1.0 / 128, bias=EPS),
                              reads=[pss_b], writes=[lnv_b[ri]])
                        kb.op("act", lambda e: e.activation(out=rstd[ri][:, :], in_=lnv[ri][:, :], func=AF.Exp, scale=-0.5),
                              reads=[lnv_b[ri]], writes=[rstd_b[ri]])
                        kb.op("dve", lambda e: e.scalar_tensor_tensor(out=on[0][:, :], in0=lnt[0][:, :], scalar=der[:, 4:5], in1=rstd[ri][:, :],
                                                                       op0=ALU.mult, op1=ALU.mult),
                              reads=[lnt_b[0], rstd_b[ri], der_b], writes=[on_b[0]])
                        kb.op("pool", lambda e: e.tensor_tensor(out=mx[mxi][:, :], in0=on[0][:, :], in1=sg[:, Q0:Q0 + 512], op=ALU.mult),
                              reads=[on_b[0], sg_b], writes=[mx_b[mxi]])
                    else:
                        fi = qb % 2
                        for m in range(2):
                            orow = slice(0, 64) if m == 0 else slice(64, 128)
                            lrow = slice(64, 128) if m == 0 else slice(0, 64)
                            am = abase + m
                            if kind == "C":
                                h = u["heads"][m]
                                kb.op("act", lambda e, am=am, orow=orow, lrow=lrow, h=h: e.activation(
                                    out=lnt[fi][orow, :], in_=ps[am][lrow, :], func=AF.Ln, scale=1.0, bias=esink[lrow, h:h + 1]),
                                    reads=[ps_b[am], der_b], writes=[lnt_b[fi]])
                            else:
                                kb.op("act", lambda e, am=am, orow=orow, lrow=lrow: e.activation(
                                    out=lnt[fi][orow, :], in_=ps[am][lrow, :], func=AF.Ln), reads=[ps_b[am]], writes=[lnt_b[fi]])
                        kb.op("act", lambda e: e.activation(out=rl[fi][:, :], in_=lnt[fi][:, :], func=AF.Exp, scale=-1.0),
                              reads=[lnt_b[fi]], writes=[rl_b[fi]])
                        kb.op("pool", lambda e: e.tensor_tensor(out=on[fi][:, :], in0=rl[fi][:, :], in1=sg[:, Q0:Q0 + 512], op=ALU.mult),
                              reads=[rl_b[fi], sg_b], writes=[on_b[fi]])
                        for m in range(2):
                            orow = slice(0, 64) if m == 0 else slice(64, 128)
                            am = abase + m
                            kb.op("dve", lambda e, am=am, orow=orow: e.tensor_tensor(out=mx[mxi][orow, :], in0=ps[am][orow, :],
                                                                                    in1=on[fi][orow, :], op=ALU.mult),
                                  reads=[ps_b[am], on_b[fi]], writes=[mx_b[mxi]])
                    kb.dma("pool", mixT_d[u["mix0"]:u["mix0"] + 128, Q0:Q0 + 512], mx[mxi][:, :], reads=[mx_b[mxi]],
                           writes=[mixblk_b[qb]])
            kb.barrier()

    def bprep():
        win = win_d[0].rearrange("(c p) n -> p c n", p=128)
        with contextlib.ExitStack() as s3:
            wst = [sb(s3, "wstb", [128, 8, 8], F32)]
            wst_b = [Buf()]
            wfb = sb(s3, "wfb", [128, 8, 8], BF16)
            wfb_b = Buf()
            l1p = sb(s3, "l1p", [8, S], F32)
            l1p_b = Buf()
            ones8 = sb(s3, "ones8", [8, S], F32)
            ones8_b = Buf()
            cum = sb(s3, "cum", [8, S], F32)
            cum_b = Buf()
            parts = [sb(s3, "part%d" % i, [8, S], BF16) for i in range(3)]
            nparts = [sb(s3, "npart%d" % i, [8, S], BF16) for i in range(3)]
            parts_b = Buf()
            res = sb(s3, "resid", [8, S], F32)
            res_b = Buf()
            kb.dma("sp", wst[0][:, :, 0:8], win[:, :, 4096:4104], writes=[wst_b[0]])
            for dc in range(8):
                kb.op("pool", lambda e, dc=dc: e.tensor_scalar(out=wfb[:, dc, :], in0=wst[0][:, dc, 0:8],
                                                               scalar1=par[:, PC_LNG0 + dc:PC_LNG0 + dc + 1], scalar2=None, op0=ALU.mult),
                      reads=[wst_b[0], par_b], writes=[wfb_b])
            kb.op("pool", lambda e: e.memset(ones8[:, :], 1.0), writes=[ones8_b])
            ones8h = sb(s3, "ones8h", [8, S], BF16)
            kb.op("pool", lambda e: e.memset(ones8h[:, :], 1.0), writes=[ones8_b])
            for tb in range(NBLK):
                t0 = tb * 512
                pf, pf_b = psum_next()

                def f(pe, t0=t0):
                    ins = None
                    for dc in range(8):
                        ins = mm(pe, pf[0:8, :], wfb[:, dc, :], xnT[:, dc, t0:t0 + 512], dc == 0, dc == 7)
                    return ins
                kb.op("pe", f, reads=[wfb_b, xnT_b[tb]], writes=[pf_b])
                kb.op("act", lambda e, t0=t0: e.activation(out=l1p[:, t0:t0 + 512], in_=pf[0:8, :], func=AF.Exp, scale=-1.0,
                                                          bias=der[0:8, 6:7]), reads=[pf_b, der_b], writes=[l1p_b])
                kb.op("act", lambda e, t0=t0: e.activation(out=l1p[:, t0:t0 + 512], in_=l1p[:, t0:t0 + 512], func=AF.Ln, scale=1.0,
                                                          bias=1.0), reads=[], writes=[l1p_b])
            kb.op("dve", lambda e: e.tensor_tensor_scan(out=cum[:, :], data0=ones8[:, :], data1=l1p[:, :], initial=0.0,
                                                        op0=ALU.mult, op1=ALU.add), reads=[ones8_b, l1p_b], writes=[cum_b])
            kb.op("dve", lambda e: e.tensor_copy(out=parts[0][:, :], in_=cum[:, :]), reads=[cum_b], writes=[parts_b])
            kb.op("dve", lambda e: e.tensor_tensor(out=res[:, :], in0=cum[:, :], in1=parts[0][:, :], op=ALU.subtract),
                  reads=[cum_b], writes=[res_b, parts_b])
            kb.op("dve", lambda e: e.tensor_copy(out=parts[1][:, :], in_=res[:, :]), reads=[res_b], writes=[parts_b])
            kb.op("dve", lambda e: e.tensor_tensor(out=res[:, :], in0=res[:, :], in1=parts[1][:, :], op=ALU.subtract),
                  reads=[], writes=[res_b, parts_b])
            kb.op("dve", lambda e: e.tensor_copy(out=parts[2][:, :], in_=res[:, :]), reads=[res_b], writes=[parts_b])
            for i in range(3):
                kb.op("dve", lambda e, i=i: e.tensor_scalar(out=nparts[i][:, :], in0=parts[i][:, :], scalar1=-1.0, scalar2=None,
                                                            op0=ALU.mult), reads=[], writes=[parts_b])
            first = True
            for r in range(3):
                kb.dma("pool", augB_d[:, r, :], ones8h[:, :], reads=[ones8_b], writes=[augB_b], par=not first)
                first = False
                kb.dma("pool", augB_d[:, 9 + r, :], ones8h[:, :], reads=[ones8_b], writes=[augB_b], par=True)
                kb.dma("pool", augB_d[:, 3 + r, :], nparts[r][:, :], reads=[parts_b], writes=[augB_b], par=True)
                kb.dma("pool", augB_d[:, 6 + r, :], parts[r][:, :], reads=[parts_b], writes=[augB_b], par=True)
            kb.barrier()

    def outproj_phase(layer):
        res_src = x_d if layer == 0 else x1_d
        last = (layer == nlayers - 1)
        with contextlib.ExitStack() as s4:
            wo = sb(s4, "wo", [128, 8, DM], BF16)
            wo_b = Buf()
            wos = [sb(s4, "wos%d" % i, [128, DM], F32) for i in range(2)]
            wos_b = [Buf() for _ in range(2)]
            mt = [sb(s4, "mt%d" % i, [128, 8, 512], BF16) for i in range(2)]
            mt_b = [Buf() for _ in range(2)]
            xr = [sb(s4, "xr%d" % i, [128, DM], F32) for i in range(2)]
            xr_b = [Buf() for _ in range(2)]
            xo = [sb(s4, "xo%d" % i, [128, DM], F32) for i in range(2)]
            xo_b = [Buf() for _ in range(2)]
            nb = norm_bufs(s4, "n1") if not last else None
            wod = wout_d[layer].rearrange("(c p) n -> p c n", p=128)
            for mc in range(8):
                i = mc % 2
                kb.dma("sp", wos[i][:, :], wod[:, mc, :], writes=[wos_b[i]])
                kb.op("pool", lambda e, i=i, mc=mc: e.tensor_copy(out=wo[:, mc, :], in_=wos[i][:, :]), reads=[wos_b[i]], writes=[wo_b])
            mixv = mixT_d.rearrange("(c p) t -> p c t", p=128)
            for tb in range(NBLK):
                bi = tb % 2
                kb.dma("sp", mt[bi][:, :, :], mixv[:, :, tb * 512:(tb + 1) * 512], reads=[mixblk_b[tb]], writes=[mt_b[bi]])
                for j in range(4):
                    tt = 4 * tb + j
                    i = tt % 2
                    rd = [x1_b] if layer == 1 else []
                    kb.dma("sp", xr[i][:, :], res_src[tt * 128:(tt + 1) * 128, :], reads=rd, writes=[xr_b[i]])
                    for half in range(2):
                        po, po_b = psum_next()

                        def f(pe, po=po, half=half, j=j, bi=bi):
                            ins = None
                            for mc in range(8):
                                ins = mm(pe, po[:, :], mt[bi][:, mc, j * 128:(j + 1) * 128], wo[:, mc, half * 512:(half + 1) * 512],
                                         mc == 0, mc == 7)
                            return ins
                        kb.op("pe", f, reads=[mt_b[bi], wo_b], writes=[po_b])
                        kb.op("dve", lambda e, po=po, half=half, i=i: e.tensor_tensor(
                            out=xo[i][:, half * 512:(half + 1) * 512], in0=po[:, :], in1=xr[i][:, half * 512:(half + 1) * 512], op=ALU.add),
                            reads=[po_b, xr_b[i]], writes=[xo_b[i]])
                    if last:
                        kb.dma("pool", y_d[tt * 128:(tt + 1) * 128, :], xo[i][:, :], reads=[xo_b[i]], writes=[y_b])
                    else:
                        kb.dma("pool", x1_d[tt * 128:(tt + 1) * 128, :], xo[i][:, :], reads=[xo_b[i]], writes=[x1_b])
                        norm_tile(nb, xo[i][:, :], xo_b[i], tt)
            kb.barrier()

    for layer in range(nlayers):
        if layer == 0:
            bprep()
        units_phase(layer)
        outproj_phase(layer)
    kb.barrier()
    return nc


def _consts():
    bf = ml_dtypes.bfloat16
    c = {}
    c["ident"] = np.eye(128, dtype=np.float32).astype(bf)
    bo = np.zeros((128, 128), np.float32)
    bo[0:64, 0:64] = 1.0
    bo[64:128, 64:128] = 1.0
    c["bones"] = bo.astype(bf)
    c["aones"] = np.ones((128, 128), np.float32).astype(bf)
    c["onesrow"] = np.ones((8, S), np.float32).astype(bf)
    t = np.arange(S)
    hi = (t // 256) * 256
    lo = t % 256
    augA = np.zeros((4, 12, S), np.float32)
    for h in range(4):
        s = 2.0 ** (-8.0 * (h + 1) / 4)
        augA[h, 0] = 1.0
        augA[h, 1] = 1.0
        augA[h, 3] = -s * hi
        augA[h, 4] = -s * lo
        augA[h, 6] = s * hi
        augA[h, 7] = s * lo
        augA[h, 9] = 1.0
        augA[h, 10] = 1.0
    c["augA"] = augA.astype(bf)
    k = np.arange(128)[:, None]
    i = np.arange(128)[None, :]
    corrA = np.zeros((4, 128, 128), np.float32)
    for h in range(4):
        s = 2.0 ** (-8.0 * (h + 1) / 4)
        m = np.where(k <= i, 0.0, np.where((k // 64) == (i // 64), -2.0 * s * (k - i), NEG))
        corrA[h] = m
    c["corrA"] = corrA
    c["corrB"] = np.where(k <= i, 0.0, NEG).astype(np.float32)
    i2 = np.arange(256)[None, :]
    corrC = np.zeros((8, 128, 256), np.float32)
    for h in range(8):
        s = 2.0 ** (-8.0 * (h + 1) / 8)
        cd = (i2 // 64) - (k // 64)
        corrC[h] = np.where((cd >= 0) & (cd <= 2), -s * np.abs(i2 - k), NEG)
    c["corrC"] = corrC
    i3 = np.arange(640)[None, :]
    cd = (i3 // 64) - (k // 64)
    c["maskD"] = np.where((cd >= 0) & (cd <= 8), 0.0, NEG).astype(np.float32)
    c["ridxD"] = np.clip(i3 - k, -63, 256) + 63
    return c


_C = None


def _host_inputs(inputs):
    global _C
    if _C is None:
        _C = _consts()
    c = _C
    f = lambda a: np.ascontiguousarray(np.asarray(a), dtype=np.float32)
    par = np.zeros((128, NPAR), np.float32)
    par[:, PC_LNG0:PC_LNG0 + 8] = f(inputs["even_ln_g"])[0].reshape(8, 128).T
    par[:, PC_LNG1:PC_LNG1 + 8] = f(inputs["odd_ln_g"])[0].reshape(8, 128).T
    for col, name in ((PC_AQ, "a_q_norm_g"), (PC_AK, "a_k_norm_g"), (PC_BQ, "b_q_norm_g"), (PC_BK, "b_k_norm_g"),
                      (PC_CQ, "c_q_norm_g"), (PC_CK, "c_k_norm_g"), (PC_DQ, "d_q_norm_g"), (PC_DK, "d_k_norm_g")):
        par[:, col] = np.tile(f(inputs[name])[0], 2)
    par[:, PC_SUB] = f(inputs["a_subln_g"])[0]
    par[0:8, PC_FB] = f(inputs["b_forget_bias"])[0]
    par[:, PC_SINK:PC_SINK + 8] = np.broadcast_to(f(inputs["c_sinks"])[0], (128, 8))
    for col, name in ((PC_LQ1, "a_lambda_q1"), (PC_LK1, "a_lambda_k1"), (PC_LQ2, "a_lambda_q2"), (PC_LK2, "a_lambda_k2")):
        par[:, col:col + 64] = np.broadcast_to(f(inputs[name])[0], (128, 64))
    relD = np.ascontiguousarray(f(inputs["d_rel_bias"])[0][:, c["ridxD"]])
    shared = {
        "w_in0": f(inputs["even_w_in"])[0], "w_in1": f(inputs["odd_w_in"])[0],
        "w_out0": f(inputs["even_w_out"])[0], "w_out1": f(inputs["odd_w_out"])[0],
        "params": par, "ident": c["ident"], "bones": c["bones"], "aones": c["aones"], "augA": c["augA"],
        "onesrow": c["onesrow"], "corrA": c["corrA"], "corrB": c["corrB"], "corrC": c["corrC"], "relD": relD,
        "maskD": c["maskD"],
    }
    return shared


def kernel(**inputs):
    x = np.ascontiguousarray(np.asarray(inputs["x"]), dtype=np.float32)
    shared = _host_inputs(inputs)
    nb = x.shape[0]
    in_maps = []
    for b in range(nb):
        m = dict(shared)
        m["x"] = x[b]
        in_maps.append(m)
    nc = build()
    res = run_bass_kernel_spmd(nc, in_maps, core_ids=list(range(nb)))
    return np.stack([np.asarray(r["y"]) for r in res.results], axis=0).astype(np.float32)
```

```python
import contextlib
import numpy as np
import ml_dtypes
import concourse.bass as bass
import concourse.mybir as mybir
from concourse.bass_utils import run_bass_kernel_spmd

F32 = mybir.dt.float32
BF16 = mybir.dt.bfloat16
ALU = mybir.AluOpType
AF = mybir.ActivationFunctionType
AX = mybir.AxisListType

S = 4096
DM = 1024
NT = 32
NBLK = 8
EPS = 1e-6
NEG = -30000.0
P_EVEN = 4104
P_ODD = 3328
NPAR = 290
PC_LNG0, PC_LNG1 = 0, 8
PC_AQ, PC_AK, PC_SUB, PC_BQ, PC_BK, PC_CQ, PC_CK, PC_DQ, PC_DK, PC_FB = 16, 17, 18, 19, 20, 21, 22, 23, 24, 25
PC_SINK = 26
PC_LQ1, PC_LK1, PC_LQ2, PC_LK2 = 34, 98, 162, 226


class Buf:
    __slots__ = ("w", "r", "dsem", "dcnt", "name")

    def __init__(self, name=""):
        self.w = {}
        self.r = {}
        self.dsem = None
        self.dcnt = 0
        self.name = name


def _merge(d, s):
    for k, (sem, v) in s.items():
        if k not in d or d[k][1] < v:
            d[k] = (sem, v)


class KB:
    def __init__(self, nc):
        self.nc = nc
        self.stack = contextlib.ExitStack()
        self.eng = {}
        for name, e in (("pe", nc.tensor), ("act", nc.scalar), ("dve", nc.vector),
                        ("pool", nc.gpsimd), ("sp", nc.sync)):
            sem = self.stack.enter_context(nc.semaphore("s_" + name))
            self.eng[name] = dict(e=e, sem=sem, cnt=0, waited={}, name=name)
        self.dbufs = []
        self.nsem = 5

    def _deps(self, reads, writes):
        d = {}
        for b in reads:
            _merge(d, b.w)
        for b in writes:
            _merge(d, b.w)
            _merge(d, b.r)
        return d

    def _wait(self, E, deps, skip_key=None):
        for key, (sem, val) in deps.items():
            if key == skip_key:
                continue
            if E["name"] == "pe" and key == id(self.eng["pe"]["sem"]):
                continue
            if E["waited"].get(key, 0) < val:
                E["e"].wait_ge(sem, val)
                E["waited"][key] = val

    def op(self, en, fn, reads=(), writes=()):
        E = self.eng[en]
        self._wait(E, self._deps(reads, writes))
        ins = fn(E["e"])
        E["cnt"] += 1
        ins.then_inc(E["sem"], 1)
        key = id(E["sem"])
        tok = (E["sem"], E["cnt"])
        for b in reads:
            b.r[key] = tok
        for b in writes:
            b.w[key] = tok
            b.r = {}
        return tok

    def dma(self, qn, out, in_, reads=(), writes=(), par=False):
        E = self.eng[qn]
        b = writes[0]
        if b.dsem is None:
            b.dsem = self.stack.enter_context(self.nc.semaphore("d%d" % self.nsem))
            self.nsem += 1
            self.dbufs.append(b)
        key = id(b.dsem)
        self._wait(E, self._deps(reads, writes), skip_key=key if par else None)
        ins = E["e"].dma_start(out=out, in_=in_)
        b.dcnt += 16
        ins.then_inc(b.dsem, 16)
        tok = (b.dsem, b.dcnt)
        for rb in reads:
            rb.r[key] = tok
        for wb in writes:
            wb.w[key] = tok
            wb.r = {}
        return tok

    def barrier(self):
        deps = {}
        for F in self.eng.values():
            if F["cnt"] > 0:
                deps[id(F["sem"])] = (F["sem"], F["cnt"])
        for b in self.dbufs:
            deps[id(b.dsem)] = (b.dsem, b.dcnt)
        for E in self.eng.values():
            for key, (sem, val) in deps.items():
                if key == id(E["sem"]):
                    continue
                if E["waited"].get(key, 0) < val:
                    E["e"].wait_ge(sem, val)
                    E["waited"][key] = val


def mm(pe, out, lhsT, rhs, start, stop, skip=False):
    return pe.matmul(out, lhsT=lhsT, rhs=rhs, start=start, stop=stop, skip_group_check=skip)


def units_l0():
    us = []
    for h in range(4):
        us.append(dict(kind="A", qcol=h * 128, kcol=512 + h * 128, kw=128, vcol=1024 + h * 128, vw=128,
                       gcol=1536 + h * 128, mix0=h * 128, heads=(h, h), pq=PC_AQ, pk=PC_AK))
    for p in range(4):
        us.append(dict(kind="B", qcol=2048 + p * 128, kcol=2560 + p * 128, kw=128, vcol=3072 + p * 128, vw=128,
                       gcol=3584 + p * 128, mix0=512 + p * 128, heads=(2 * p, 2 * p + 1), pq=PC_BQ, pk=PC_BK))
    return us


def units_l1():
    us = []
    for j in range(2):
        for p in range(2):
            us.append(dict(kind="C", qcol=j * 256 + p * 128, kcol=512 + j * 64, kw=64, vcol=640 + j * 64, vw=64,
                           gcol=768 + j * 256 + p * 128, mix0=j * 256 + p * 128,
                           heads=(4 * j + 2 * p, 4 * j + 2 * p + 1), pq=PC_CQ, pk=PC_CK))
    for p in range(4):
        us.append(dict(kind="D", qcol=1280 + p * 128, kcol=1792 + p * 128, kw=128, vcol=2304 + p * 128, vw=128,
                       gcol=2816 + p * 128, mix0=512 + p * 128, heads=(2 * p, 2 * p + 1), pq=PC_DQ, pk=PC_DK))
    return us


def tiles_for(kind, qb):
    out = []
    if kind in ("A", "B"):
        for kt in range(4 * qb + 4):
            if kt < 4 * qb:
                out.append((kt, 0, 512, None))
            else:
                j = kt - 4 * qb
                out.append((kt, 128 * j, 512, 0))
        return out
    span = 256 if kind == "C" else 640
    back = 1 if kind == "C" else 4
    kts = [kt for kt in range(4 * qb - back, 4 * qb + 4) if kt >= 0]
    for kt in kts:
        lo = max(128 * kt, 512 * qb)
        hi = min(128 * kt + span, 512 * qb + 512)
        if hi > lo:
            out.append((kt, lo - 512 * qb, hi - 512 * qb, lo - 128 * kt))
    return out


def build(debug=False, nlayers=2, max_units=None):
    nc = bass.Bass("TRN2", target_bir_lowering=False)
    kb = KB(nc)
    st = kb.stack

    def dram(name, shape, dt, kind="ExternalInput"):
        return nc.dram_tensor(name, shape, dt, kind=kind).ap()

    x_d = dram("x", [S, DM], F32)
    win_d = [dram("w_in0", [DM, P_EVEN], F32), dram("w_in1", [DM, P_ODD], F32)]
    wout_d = [dram("w_out0", [DM, DM], F32), dram("w_out1", [DM, DM], F32)]
    par_d = dram("params", [128, NPAR], F32)
    ident_d = dram("ident", [128, 128], BF16)
    bones_d = dram("bones", [128, 128], BF16)
    aones_d = dram("aones", [128, 128], BF16)
    augA_d = dram("augA", [4, 12, S], BF16)
    onesrow_d = dram("onesrow", [8, S], BF16)
    corrA_d = dram("corrA", [4, 128, 128], F32)
    corrB_d = dram("corrB", [128, 128], F32)
    corrC_d = dram("corrC", [8, 128, 256], F32)
    relD_d = dram("relD", [8, 128, 640], F32)
    maskD_d = dram("maskD", [128, 640], F32)
    y_d = dram("y", [S, DM], F32, kind="ExternalOutput")
    skind = "ExternalOutput" if debug else "Internal"
    mixT_d = dram("mixT_scr", [DM, S], BF16, kind=skind)
    x1_d = dram("x1_scr", [S, DM], F32, kind=skind)
    augB_d = dram("augB_scr", [8, 12, S], BF16, kind=skind)

    uniq = [0]

    def sb(stack, name, shape, dt):
        uniq[0] += 1
        return stack.enter_context(nc.sbuf_tensor("%s_%d" % (name, uniq[0]), shape, dt))

    xnT = sb(st, "xnT", [128, 8, S], BF16)
    xnT_b = [Buf("xnT%d" % i) for i in range(NBLK)]
    ident = sb(st, "ident_sb", [128, 128], BF16)
    bones = sb(st, "bones_sb", [128, 128], BF16)
    aones = sb(st, "aones_sb", [128, 128], BF16)
    par = sb(st, "par_sb", [128, NPAR], F32)
    der = sb(st, "der_sb", [128, 32], F32)
    esink = sb(st, "esink_sb", [128, 8], F32)
    cst_b = Buf("consts")
    par_b = Buf("par")
    der_b = Buf("der")
    ps = [st.enter_context(nc.psum_tensor("ps%d" % i, [128, 512], F32)) for i in range(8)]
    ps_b = [Buf("ps%d" % i) for i in range(8)]
    psrr = [0]

    def psum_next(allowed=range(8)):
        allowed = list(allowed)
        i = allowed[psrr[0] % len(allowed)]
        psrr[0] += 1
        return ps[i], ps_b[i]

    mixblk_b = [Buf("mixblk%d" % i) for i in range(NBLK)]
    x1_b = Buf("x1scr")
    y_b = Buf("y")
    augB_b = Buf("augB")

    kb.dma("sp", ident[:], ident_d[:, :], writes=[cst_b])
    kb.dma("sp", bones[:], bones_d[:, :], writes=[cst_b], par=True)
    kb.dma("sp", aones[:], aones_d[:, :], writes=[cst_b], par=True)
    kb.dma("sp", par[:], par_d[:, :], writes=[par_b])

    DQ = {PC_AQ: 0, PC_BQ: 1, PC_CQ: 2, PC_DQ: 3}
    for pc, dc_ in DQ.items():
        kb.op("dve", lambda e, pc=pc, dc_=dc_: e.tensor_scalar(out=der[:, dc_:dc_ + 1], in0=par[:, pc:pc + 1], scalar1=0.125,
                                                             scalar2=None, op0=ALU.mult), reads=[par_b], writes=[der_b])
    kb.op("dve", lambda e: e.tensor_scalar(out=der[:, 4:5], in0=par[:, PC_SUB:PC_SUB + 1], scalar1=0.8, scalar2=None,
                                           op0=ALU.mult), reads=[par_b], writes=[der_b])
    kb.op("dve", lambda e: e.tensor_scalar(out=der[:, 6:7], in0=par[:, PC_FB:PC_FB + 1], scalar1=-1.0, scalar2=None,
                                           op0=ALU.mult), reads=[par_b], writes=[der_b])
    with contextlib.ExitStack() as s0:
        lt = sb(s0, "lamtmp", [128, 128], F32)
        lt_b = Buf("lt")
        kb.op("dve", lambda e: e.tensor_tensor(out=lt[:, 0:64], in0=par[:, PC_LQ1:PC_LQ1 + 64], in1=par[:, PC_LK1:PC_LK1 + 64],
                                               op=ALU.mult), reads=[par_b], writes=[lt_b])
        kb.op("dve", lambda e: e.tensor_tensor(out=lt[:, 64:128], in0=par[:, PC_LQ2:PC_LQ2 + 64], in1=par[:, PC_LK2:PC_LK2 + 64],
                                               op=ALU.mult), reads=[par_b], writes=[lt_b])
        kb.op("dve", lambda e: e.reduce_sum(out=der[:, 8:9], in_=lt[:, 0:64], axis=AX.X), reads=[lt_b], writes=[der_b])
        kb.op("dve", lambda e: e.reduce_sum(out=der[:, 9:10], in_=lt[:, 64:128], axis=AX.X), reads=[lt_b], writes=[der_b])
        kb.op("act", lambda e: e.activation(out=der[:, 10:12], in_=der[:, 8:10], func=AF.Exp), reads=[der_b], writes=[der_b])
        kb.op("dve", lambda e: e.tensor_tensor(out=der[:, 12:13], in0=der[:, 11:12], in1=der[:, 10:11], op=ALU.subtract),
              reads=[der_b], writes=[der_b])
        kb.op("dve", lambda e: e.tensor_scalar(out=der[:, 5:6], in0=der[:, 12:13], scalar1=-0.2, scalar2=None, op0=ALU.add),
              reads=[der_b], writes=[der_b])
        kb.op("act", lambda e: e.activation(out=esink[:, :], in_=par[:, PC_SINK:PC_SINK + 8], func=AF.Exp), reads=[par_b],
              writes=[der_b])
        kb.barrier()

    def norm_tile(nb, xs, xs_b, tt):
        i = tt % 2
        sq, sq_b = nb["sq"][i], nb["sq_b"][i]
        sm, sm_b = nb["sm"][i], nb["sm_b"][i]
        xb, xb_b = nb["xb"][i], nb["xb_b"][i]
        kb.op("act", lambda e: e.activation(out=sq[:, :], in_=xs, func=AF.Square), reads=[xs_b], writes=[sq_b])
        kb.op("dve", lambda e: e.reduce_sum(out=sm[:, 0:1], in_=sq[:, :], axis=AX.X), reads=[sq_b], writes=[sm_b])
        kb.op("act", lambda e: e.activation(out=sm[:, 1:2], in_=sm[:, 0:1], func=AF.Ln, scale=1.0 / DM, bias=EPS),
              reads=[sm_b], writes=[sm_b])
        kb.op("act", lambda e: e.activation(out=sm[:, 2:3], in_=sm[:, 1:2], func=AF.Exp, scale=-0.5), reads=[sm_b], writes=[sm_b])
        kb.op("dve", lambda e: e.tensor_scalar(out=xb[:, :], in0=xs, scalar1=sm[:, 2:3], scalar2=None, op0=ALU.mult),
              reads=[xs_b, sm_b], writes=[xb_b])
        pt, pt_b = psum_next()
        ptv = pt[:, :].bitcast(BF16)

        def tr(pe):
            ins = None
            for dc in range(8):
                ins = pe.transpose(ptv[:, dc * 128:(dc + 1) * 128], xb[:, dc * 128:(dc + 1) * 128], ident[:, :])
            return ins
        kb.op("pe", tr, reads=[xb_b, cst_b], writes=[pt_b])
        tb = tt // 4
        kb.op("act", lambda e: e.copy(out=xnT[:, :, tt * 128:(tt + 1) * 128],
                                      in_=ptv.rearrange("p (c t) -> p c t", c=8)),
              reads=[pt_b], writes=[xnT_b[tb]])

    def norm_bufs(stack, pfx):
        nb = dict(sq=[], sq_b=[], sm=[], sm_b=[], xb=[], xb_b=[])
        for i in range(2):
            nb["sq"].append(sb(stack, pfx + "sq%d" % i, [128, DM], F32))
            nb["sq_b"].append(Buf())
            nb["sm"].append(sb(stack, pfx + "sm%d" % i, [128, 4], F32))
            nb["sm_b"].append(Buf())
            nb["xb"].append(sb(stack, pfx + "xb%d" % i, [128, DM], BF16))
            nb["xb_b"].append(Buf())
        return nb

    with contextlib.ExitStack() as s1:
        nb = norm_bufs(s1, "n0")
        xst = [sb(s1, "xst%d" % i, [128, DM], F32) for i in range(2)]
        xst_b = [Buf() for _ in range(2)]
        for tt in range(NT):
            i = tt % 2
            kb.dma("sp", xst[i][:, :], x_d[tt * 128:(tt + 1) * 128, :], writes=[xst_b[i]])
            norm_tile(nb, xst[i][:, :], xst_b[i], tt)
        kb.barrier()

    def units_phase(layer):
        units = units_l0() if layer == 0 else units_l1()
        if max_units is not None:
            units = units[:max_units] if isinstance(max_units, int) else [units[i] for i in max_units]
        lng = PC_LNG0 if layer == 0 else PC_LNG1
        win = win_d[layer].rearrange("(c p) n -> p c n", p=128)
        KK = 70 if layer == 0 else 64
        with contextlib.ExitStack() as s2:
            qT = [sb(s2, "qT%d" % m, [128, S], BF16) for m in range(2)]
            kT = [sb(s2, "kT%d" % m, [128, S], BF16) for m in range(2)]
            qT_b = [Buf("qT%d" % m) for m in range(2)]
            kT_b = [Buf("kT%d" % m) for m in range(2)]
            Vaug = sb(s2, "Vaug", [128, NT, 2, 128], BF16)
            Vaug_b = Buf("Vaug")
            sg = sb(s2, "sg", [128, S], F32)
            sg_b = Buf("sg")
            wbf = sb(s2, "wbf", [128, 8, 512], BF16)
            wbf_b = Buf("wbf")
            wst = [sb(s2, "wst%d" % i, [128, 8, 128], F32) for i in range(2)]
            wst_b = [Buf() for _ in range(2)]
            Pt = [sb(s2, "P%d" % i, [128, 512], BF16) for i in range(4)]
            Pt_b = [Buf() for _ in range(4)]
            ncorr = 640 if layer == 1 else 128
            corr = [sb(s2, "corr%d" % m, [128, ncorr], F32) for m in range(2)]
            corr_b = [Buf() for _ in range(2)]
            if layer == 1:
                maskD = sb(s2, "maskD", [128, 640], F32)
                maskD_b = Buf()
                kb.dma("sp", maskD[:, :], maskD_d[:, :], writes=[maskD_b])
            rl = [sb(s2, "rl%d" % i, [128, 512], F32) for i in range(2)]
            rl_b = [Buf() for _ in range(2)]
            on = [sb(s2, "on%d" % i, [128, 512], F32) for i in range(2)]
            on_b = [Buf() for _ in range(2)]
            lnt = [sb(s2, "lnt%d" % i, [128, 512], F32) for i in range(2)]
            lnt_b = [Buf() for _ in range(2)]
            sqt = [sb(s2, "sqt%d" % i, [128, 512], BF16) for i in range(2)]
            sqt_b = [Buf() for _ in range(2)]
            lnv = [sb(s2, "lnv%d" % i, [128, 512], F32) for i in range(2)]
            lnv_b = [Buf() for _ in range(2)]
            rstd = [sb(s2, "rstd%d" % i, [128, 512], F32) for i in range(2)]
            rstd_b = [Buf() for _ in range(2)]
            nrm_rr = [0]
            mx = [sb(s2, "mx%d" % i, [128, 512], BF16) for i in range(2)]
            mx_b = [Buf() for _ in range(2)]

            kb.op("pool", lambda e: e.memset(Vaug[:, :, 0, 64:128], 1.0), writes=[Vaug_b])
            kb.op("pool", lambda e: e.memset(Vaug[:, :, 1, 0:64], 1.0), writes=[Vaug_b])

            wst4 = [wst[0], wst[1], sb(s2, "wst2", [128, 8, 128], F32), sb(s2, "wst3", [128, 8, 128], F32)]
            wst4_b = [wst_b[0], wst_b[1], Buf(), Buf()]
            accS = [sb(s2, "accS%d" % i, [128, 512], F32) for i in range(4)]
            accS_b = [Buf() for _ in range(4)]
            Ecorr = [sb(s2, "Ecorr%d" % m, [128, ncorr], BF16) for m in range(2)]
            Ecorr_b = [Buf() for _ in range(2)]
            bg = []

            def drain(k):
                for _ in range(k):
                    if bg:
                        bg.pop(0)()

            def drain_all():
                while bg:
                    bg.pop(0)()

            def prep_weights(u, defer):
                groups = [(u["qcol"], 128), (u["kcol"], u["kw"]), (u["vcol"], u["vw"]), (u["gcol"], 128)]
                for gi, (col, wd) in enumerate(groups):
                    kb.dma("sp", wst4[gi][:, :, 0:wd], win[:, :, col:col + wd], writes=[wst4_b[gi]])
                    for dc in range(8):
                        def cast(gi=gi, dc=dc, wd=wd):
                            kb.op("dve", lambda e: e.tensor_scalar(
                                out=wbf[:, dc, gi * 128:gi * 128 + wd], in0=wst4[gi][:, dc, 0:wd],
                                scalar1=par[:, lng + dc:lng + dc + 1], scalar2=None, op0=ALU.mult),
                                reads=[wst4_b[gi], par_b], writes=[wbf_b])
                        if defer:
                            bg.append(cast)
                        else:
                            cast()

            def mm_group(pdst, M, c0, t0):
                def f(pe):
                    ins = None
                    for dc in range(8):
                        ins = mm(pe, pdst[0:M, :], wbf[:, dc, c0:c0 + M], xnT[:, dc, t0:t0 + 512], dc == 0, dc == 7)
                    return ins
                return f

            def emit_projection(u):
                dq = DQ[u["pq"]]
                kw, vw, pk = u["kw"], u["vw"], u["pk"]

                def stageA(tb):
                    p = tb % 2
                    t0 = tb * 512
                    xb_ = xnT_b[tb]
                    kb.op("pe", mm_group(ps[0 + 3 * p], 128, 0, t0), reads=[wbf_b, xb_], writes=[ps_b[0 + 3 * p]])
                    kb.op("pe", mm_group(ps[1 + 3 * p], kw, 128, t0), reads=[wbf_b, xb_], writes=[ps_b[1 + 3 * p]])
                    pv_, pv_b_ = ps[2 + 3 * p], ps_b[2 + 3 * p]

                    def vproj(pe):
                        ins = None
                        for j in range(4):
                            for dc in range(8):
                                ins = mm(pe, pv_[:, j * vw:(j + 1) * vw], xnT[:, dc, t0 + j * 128:t0 + (j + 1) * 128],
                                         wbf[:, dc, 256:256 + vw], dc == 0, dc == 7)
                        return ins
                    kb.op("pe", vproj, reads=[wbf_b, xb_], writes=[pv_b_])
                    pvv = pv_[:, 0:4 * vw].rearrange("p (j c) -> p j c", j=4)
                    hi = pvv[:, :, 64:128] if vw == 128 else pvv[:, :, 0:64]
                    kb.op("dve", lambda e: e.tensor_copy(out=Vaug[:, 4 * tb:4 * tb + 4, 0, 0:64], in_=pvv[:, :, 0:64]),
                          reads=[pv_b_], writes=[Vaug_b])
                    kb.op("dve", lambda e: e.tensor_copy(out=Vaug[:, 4 * tb:4 * tb + 4, 1, 64:128], in_=hi),
                          reads=[pv_b_], writes=[Vaug_b])

                def stageB(tb):
                    p = tb % 2
                    t0 = tb * 512
                    specs = [(ps[0 + 3 * p], ps_b[0 + 3 * p], 128, (lambda rows: der[rows, dq:dq + 1]), qT, qT_b, 6, 0),
                             (ps[1 + 3 * p], ps_b[1 + 3 * p], kw, (lambda rows: par[rows, pk:pk + 1]), kT, kT_b, 7, 1)]
                    for (pq_, pq_b_, R, gcolap, dst, dst_b, sbank, ri) in specs:
                        kb.op("act", lambda e, pq_=pq_, R=R, ri=ri: e.activation(out=sqt[ri][0:R, :], in_=pq_[0:R, :], func=AF.Square),
                              reads=[pq_b_], writes=[sqt_b[ri]])
                    for (pq_, pq_b_, R, gcolap, dst, dst_b, sbank, ri) in specs:
                        kb.op("pe", lambda pe, R=R, ri=ri, sbank=sbank: mm(pe, ps[sbank][0:R, :], bones[0:R, 0:R], sqt[ri][0:R, :], True, True),
                              reads=[sqt_b[ri], cst_b], writes=[ps_b[sbank]])
                    for (pq_, pq_b_, R, gcolap, dst, dst_b, sbank, ri) in specs:
                        kb.op("act", lambda e, R=R, ri=ri, sbank=sbank: e.activation(out=lnv[ri][0:R, :], in_=ps[sbank][0:R, :], func=AF.Ln,
                                                                                       scale=1.0 / 64, bias=EPS),
                              reads=[ps_b[sbank]], writes=[lnv_b[ri]])
                    for (pq_, pq_b_, R, gcolap, dst, dst_b, sbank, ri) in specs:
                        kb.op("act", lambda e, R=R, ri=ri: e.activation(out=rstd[ri][0:R, :], in_=lnv[ri][0:R, :], func=AF.Exp, scale=-0.5),
                              reads=[lnv_b[ri]], writes=[rstd_b[ri]])
                    for (pq_, pq_b_, R, gcolap, dst, dst_b, sbank, ri) in specs:
                        for m in range(R // 64):
                            rows = slice(64 * m, 64 * m + 64)
                            kb.op("dve", lambda e, m=m, rows=rows, pq_=pq_, gcolap=gcolap, dst=dst, ri=ri: e.scalar_tensor_tensor(
                                out=dst[m][0:64, t0:t0 + 512], in0=pq_[rows, :], scalar=gcolap(rows), in1=rstd[ri][rows, :],
                                op0=ALU.mult, op1=ALU.mult), reads=[pq_b_, rstd_b[ri], par_b, der_b], writes=[dst_b[m]])

                stageA(0)
                for tb in range(NBLK):
                    if tb + 1 < NBLK:
                        stageA(tb + 1)
                    stageB(tb)
                for tb in range(NBLK):
                    t0 = tb * 512
                    bi = tb % 6
                    kb.op("pe", mm_group(ps[bi], 128, 384, t0), reads=[wbf_b, xnT_b[tb]], writes=[ps_b[bi]])
                    kb.op("act", lambda e, bi=bi, t0=t0: e.activation(out=sg[:, t0:t0 + 512], in_=ps[bi][:, :], func=AF.Silu),
                          reads=[ps_b[bi]], writes=[sg_b])

            if units:
                prep_weights(units[0], False)
            for ui, u in enumerate(units):
                kind = u["kind"]
                drain_all()
                if layer == 0:
                    for m in range(2):
                        if kind == "A":
                            srcq = augA_d[u["heads"][0], 0:6, :]
                            srck = augA_d[u["heads"][0], 6:12, :]
                            rd = []
                        else:
                            srcq = augB_d[u["heads"][m], 0:6, :]
                            srck = augB_d[u["heads"][m], 6:12, :]
                            rd = [augB_b]
                        kb.dma("sp", qT[m][64:70, :], srcq, reads=rd, writes=[qT_b[m]])
                        kb.dma("sp", kT[m][64:70, :], srck, reads=rd, writes=[kT_b[m]])
                    if kind == "A":
                        kb.dma("sp", corr[0][:, 0:128], corrA_d[u["heads"][0], :, :], writes=[corr_b[0]])
                    else:
                        kb.dma("sp", corr[0][:, 0:128], corrB_d[:, :], writes=[corr_b[0]])
                else:
                    for m in range(2):
                        h = u["heads"][m]
                        if kind == "C":
                            kb.dma("sp", corr[m][:, 0:256], corrC_d[h, :, :], writes=[corr_b[m]])
                            nE = 256
                        else:
                            kb.dma("sp", corr[m][:, 0:640], relD_d[h, :, :], writes=[corr_b[m]])
                            kb.op("dve", lambda e, m=m: e.tensor_tensor(out=corr[m][:, :], in0=corr[m][:, :], in1=maskD[:, :],
                                                                        op=ALU.add), reads=[maskD_b], writes=[corr_b[m]])
                            nE = 640
                        kb.op("act", lambda e, m=m, nE=nE: e.activation(out=Ecorr[m][:, 0:nE], in_=corr[m][:, 0:nE], func=AF.Exp),
                              reads=[corr_b[m]], writes=[Ecorr_b[m]])
                emit_projection(u)
                if ui + 1 < len(units):
                    prep_weights(units[ui + 1], True)
                if kind == "A":
                    maps = [dict(q=0, k=0, pv=[(0, 0), (1, 1)], corr=0), dict(q=1, k=1, pv=[(0, 2), (1, 3)], corr=0)]
                    nacc = 4
                elif kind == "C":
                    maps = [dict(q=0, k=0, pv=[(0, 0)], corr=0), dict(q=1, k=0, pv=[(1, 1)], corr=1)]
                    nacc = 2
                else:
                    maps = [dict(q=0, k=0, pv=[(0, 0)], corr=0), dict(q=1, k=1, pv=[(1, 1)], corr=0 if layer == 0 else 1)]
                    nacc = 2
                dbl = (nacc == 2)
                sbanks = list(range(4, 8))
                nS = len(sbanks)
                skew = 2
                for qb in range(NBLK):
                    Q0 = qb * 512
                    abase = 2 * (qb % 2) if dbl else 0
                    items = []
                    for (kt, q0, q1, c0) in tiles_for(kind, qb):
                        for mi, mp in enumerate(maps):
                            items.append((kt, q0, q1, c0, mi))
                    started = set()

                    def emit_qk(w):
                        kt, q0, q1, c0, mi = items[w]
                        mp = maps[mi]
                        bi = sbanks[w % nS]
                        Sx, Sb = ps[bi], ps_b[bi]
                        kb.op("pe", lambda pe: mm(pe, Sx[:, q0:q1], kT[mp["k"]][0:KK, kt * 128:(kt + 1) * 128],
                                                  qT[mp["q"]][0:KK, Q0 + q0:Q0 + q1], True, True),
                              reads=[kT_b[mp["k"]], qT_b[mp["q"]]], writes=[Sb])
                        cm = mp["corr"]
                        if c0 is not None and layer == 0:
                            a0, a1 = q0, q0 + 128
                            cap = corr[cm][:, 0:128]
                            kb.op("dve", lambda e: e.tensor_tensor(out=Sx[:, a0:a1], in0=Sx[:, a0:a1], in1=cap, op=ALU.add),
                                  reads=[corr_b[cm]], writes=[Sb])
                        pi = w % 4
                        kb.op("act", lambda e: e.activation(out=Pt[pi][:, q0:q1], in_=Sx[:, q0:q1], func=AF.Exp),
                              reads=[Sb], writes=[Pt_b[pi]])
                        if c0 is not None and layer == 1:
                            eap = Ecorr[cm][:, c0:c0 + (q1 - q0)]
                            kb.op("dve", lambda e: e.tensor_tensor(out=Pt[pi][:, q0:q1], in0=Pt[pi][:, q0:q1], in1=eap, op=ALU.mult),
                                  reads=[Ecorr_b[cm]], writes=[Pt_b[pi]])

                    def emit_pv(w):
                        kt, q0, q1, c0, mi = items[w]
                        mp = maps[mi]
                        pi = w % 4
                        for (slot, ai) in mp["pv"]:
                            first = ai not in started
                            started.add(ai)
                            kb.op("pe", lambda pe, slot=slot, ai=ai, first=first: mm(
                                pe, ps[abase + ai][:, q0:q1], Vaug[:, kt, slot, :], Pt[pi][:, q0:q1], first, True, skip=True),
                                reads=[Pt_b[pi], Vaug_b], writes=[ps_b[abase + ai]])

                    n = len(items)
                    for w in range(n + skew):
                        if w < n:
                            emit_qk(w)
                        if w - skew >= 0:
                            emit_pv(w - skew)
                        drain(1)
                    drain_all()

                    mxi = qb % 2
                    steps = []
                    if kind == "A":
                        for a in range(4):
                            en = "act" if a % 2 == 0 else "dve"
                            if en == "act":
                                kb.op("act", lambda e, a=a: e.copy(out=accS[a][:, :], in_=ps[a][:, :]), reads=[ps_b[a]], writes=[accS_b[a]])
                            else:
                                kb.op("dve", lambda e, a=a: e.tensor_copy(out=accS[a][:, :], in_=ps[a][:, :]), reads=[ps_b[a]],
                                      writes=[accS_b[a]])
                        for a in range(4):
                            c, half = a // 2, a % 2
                            orow = slice(0, 64) if half == 0 else slice(64, 128)
                            lrow = slice(64, 128) if half == 0 else slice(0, 64)
                            steps.append(lambda a=a, c=c, orow=orow, lrow=lrow: kb.op("act", lambda e: e.activation(
                                out=lnt[c][orow, :], in_=accS[a][lrow, :], func=AF.Ln), reads=[accS_b[a]], writes=[lnt_b[c]]))
                        for c in range(2):
                            steps.append(lambda c=c: kb.op("act", lambda e: e.activation(out=rl[c][:, :], in_=lnt[c][:, :], func=AF.Exp,
                                                                                      scale=-1.0), reads=[lnt_b[c]], writes=[rl_b[c]]))
                        for a in range(4):
                            c, half = a // 2, a % 2
                            orow = slice(0, 64) if half == 0 else slice(64, 128)
                            steps.append(lambda a=a, c=c, orow=orow: kb.op("dve", lambda e: e.tensor_tensor(
                                out=on[c][orow, :], in0=accS[a][orow, :], in1=rl[c][orow, :], op=ALU.mult),
                                reads=[accS_b[a], rl_b[c]], writes=[on_b[c]]))
                        steps.append(lambda: kb.op("dve", lambda e: e.scalar_tensor_tensor(
                            out=lnt[0][:, :], in0=on[1][:, :], scalar=der[:, 5:6], in1=on[0][:, :], op0=ALU.mult, op1=ALU.add),
                            reads=[on_b[0], on_b[1], der_b], writes=[lnt_b[0]]))
                        steps.append(lambda: kb.op("act", lambda e: e.activation(out=sqt[0][:, :], in_=lnt[0][:, :], func=AF.Square),
                                                   reads=[lnt_b[0]], writes=[sqt_b[0]]))
                        pbank = sbanks[(n + 1) % nS]
                        def ss_step(pbank=pbank):
                            kb.op("pe", lambda pe: mm(pe, ps[pbank][:, :], aones[:, :], sqt[0][:, :], True, True),
                                  reads=[sqt_b[0], cst_b], writes=[ps_b[pbank]])
                            kb.op("act", lambda e: e.activation(out=lnv[0][:, :], in_=ps[pbank][:, :], func=AF.Ln,
                                                                scale=1.0 / 128, bias=EPS),
                                  reads=[ps_b[pbank]], writes=[lnv_b[0]])
                        steps.append(ss_step)
                        steps.append(lambda: kb.op("act", lambda e: e.activation(out=rstd[0][:, :], in_=lnv[0][:, :], func=AF.Exp, scale=-0.5),
                                                   reads=[lnv_b[0]], writes=[rstd_b[0]]))
                        steps.append(lambda: kb.op("dve", lambda e: e.scalar_tensor_tensor(
                            out=on[0][:, :], in0=lnt[0][:, :], scalar=der[:, 4:5], in1=rstd[0][:, :], op0=ALU.mult, op1=ALU.mult),
                            reads=[lnt_b[0], rstd_b[0], der_b], writes=[on_b[0]]))
                        steps.append(lambda mxi=mxi, Q0=Q0: kb.op("pool", lambda e: e.tensor_tensor(
                            out=mx[mxi][:, :], in0=on[0][:, :], in1=sg[:, Q0:Q0 + 512], op=ALU.mult),
                            reads=[on_b[0], sg_b], writes=[mx_b[mxi]]))
                    else:
                        fi = qb % 2
                        for m in range(2):
                            orow = slice(0, 64) if m == 0 else slice(64, 128)
                            lrow = slice(64, 128) if m == 0 else slice(0, 64)
                            am = abase + m
                            if kind == "C":
                                h = u["heads"][m]
                                steps.append(lambda am=am, orow=orow, lrow=lrow, h=h, fi=fi: kb.op("act", lambda e: e.activation(
                                    out=lnt[fi][orow, :], in_=ps[am][lrow, :], func=AF.Ln, scale=1.0, bias=esink[lrow, h:h + 1]),
                                    reads=[ps_b[am], der_b], writes=[lnt_b[fi]]))
                            else:
                                steps.append(lambda am=am, orow=orow, lrow=lrow, fi=fi: kb.op("act", lambda e: e.activation(
                                    out=lnt[fi][orow, :], in_=ps[am][lrow, :], func=AF.Ln), reads=[ps_b[am]], writes=[lnt_b[fi]]))
                        steps.append(lambda fi=fi: kb.op("act", lambda e: e.activation(out=rl[fi][:, :], in_=lnt[fi][:, :], func=AF.Exp,
                                                                                        scale=-1.0), reads=[lnt_b[fi]], writes=[rl_b[fi]]))
                        steps.append(lambda fi=fi, Q0=Q0: kb.op("pool", lambda e: e.tensor_tensor(
                            out=on[fi][:, :], in0=rl[fi][:, :], in1=sg[:, Q0:Q0 + 512], op=ALU.mult),
                            reads=[rl_b[fi], sg_b], writes=[on_b[fi]]))
                        for m in range(2):
                            orow = slice(0, 64) if m == 0 else slice(64, 128)
                            am = abase + m
                            steps.append(lambda am=am, orow=orow, fi=fi, mxi=mxi: kb.op("dve", lambda e: e.tensor_tensor(
                                out=mx[mxi][orow, :], in0=ps[am][orow, :], in1=on[fi][orow, :], op=ALU.mult),
                                reads=[ps_b[am], on_b[fi]], writes=[mx_b[mxi]]))
                    steps.append(lambda mxi=mxi, Q0=Q0, mix0=u["mix0"], qb=qb: kb.dma(
                        "pool", mixT_d[mix0:mix0 + 128, Q0:Q0 + 512], mx[mxi][:, :], reads=[mx_b[mxi]], writes=[mixblk_b[qb]]))
                    bg.extend(steps)
            drain_all()
            kb.barrier()

    def bprep():
        win = win_d[0].rearrange("(c p) n -> p c n", p=128)
        with contextlib.ExitStack() as s3:
            wst = [sb(s3, "wstb", [128, 8, 8], F32)]
            wst_b = [Buf()]
            wfb = sb(s3, "wfb", [128, 8, 8], BF16)
            wfb_b = Buf()
            l1p = sb(s3, "l1p", [8, S], F32)
            l1p_b = Buf()
            ones8 = sb(s3, "ones8", [8, S], F32)
            ones8_b = Buf()
            cum = sb(s3, "cum", [8, S], F32)
            cum_b = Buf()
            parts = [sb(s3, "part%d" % i, [8, S], BF16) for i in range(3)]
            nparts = [sb(s3, "npart%d" % i, [8, S], BF16) for i in range(3)]
            parts_b = Buf()
            res = sb(s3, "resid", [8, S], F32)
            res_b = Buf()
            kb.dma("sp", wst[0][:, :, 0:8], win[:, :, 4096:4104], writes=[wst_b[0]])
            for dc in range(8):
                kb.op("pool", lambda e, dc=dc: e.tensor_scalar(out=wfb[:, dc, :], in0=wst[0][:, dc, 0:8],
                                                               scalar1=par[:, PC_LNG0 + dc:PC_LNG0 + dc + 1], scalar2=None, op0=ALU.mult),
                      reads=[wst_b[0], par_b], writes=[wfb_b])
            kb.op("pool", lambda e: e.memset(ones8[:, :], 1.0), writes=[ones8_b])
            ones8h = sb(s3, "ones8h", [8, S], BF16)
            kb.op("pool", lambda e: e.memset(ones8h[:, :], 1.0), writes=[ones8_b])
            for tb in range(NBLK):
                t0 = tb * 512
                pf, pf_b = psum_next()

                def f(pe, t0=t0):
                    ins = None
                    for dc in range(8):
                        ins = mm(pe, pf[0:8, :], wfb[:, dc, :], xnT[:, dc, t0:t0 + 512], dc == 0, dc == 7)
                    return ins
                kb.op("pe", f, reads=[wfb_b, xnT_b[tb]], writes=[pf_b])
                kb.op("act", lambda e, t0=t0: e.activation(out=l1p[:, t0:t0 + 512], in_=pf[0:8, :], func=AF.Exp, scale=-1.0,
                                                          bias=der[0:8, 6:7]), reads=[pf_b, der_b], writes=[l1p_b])
                kb.op("act", lambda e, t0=t0: e.activation(out=l1p[:, t0:t0 + 512], in_=l1p[:, t0:t0 + 512], func=AF.Ln, scale=1.0,
                                                          bias=1.0), reads=[], writes=[l1p_b])
            kb.op("dve", lambda e: e.tensor_tensor_scan(out=cum[:, :], data0=ones8[:, :], data1=l1p[:, :], initial=0.0,
                                                        op0=ALU.mult, op1=ALU.add), reads=[ones8_b, l1p_b], writes=[cum_b])
            kb.op("dve", lambda e: e.tensor_copy(out=parts[0][:, :], in_=cum[:, :]), reads=[cum_b], writes=[parts_b])
            kb.op("dve", lambda e: e.tensor_tensor(out=res[:, :], in0=cum[:, :], in1=parts[0][:, :], op=ALU.subtract),
                  reads=[cum_b], writes=[res_b, parts_b])
            kb.op("dve", lambda e: e.tensor_copy(out=parts[1][:, :], in_=res[:, :]), reads=[res_b], writes=[parts_b])
            kb.op("dve", lambda e: e.tensor_tensor(out=res[:, :], in0=res[:, :], in1=parts[1][:, :], op=ALU.subtract),
                  reads=[], writes=[res_b, parts_b])
            kb.op("dve", lambda e: e.tensor_copy(out=parts[2][:, :], in_=res[:, :]), reads=[res_b], writes=[parts_b])
            for i in range(3):
                kb.op("dve", lambda e, i=i: e.tensor_scalar(out=nparts[i][:, :], in0=parts[i][:, :], scalar1=-1.0, scalar2=None,
                                                            op0=ALU.mult), reads=[], writes=[parts_b])
            first = True
            for r in range(3):
                kb.dma("pool", augB_d[:, r, :], ones8h[:, :], reads=[ones8_b], writes=[augB_b], par=not first)
                first = False
                kb.dma("pool", augB_d[:, 9 + r, :], ones8h[:, :], reads=[ones8_b], writes=[augB_b], par=True)
                kb.dma("pool", augB_d[:, 3 + r, :], nparts[r][:, :], reads=[parts_b], writes=[augB_b], par=True)
                kb.dma("pool", augB_d[:, 6 + r, :], parts[r][:, :], reads=[parts_b], writes=[augB_b], par=True)
            kb.barrier()

    def outproj_phase(layer):
        res_src = x_d if layer == 0 else x1_d
        last = (layer == nlayers - 1)
        with contextlib.ExitStack() as s4:
            wo = sb(s4, "wo", [128, 8, DM], BF16)
            wo_b = Buf()
            wos = [sb(s4, "wos%d" % i, [128, DM], F32) for i in range(2)]
            wos_b = [Buf() for _ in range(2)]
            mt = [sb(s4, "mt%d" % i, [128, 8, 512], BF16) for i in range(2)]
            mt_b = [Buf() for _ in range(2)]
            xr = [sb(s4, "xr%d" % i, [128, DM], F32) for i in range(2)]
            xr_b = [Buf() for _ in range(2)]
            xo = [sb(s4, "xo%d" % i, [128, DM], F32) for i in range(2)]
            xo_b = [Buf() for _ in range(2)]
            nb = norm_bufs(s4, "n1") if not last else None
            wod = wout_d[layer].rearrange("(c p) n -> p c n", p=128)
            for mc in range(8):
                i = mc % 2
                kb.dma("sp", wos[i][:, :], wod[:, mc, :], writes=[wos_b[i]])
                kb.op("pool", lambda e, i=i, mc=mc: e.tensor_copy(out=wo[:, mc, :], in_=wos[i][:, :]), reads=[wos_b[i]], writes=[wo_b])
            mixv = mixT_d.rearrange("(c p) t -> p c t", p=128)
            for tb in range(NBLK):
                bi = tb % 2
                kb.dma("sp", mt[bi][:, :, :], mixv[:, :, tb * 512:(tb + 1) * 512], reads=[mixblk_b[tb]], writes=[mt_b[bi]])
                for j in range(4):
                    tt = 4 * tb + j
                    i = tt % 2
                    rd = [x1_b] if layer == 1 else []
                    kb.dma("sp", xr[i][:, :], res_src[tt * 128:(tt + 1) * 128, :], reads=rd, writes=[xr_b[i]])
                    for half in range(2):
                        po, po_b = psum_next()

                        def f(pe, po=po, half=half, j=j, bi=bi):
                            ins = None
                            for mc in range(8):
                                ins = mm(pe, po[:, :], mt[bi][:, mc, j * 128:(j + 1) * 128], wo[:, mc, half * 512:(half + 1) * 512],
                                         mc == 0, mc == 7)
                            return ins
                        kb.op("pe", f, reads=[mt_b[bi], wo_b], writes=[po_b])
                        kb.op("dve", lambda e, po=po, half=half, i=i: e.tensor_tensor(
                            out=xo[i][:, half * 512:(half + 1) * 512], in0=po[:, :], in1=xr[i][:, half * 512:(half + 1) * 512], op=ALU.add),
                            reads=[po_b, xr_b[i]], writes=[xo_b[i]])
                    if last:
                        kb.dma("pool", y_d[tt * 128:(tt + 1) * 128, :], xo[i][:, :], reads=[xo_b[i]], writes=[y_b])
                    else:
                        kb.dma("pool", x1_d[tt * 128:(tt + 1) * 128, :], xo[i][:, :], reads=[xo_b[i]], writes=[x1_b])
                        norm_tile(nb, xo[i][:, :], xo_b[i], tt)
            kb.barrier()

    for layer in range(nlayers):
        if layer == 0:
            bprep()
        units_phase(layer)
        outproj_phase(layer)
    kb.barrier()
    return nc


def _consts():
    bf = ml_dtypes.bfloat16
    c = {}
    c["ident"] = np.eye(128, dtype=np.float32).astype(bf)
    bo = np.zeros((128, 128), np.float32)
    bo[0:64, 0:64] = 1.0
    bo[64:128, 64:128] = 1.0
    c["bones"] = bo.astype(bf)
    c["aones"] = np.ones((128, 128), np.float32).astype(bf)
    c["onesrow"] = np.ones((8, S), np.float32).astype(bf)
    t = np.arange(S)
    hi = (t // 256) * 256
    lo = t % 256
    augA = np.zeros((4, 12, S), np.float32)
    for h in range(4):
        s = 2.0 ** (-8.0 * (h + 1) / 4)
        augA[h, 0] = 1.0
        augA[h, 1] = 1.0
        augA[h, 3] = -s * hi
        augA[h, 4] = -s * lo
        augA[h, 6] = s * hi
        augA[h, 7] = s * lo
        augA[h, 9] = 1.0
        augA[h, 10] = 1.0
    c["augA"] = augA.astype(bf)
    k = np.arange(128)[:, None]
    i = np.arange(128)[None, :]
    corrA = np.zeros((4, 128, 128), np.float32)
    for h in range(4):
        s = 2.0 ** (-8.0 * (h + 1) / 4)
        m = np.where(k <= i, 0.0, np.where((k // 64) == (i // 64), -2.0 * s * (k - i), NEG))
        corrA[h] = m
    c["corrA"] = corrA
    c["corrB"] = np.where(k <= i, 0.0, NEG).astype(np.float32)
    i2 = np.arange(256)[None, :]
    corrC = np.zeros((8, 128, 256), np.float32)
    for h in range(8):
        s = 2.0 ** (-8.0 * (h + 1) / 8)
        cd = (i2 // 64) - (k // 64)
        corrC[h] = np.where((cd >= 0) & (cd <= 2), -s * np.abs(i2 - k), NEG)
    c["corrC"] = corrC
    i3 = np.arange(640)[None, :]
    cd = (i3 // 64) - (k // 64)
    c["maskD"] = np.where((cd >= 0) & (cd <= 8), 0.0, NEG).astype(np.float32)
    c["ridxD"] = np.clip(i3 - k, -63, 256) + 63
    return c


_C = None


def _host_inputs(inputs):
    global _C
    if _C is None:
        _C = _consts()
    c = _C
    f = lambda a: np.ascontiguousarray(np.asarray(a), dtype=np.float32)
    par = np.zeros((128, NPAR), np.float32)
    par[:, PC_LNG0:PC_LNG0 + 8] = f(inputs["even_ln_g"])[0].reshape(8, 128).T
    par[:, PC_LNG1:PC_LNG1 + 8] = f(inputs["odd_ln_g"])[0].reshape(8, 128).T
    for col, name in ((PC_AQ, "a_q_norm_g"), (PC_AK, "a_k_norm_g"), (PC_BQ, "b_q_norm_g"), (PC_BK, "b_k_norm_g"),
                      (PC_CQ, "c_q_norm_g"), (PC_CK, "c_k_norm_g"), (PC_DQ, "d_q_norm_g"), (PC_DK, "d_k_norm_g")):
        par[:, col] = np.tile(f(inputs[name])[0], 2)
    par[:, PC_SUB] = f(inputs["a_subln_g"])[0]
    par[0:8, PC_FB] = f(inputs["b_forget_bias"])[0]
    par[:, PC_SINK:PC_SINK + 8] = np.broadcast_to(f(inputs["c_sinks"])[0], (128, 8))
    for col, name in ((PC_LQ1, "a_lambda_q1"), (PC_LK1, "a_lambda_k1"), (PC_LQ2, "a_lambda_q2"), (PC_LK2, "a_lambda_k2")):
        par[:, col:col + 64] = np.broadcast_to(f(inputs[name])[0], (128, 64))
    relD = np.ascontiguousarray(f(inputs["d_rel_bias"])[0][:, c["ridxD"]])
    shared = {
        "w_in0": f(inputs["even_w_in"])[0], "w_in1": f(inputs["odd_w_in"])[0],
        "w_out0": f(inputs["even_w_out"])[0], "w_out1": f(inputs["odd_w_out"])[0],
        "params": par, "ident": c["ident"], "bones": c["bones"], "aones": c["aones"], "augA": c["augA"],
        "onesrow": c["onesrow"], "corrA": c["corrA"], "corrB": c["corrB"], "corrC": c["corrC"], "relD": relD,
        "maskD": c["maskD"],
    }
    return shared


def kernel(**inputs):
    x = np.ascontiguousarray(np.asarray(inputs["x"]), dtype=np.float32)
    shared = _host_inputs(inputs)
    nb = x.shape[0]
    in_maps = []
    for b in range(nb):
        m = dict(shared)
        m["x"] = x[b]
        in_maps.append(m)
    nc = build()
    res = run_bass_kernel_spmd(nc, in_maps, core_ids=list(range(nb)))
    return np.stack([np.asarray(r["y"]) for r in res.results], axis=0).astype(np.float32)
```

```python
import contextlib
import numpy as np
import ml_dtypes
import concourse.bass as bass
import concourse.mybir as mybir
from concourse.bass_utils import run_bass_kernel_spmd

F32 = mybir.dt.float32
BF16 = mybir.dt.bfloat16
ALU = mybir.AluOpType
AF = mybir.ActivationFunctionType
AX = mybir.AxisListType

S = 4096
DM = 1024
NT = 32
NBLK = 8
EPS = 1e-6
NEG = -30000.0
P_EVEN = 4104
P_ODD = 3328
NPAR = 290
PC_LNG0, PC_LNG1 = 0, 8
PC_AQ, PC_AK, PC_SUB, PC_BQ, PC_BK, PC_CQ, PC_CK, PC_DQ, PC_DK, PC_FB = 16, 17, 18, 19, 20, 21, 22, 23, 24, 25
PC_SINK = 26
PC_LQ1, PC_LK1, PC_LQ2, PC_LK2 = 34, 98, 162, 226


class Buf:
    __slots__ = ("w", "r", "dsem", "dcnt", "name")

    def __init__(self, name=""):
        self.w = {}
        self.r = {}
        self.dsem = None
        self.dcnt = 0
        self.name = name


def _merge(d, s):
    for k, (sem, v) in s.items():
        if k not in d or d[k][1] < v:
            d[k] = (sem, v)


class KB:
    def __init__(self, nc):
        self.nc = nc
        self.stack = contextlib.ExitStack()
        self.eng = {}
        for name, e in (("pe", nc.tensor), ("act", nc.scalar), ("dve", nc.vector),
                        ("pool", nc.gpsimd), ("sp", nc.sync)):
            sem = self.stack.enter_context(nc.semaphore("s_" + name))
            self.eng[name] = dict(e=e, sem=sem, cnt=0, waited={}, name=name)
        self.dbufs = []
        self.nsem = 5

    def _deps(self, reads, writes):
        d = {}
        for b in reads:
            _merge(d, b.w)
        for b in writes:
            _merge(d, b.w)
            _merge(d, b.r)
        return d

    def _wait(self, E, deps, skip_key=None):
        for key, (sem, val) in deps.items():
            if key == skip_key:
                continue
            if E["name"] == "pe" and key == id(self.eng["pe"]["sem"]):
                continue
            if E["waited"].get(key, 0) < val:
                E["e"].wait_ge(sem, val)
                E["waited"][key] = val

    def op(self, en, fn, reads=(), writes=()):
        E = self.eng[en]
        self._wait(E, self._deps(reads, writes))
        ins = fn(E["e"])
        E["cnt"] += 1
        ins.then_inc(E["sem"], 1)
        key = id(E["sem"])
        tok = (E["sem"], E["cnt"])
        for b in reads:
            b.r[key] = tok
        for b in writes:
            b.w[key] = tok
            b.r = {}
        return tok

    def dma(self, qn, out, in_, reads=(), writes=(), par=False):
        E = self.eng[qn]
        b = writes[0]
        if b.dsem is None:
            b.dsem = self.stack.enter_context(self.nc.semaphore("d%d" % self.nsem))
            self.nsem += 1
            self.dbufs.append(b)
        key = id(b.dsem)
        self._wait(E, self._deps(reads, writes), skip_key=key if par else None)
        ins = E["e"].dma_start(out=out, in_=in_)
        b.dcnt += 16
        ins.then_inc(b.dsem, 16)
        tok = (b.dsem, b.dcnt)
        for rb in reads:
            rb.r[key] = tok
        for wb in writes:
            wb.w[key] = tok
            wb.r = {}
        return tok

    def barrier(self):
        deps = {}
        for F in self.eng.values():
            if F["cnt"] > 0:
                deps[id(F["sem"])] = (F["sem"], F["cnt"])
        for b in self.dbufs:
            deps[id(b.dsem)] = (b.dsem, b.dcnt)
        for E in self.eng.values():
            for key, (sem, val) in deps.items():
                if key == id(E["sem"]):
                    continue
                if E["waited"].get(key, 0) < val:
                    E["e"].wait_ge(sem, val)
                    E["waited"][key] = val


def mm(pe, out, lhsT, rhs, start, stop, skip=False):
    return pe.matmul(out, lhsT=lhsT, rhs=rhs, start=start, stop=stop, skip_group_check=skip)


def units_l0():
    us = []
    for h in range(4):
        us.append(dict(kind="A", qcol=h * 128, kcol=512 + h * 128, kw=128, vcol=1024 + h * 128, vw=128,
                       gcol=1536 + h * 128, mix0=h * 128, heads=(h, h), pq=PC_AQ, pk=PC_AK))
    for p in range(4):
        us.append(dict(kind="B", qcol=2048 + p * 128, kcol=2560 + p * 128, kw=128, vcol=3072 + p * 128, vw=128,
                       gcol=3584 + p * 128, mix0=512 + p * 128, heads=(2 * p, 2 * p + 1), pq=PC_BQ, pk=PC_BK))
    return us


def units_l1():
    us = []
    for j in range(2):
        for p in range(2):
            us.append(dict(kind="C", qcol=j * 256 + p * 128, kcol=512 + j * 64, kw=64, vcol=640 + j * 64, vw=64,
                           gcol=768 + j * 256 + p * 128, mix0=j * 256 + p * 128,
                           heads=(4 * j + 2 * p, 4 * j + 2 * p + 1), pq=PC_CQ, pk=PC_CK))
    for p in range(4):
        us.append(dict(kind="D", qcol=1280 + p * 128, kcol=1792 + p * 128, kw=128, vcol=2304 + p * 128, vw=128,
                       gcol=2816 + p * 128, mix0=512 + p * 128, heads=(2 * p, 2 * p + 1), pq=PC_DQ, pk=PC_DK))
    return us


def tiles_for(kind, qb):
    out = []
    if kind in ("A", "B"):
        for kt in range(4 * qb + 4):
            if kt < 4 * qb:
                out.append((kt, 0, 512, None))
            else:
                j = kt - 4 * qb
                out.append((kt, 128 * j, 512, 0))
        return out
    span = 256 if kind == "C" else 640
    back = 1 if kind == "C" else 4
    kts = [kt for kt in range(4 * qb - back, 4 * qb + 4) if kt >= 0]
    for kt in kts:
        lo = max(128 * kt, 512 * qb)
        hi = min(128 * kt + span, 512 * qb + 512)
        if hi > lo:
            out.append((kt, lo - 512 * qb, hi - 512 * qb, lo - 128 * kt))
    return out


def build(debug=False, nlayers=2, max_units=None):
    nc = bass.Bass("TRN2", target_bir_lowering=False)
    kb = KB(nc)
    st = kb.stack

    def dram(name, shape, dt, kind="ExternalInput"):
        return nc.dram_tensor(name, shape, dt, kind=kind).ap()

    x_d = dram("x", [S, DM], F32)
    win_d = [dram("w_in0", [DM, P_EVEN], F32), dram("w_in1", [DM, P_ODD], F32)]
    wout_d = [dram("w_out0", [DM, DM], F32), dram("w_out1", [DM, DM], F32)]
    par_d = dram("params", [128, NPAR], F32)
    ident_d = dram("ident", [128, 128], BF16)
    bones_d = dram("bones", [128, 128], BF16)
    aones_d = dram("aones", [128, 128], BF16)
    augA_d = dram("augA", [4, 12, S], BF16)
    onesrow_d = dram("onesrow", [8, S], BF16)
    corrA_d = dram("corrA", [4, 128, 128], F32)
    corrB_d = dram("corrB", [128, 128], F32)
    corrC_d = dram("corrC", [8, 128, 256], F32)
    relD_d = dram("relD", [8, 128, 640], F32)
    maskD_d = dram("maskD", [128, 640], F32)
    y_d = dram("y", [S, DM], F32, kind="ExternalOutput")
    skind = "ExternalOutput" if debug else "Internal"
    mixT_d = dram("mixT_scr", [DM, S], BF16, kind=skind)
    x1_d = dram("x1_scr", [S, DM], F32, kind=skind)
    augB_d = dram("augB_scr", [8, 12, S], BF16, kind=skind)

    uniq = [0]

    def sb(stack, name, shape, dt):
        uniq[0] += 1
        return stack.enter_context(nc.sbuf_tensor("%s_%d" % (name, uniq[0]), shape, dt))

    xnT = sb(st, "xnT", [128, 8, S], BF16)
    xnT_b = [Buf("xnT%d" % i) for i in range(NBLK)]
    ident = sb(st, "ident_sb", [128, 128], BF16)
    bones = sb(st, "bones_sb", [128, 128], BF16)
    aones = sb(st, "aones_sb", [128, 128], BF16)
    par = sb(st, "par_sb", [128, NPAR], F32)
    der = sb(st, "der_sb", [128, 32], F32)
    esink = sb(st, "esink_sb", [128, 8], F32)
    cst_b = Buf("consts")
    par_b = Buf("par")
    der_b = Buf("der")
    ps = [st.enter_context(nc.psum_tensor("ps%d" % i, [128, 512], F32)) for i in range(8)]
    ps_b = [Buf("ps%d" % i) for i in range(8)]
    psrr = [0]

    def psum_next(allowed=range(8)):
        allowed = list(allowed)
        i = allowed[psrr[0] % len(allowed)]
        psrr[0] += 1
        return ps[i], ps_b[i]

    mixblk_b = [Buf("mixblk%d" % i) for i in range(NBLK)]
    x1_b = Buf("x1scr")
    y_b = Buf("y")
    augB_b = Buf("augB")

    kb.dma("sp", ident[:], ident_d[:, :], writes=[cst_b])
    kb.dma("sp", bones[:], bones_d[:, :], writes=[cst_b], par=True)
    kb.dma("sp", aones[:], aones_d[:, :], writes=[cst_b], par=True)
    kb.dma("sp", par[:], par_d[:, :], writes=[par_b])

    DQ = {PC_AQ: 0, PC_BQ: 1, PC_CQ: 2, PC_DQ: 3}
    for pc, dc_ in DQ.items():
        kb.op("dve", lambda e, pc=pc, dc_=dc_: e.tensor_scalar(out=der[:, dc_:dc_ + 1], in0=par[:, pc:pc + 1], scalar1=0.125,
                                                             scalar2=None, op0=ALU.mult), reads=[par_b], writes=[der_b])
    kb.op("dve", lambda e: e.tensor_scalar(out=der[:, 4:5], in0=par[:, PC_SUB:PC_SUB + 1], scalar1=0.8, scalar2=None,
                                           op0=ALU.mult), reads=[par_b], writes=[der_b])
    kb.op("dve", lambda e: e.tensor_scalar(out=der[:, 6:7], in0=par[:, PC_FB:PC_FB + 1], scalar1=-1.0, scalar2=None,
                                           op0=ALU.mult), reads=[par_b], writes=[der_b])
    with contextlib.ExitStack() as s0:
        lt = sb(s0, "lamtmp", [128, 128], F32)
        lt_b = Buf("lt")
        kb.op("dve", lambda e: e.tensor_tensor(out=lt[:, 0:64], in0=par[:, PC_LQ1:PC_LQ1 + 64], in1=par[:, PC_LK1:PC_LK1 + 64],
                                               op=ALU.mult), reads=[par_b], writes=[lt_b])
        kb.op("dve", lambda e: e.tensor_tensor(out=lt[:, 64:128], in0=par[:, PC_LQ2:PC_LQ2 + 64], in1=par[:, PC_LK2:PC_LK2 + 64],
                                               op=ALU.mult), reads=[par_b], writes=[lt_b])
        kb.op("dve", lambda e: e.reduce_sum(out=der[:, 8:9], in_=lt[:, 0:64], axis=AX.X), reads=[lt_b], writes=[der_b])
        kb.op("dve", lambda e: e.reduce_sum(out=der[:, 9:10], in_=lt[:, 64:128], axis=AX.X), reads=[lt_b], writes=[der_b])
        kb.op("act", lambda e: e.activation(out=der[:, 10:12], in_=der[:, 8:10], func=AF.Exp), reads=[der_b], writes=[der_b])
        kb.op("dve", lambda e: e.tensor_tensor(out=der[:, 12:13], in0=der[:, 11:12], in1=der[:, 10:11], op=ALU.subtract),
              reads=[der_b], writes=[der_b])
        kb.op("dve", lambda e: e.tensor_scalar(out=der[:, 5:6], in0=der[:, 12:13], scalar1=-0.2, scalar2=None, op0=ALU.add),
              reads=[der_b], writes=[der_b])
        kb.op("act", lambda e: e.activation(out=esink[:, :], in_=par[:, PC_SINK:PC_SINK + 8], func=AF.Exp), reads=[par_b],
              writes=[der_b])
        kb.barrier()

    def norm_s1(nb, xs, xs_b, tt):
        i = tt % 3
        sq, sq_b = nb["sq"][i], nb["sq_b"][i]
        sm, sm_b = nb["sm"][i], nb["sm_b"][i]
        kb.op("act", lambda e: e.activation(out=sq[:, :], in_=xs, func=AF.Square), reads=[xs_b], writes=[sq_b])
        kb.op("dve", lambda e: e.reduce_sum(out=sm[:, 0:1], in_=sq[:, :], axis=AX.X), reads=[sq_b], writes=[sm_b])

    def norm_s2(nb, xs, xs_b, tt):
        i = tt % 3
        sm, sm_b = nb["sm"][i], nb["sm_b"][i]
        xb, xb_b = nb["xb"][i], nb["xb_b"][i]
        kb.op("act", lambda e: e.activation(out=sm[:, 1:2], in_=sm[:, 0:1], func=AF.Ln, scale=1.0 / DM, bias=EPS),
              reads=[sm_b], writes=[sm_b])
        kb.op("act", lambda e: e.activation(out=sm[:, 2:3], in_=sm[:, 1:2], func=AF.Exp, scale=-0.5), reads=[sm_b], writes=[sm_b])
        kb.op("dve", lambda e: e.tensor_scalar(out=xb[:, :], in0=xs, scalar1=sm[:, 2:3], scalar2=None, op0=ALU.mult),
              reads=[xs_b, sm_b], writes=[xb_b])
        pt, pt_b = psum_next()
        ptv = pt[:, :].bitcast(BF16)

        def tr(pe):
            ins = None
            for dc in range(8):
                ins = pe.transpose(ptv[:, dc * 128:(dc + 1) * 128], xb[:, dc * 128:(dc + 1) * 128], ident[:, :])
            return ins
        kb.op("pe", tr, reads=[xb_b, cst_b], writes=[pt_b])
        tb = tt // 4
        kb.op("act", lambda e: e.copy(out=xnT[:, :, tt * 128:(tt + 1) * 128],
                                      in_=ptv.rearrange("p (c t) -> p c t", c=8)),
              reads=[pt_b], writes=[xnT_b[tb]])

    def norm_bufs(stack, pfx):
        nb = dict(sq=[], sq_b=[], sm=[], sm_b=[], xb=[], xb_b=[])
        for i in range(3):
            nb["sq"].append(sb(stack, pfx + "sq%d" % i, [128, DM], F32))
            nb["sq_b"].append(Buf())
            nb["sm"].append(sb(stack, pfx + "sm%d" % i, [128, 4], F32))
            nb["sm_b"].append(Buf())
            nb["xb"].append(sb(stack, pfx + "xb%d" % i, [128, DM], BF16))
            nb["xb_b"].append(Buf())
        return nb

    with contextlib.ExitStack() as s1:
        nb = norm_bufs(s1, "n0")
        xst = [sb(s1, "xst%d" % i, [128, DM], F32) for i in range(3)]
        xst_b = [Buf() for _ in range(3)]
        for tt in range(NT):
            i = tt % 3
            kb.dma("sp", xst[i][:, :], x_d[tt * 128:(tt + 1) * 128, :], writes=[xst_b[i]])
            norm_s1(nb, xst[i][:, :], xst_b[i], tt)
            if tt >= 1:
                j = (tt - 1) % 3
                norm_s2(nb, xst[j][:, :], xst_b[j], tt - 1)
        j = (NT - 1) % 3
        norm_s2(nb, xst[j][:, :], xst_b[j], NT - 1)
        kb.barrier()

    def units_phase(layer):
        units = units_l0() if layer == 0 else units_l1()
        if max_units is not None:
            units = units[:max_units] if isinstance(max_units, int) else [units[i] for i in max_units]
        lng = PC_LNG0 if layer == 0 else PC_LNG1
        win = win_d[layer].rearrange("(c p) n -> p c n", p=128)
        KK = 70 if layer == 0 else 64
        with contextlib.ExitStack() as s2:
            qT = [sb(s2, "qT%d" % m, [128, S], BF16) for m in range(2)]
            kT = [sb(s2, "kT%d" % m, [128, S], BF16) for m in range(2)]
            qT_b = [Buf("qT%d" % m) for m in range(2)]
            kT_b = [Buf("kT%d" % m) for m in range(2)]
            Vaug = sb(s2, "Vaug", [128, NT, 2, 128], BF16)
            Vaug_b = Buf("Vaug")
            sg = sb(s2, "sg", [128, S], F32)
            sg_b = Buf("sg")
            wbf = sb(s2, "wbf", [128, 8, 512], BF16)
            wbf_b = Buf("wbf")
            wst = [sb(s2, "wst%d" % i, [128, 8, 128], F32) for i in range(2)]
            wst_b = [Buf() for _ in range(2)]
            NP_ = 6
            Pt = [sb(s2, "P%d" % i, [128, 512], BF16) for i in range(NP_)]
            Pt_b = [Buf() for _ in range(NP_)]
            ncorr = 640 if layer == 1 else 128
            corr = [sb(s2, "corr%d" % m, [128, ncorr], F32) for m in range(2)]
            corr_b = [Buf() for _ in range(2)]
            if layer == 1:
                maskD = sb(s2, "maskD", [128, 640], F32)
                maskD_b = Buf()
                kb.dma("sp", maskD[:, :], maskD_d[:, :], writes=[maskD_b])
            rl = [sb(s2, "rl%d" % i, [128, 512], F32) for i in range(2)]
            rl_b = [Buf() for _ in range(2)]
            on = [sb(s2, "on%d" % i, [128, 512], F32) for i in range(2)]
            on_b = [Buf() for _ in range(2)]
            lnt = [sb(s2, "lnt%d" % i, [128, 512], F32) for i in range(2)]
            lnt_b = [Buf() for _ in range(2)]
            sqt = [sb(s2, "sqt%d" % i, [128, 512], BF16) for i in range(2)]
            sqt_b = [Buf() for _ in range(2)]
            lnv = [sb(s2, "lnv%d" % i, [128, 512], F32) for i in range(2)]
            lnv_b = [Buf() for _ in range(2)]
            rstd = [sb(s2, "rstd%d" % i, [128, 512], F32) for i in range(2)]
            rstd_b = [Buf() for _ in range(2)]
            nrm_rr = [0]
            mx = [sb(s2, "mx%d" % i, [128, 512], BF16) for i in range(2)]
            mx_b = [Buf() for _ in range(2)]

            kb.op("pool", lambda e: e.memset(Vaug[:, :, 0, 64:128], 1.0), writes=[Vaug_b])
            kb.op("pool", lambda e: e.memset(Vaug[:, :, 1, 0:64], 1.0), writes=[Vaug_b])

            wst4 = [wst[0], wst[1], sb(s2, "wst2", [128, 8, 128], F32), sb(s2, "wst3", [128, 8, 128], F32)]
            wst4_b = [wst_b[0], wst_b[1], Buf(), Buf()]
            accS = [sb(s2, "accS%d" % i, [128, 512], F32) for i in range(4)]
            accS_b = [Buf() for _ in range(4)]
            Ecorr2 = sb(s2, "Ecorr2", [128, 2, ncorr], BF16)
            Ecorr_b = Buf()
            bg = []

            def drain(k):
                for _ in range(k):
                    if bg:
                        bg.pop(0)()

            def drain_all():
                while bg:
                    bg.pop(0)()

            def prep_weights(u, defer):
                groups = [(u["qcol"], 128), (u["kcol"], u["kw"]), (u["vcol"], u["vw"]), (u["gcol"], 128)]
                for gi, (col, wd) in enumerate(groups):
                    kb.dma("sp", wst4[gi][:, :, 0:wd], win[:, :, col:col + wd], writes=[wst4_b[gi]])
                    for dc in range(8):
                        def cast(gi=gi, dc=dc, wd=wd):
                            kb.op("dve", lambda e: e.tensor_scalar(
                                out=wbf[:, dc, gi * 128:gi * 128 + wd], in0=wst4[gi][:, dc, 0:wd],
                                scalar1=par[:, lng + dc:lng + dc + 1], scalar2=None, op0=ALU.mult),
                                reads=[wst4_b[gi], par_b], writes=[wbf_b])
                        if defer:
                            bg.append(cast)
                        else:
                            cast()

            def mm_group(pdst, M, c0, t0):
                def f(pe):
                    ins = None
                    for dc in range(8):
                        ins = mm(pe, pdst[0:M, :], wbf[:, dc, c0:c0 + M], xnT[:, dc, t0:t0 + 512], dc == 0, dc == 7)
                    return ins
                return f

            def emit_projection(u):
                dq = DQ[u["pq"]]
                kw, vw, pk = u["kw"], u["vw"], u["pk"]

                def stageA(tb):
                    p = tb % 2
                    t0 = tb * 512
                    xb_ = xnT_b[tb]
                    kb.op("pe", mm_group(ps[0 + 3 * p], 128, 0, t0), reads=[wbf_b, xb_], writes=[ps_b[0 + 3 * p]])
                    kb.op("pe", mm_group(ps[1 + 3 * p], kw, 128, t0), reads=[wbf_b, xb_], writes=[ps_b[1 + 3 * p]])
                    pv_, pv_b_ = ps[2 + 3 * p], ps_b[2 + 3 * p]

                    def vproj(pe):
                        ins = None
                        for j in range(4):
                            for dc in range(8):
                                ins = mm(pe, pv_[:, j * vw:(j + 1) * vw], xnT[:, dc, t0 + j * 128:t0 + (j + 1) * 128],
                                         wbf[:, dc, 256:256 + vw], dc == 0, dc == 7)
                        return ins
                    kb.op("pe", vproj, reads=[wbf_b, xb_], writes=[pv_b_])
                    pvv = pv_[:, 0:4 * vw].rearrange("p (j c) -> p j c", j=4)
                    hi = pvv[:, :, 64:128] if vw == 128 else pvv[:, :, 0:64]
                    kb.op("dve", lambda e: e.tensor_copy(out=Vaug[:, 4 * tb:4 * tb + 4, 0, 0:64], in_=pvv[:, :, 0:64]),
                          reads=[pv_b_], writes=[Vaug_b])
                    kb.op("dve", lambda e: e.tensor_copy(out=Vaug[:, 4 * tb:4 * tb + 4, 1, 64:128], in_=hi),
                          reads=[pv_b_], writes=[Vaug_b])

                def stageB(tb):
                    p = tb % 2
                    t0 = tb * 512
                    specs = [(ps[0 + 3 * p], ps_b[0 + 3 * p], 128, (lambda rows: der[rows, dq:dq + 1]), qT, qT_b, 6, 0),
                             (ps[1 + 3 * p], ps_b[1 + 3 * p], kw, (lambda rows: par[rows, pk:pk + 1]), kT, kT_b, 7, 1)]
                    for (pq_, pq_b_, R, gcolap, dst, dst_b, sbank, ri) in specs:
                        kb.op("act", lambda e, pq_=pq_, R=R, ri=ri: e.activation(out=sqt[ri][0:R, :], in_=pq_[0:R, :], func=AF.Square),
                              reads=[pq_b_], writes=[sqt_b[ri]])
                    for (pq_, pq_b_, R, gcolap, dst, dst_b, sbank, ri) in specs:
                        kb.op("pe", lambda pe, R=R, ri=ri, sbank=sbank: mm(pe, ps[sbank][0:R, :], bones[0:R, 0:R], sqt[ri][0:R, :], True, True),
                              reads=[sqt_b[ri], cst_b], writes=[ps_b[sbank]])
                    for (pq_, pq_b_, R, gcolap, dst, dst_b, sbank, ri) in specs:
                        kb.op("act", lambda e, R=R, ri=ri, sbank=sbank: e.activation(out=lnv[ri][0:R, :], in_=ps[sbank][0:R, :], func=AF.Ln,
                                                                                       scale=1.0 / 64, bias=EPS),
                              reads=[ps_b[sbank]], writes=[lnv_b[ri]])
                    for (pq_, pq_b_, R, gcolap, dst, dst_b, sbank, ri) in specs:
                        kb.op("act", lambda e, R=R, ri=ri: e.activation(out=rstd[ri][0:R, :], in_=lnv[ri][0:R, :], func=AF.Exp, scale=-0.5),
                              reads=[lnv_b[ri]], writes=[rstd_b[ri]])
                    for (pq_, pq_b_, R, gcolap, dst, dst_b, sbank, ri) in specs:
                        for m in range(R // 64):
                            rows = slice(64 * m, 64 * m + 64)
                            kb.op("dve", lambda e, m=m, rows=rows, pq_=pq_, gcolap=gcolap, dst=dst, ri=ri: e.scalar_tensor_tensor(
                                out=dst[m][0:64, t0:t0 + 512], in0=pq_[rows, :], scalar=gcolap(rows), in1=rstd[ri][rows, :],
                                op0=ALU.mult, op1=ALU.mult), reads=[pq_b_, rstd_b[ri], par_b, der_b], writes=[dst_b[m]])

                stageA(0)
                for tb in range(NBLK):
                    if tb + 1 < NBLK:
                        stageA(tb + 1)
                    stageB(tb)
                for tb in range(NBLK):
                    t0 = tb * 512
                    bi = tb % 6
                    kb.op("pe", mm_group(ps[bi], 128, 384, t0), reads=[wbf_b, xnT_b[tb]], writes=[ps_b[bi]])
                    kb.op("act", lambda e, bi=bi, t0=t0: e.activation(out=sg[:, t0:t0 + 512], in_=ps[bi][:, :], func=AF.Silu),
                          reads=[ps_b[bi]], writes=[sg_b])

            if units:
                prep_weights(units[0], False)
            for ui, u in enumerate(units):
                kind = u["kind"]
                drain_all()
                if layer == 0:
                    for m in range(2):
                        if kind == "A":
                            srcq = augA_d[u["heads"][0], 0:6, :]
                            srck = augA_d[u["heads"][0], 6:12, :]
                            rd = []
                        else:
                            srcq = augB_d[u["heads"][m], 0:6, :]
                            srck = augB_d[u["heads"][m], 6:12, :]
                            rd = [augB_b]
                        kb.dma("sp", qT[m][64:70, :], srcq, reads=rd, writes=[qT_b[m]])
                        kb.dma("sp", kT[m][64:70, :], srck, reads=rd, writes=[kT_b[m]])
                    if kind == "A":
                        kb.dma("sp", corr[0][:, 0:128], corrA_d[u["heads"][0], :, :], writes=[corr_b[0]])
                    else:
                        kb.dma("sp", corr[0][:, 0:128], corrB_d[:, :], writes=[corr_b[0]])
                else:
                    for m in range(2):
                        h = u["heads"][m]
                        if kind == "C":
                            kb.dma("sp", corr[m][:, 0:256], corrC_d[h, :, :], writes=[corr_b[m]])
                            nE = 256
                        else:
                            kb.dma("sp", corr[m][:, 0:640], relD_d[h, :, :], writes=[corr_b[m]])
                            kb.op("dve", lambda e, m=m: e.tensor_tensor(out=corr[m][:, :], in0=corr[m][:, :], in1=maskD[:, :],
                                                                        op=ALU.add), reads=[maskD_b], writes=[corr_b[m]])
                            nE = 640
                        kb.op("act", lambda e, m=m, nE=nE: e.activation(out=Ecorr2[:, m, 0:nE], in_=corr[m][:, 0:nE], func=AF.Exp),
                              reads=[corr_b[m]], writes=[Ecorr_b])
                emit_projection(u)
                if ui + 1 < len(units):
                    prep_weights(units[ui + 1], True)
                if kind == "A":
                    maps = [dict(q=0, k=0, pv=[(0, 0), (1, 1)], corr=0), dict(q=1, k=1, pv=[(0, 2), (1, 3)], corr=0)]
                    nacc = 4
                elif kind == "C":
                    maps = [dict(q=0, k=0, pv=[(0, 0)], corr=0), dict(q=1, k=0, pv=[(1, 1)], corr=1)]
                    nacc = 2
                else:
                    maps = [dict(q=0, k=0, pv=[(0, 0)], corr=0), dict(q=1, k=1, pv=[(1, 1)], corr=0 if layer == 0 else 1)]
                    nacc = 2
                dbl = (nacc == 2)
                sbanks = list(range(4, 8))
                nS = len(sbanks)
                skew = 2 if layer == 0 else 3
                for qb in range(NBLK):
                    Q0 = qb * 512
                    abase = 2 * (qb % 2) if dbl else 0
                    items = []
                    for (kt, q0, q1, c0) in tiles_for(kind, qb):
                        if kind == "C":
                            items.append((kt, q0, q1, c0, [0, 1]))
                        else:
                            for mi in range(len(maps)):
                                items.append((kt, q0, q1, c0, [mi]))
                    started = set()

                    def emit_qk(w):
                        kt, q0, q1, c0, mlist = items[w]
                        wd = q1 - q0
                        bi = sbanks[w % nS]
                        Sx, Sb = ps[bi], ps_b[bi]
                        pi = w % NP_
                        packed = (kind == "C")
                        for idx, mi in enumerate(mlist):
                            mp = maps[mi]
                            pb = idx * wd if packed else q0
                            kb.op("pe", lambda pe, mp=mp, pb=pb: mm(pe, Sx[:, pb:pb + wd], kT[mp["k"]][0:KK, kt * 128:(kt + 1) * 128],
                                                                  qT[mp["q"]][0:KK, Q0 + q0:Q0 + q1], True, True),
                                  reads=[kT_b[mp["k"]], qT_b[mp["q"]]], writes=[Sb])
                        lo = 0 if packed else q0
                        hi = len(mlist) * wd if packed else q1
                        cm = maps[mlist[0]]["corr"]
                        if c0 is not None and layer == 0:
                            a0, a1 = q0, q0 + 128
                            cap = corr[cm][:, 0:128]
                            kb.op("dve", lambda e: e.tensor_tensor(out=Sx[:, a0:a1], in0=Sx[:, a0:a1], in1=cap, op=ALU.add),
                                  reads=[corr_b[cm]], writes=[Sb])
                        kb.op("act", lambda e: e.activation(out=Pt[pi][:, lo:hi], in_=Sx[:, lo:hi], func=AF.Exp),
                              reads=[Sb], writes=[Pt_b[pi]])
                        if c0 is not None and layer == 1:
                            if packed:
                                pv3 = Pt[pi][:, lo:hi].rearrange("p (m c) -> p m c", m=2)
                                eap = Ecorr2[:, :, c0:c0 + wd]
                                kb.op("dve", lambda e: e.tensor_tensor(out=pv3, in0=pv3, in1=eap, op=ALU.mult),
                                      reads=[Ecorr_b], writes=[Pt_b[pi]])
                            else:
                                eap = Ecorr2[:, cm, c0:c0 + wd]
                                kb.op("dve", lambda e: e.tensor_tensor(out=Pt[pi][:, q0:q1], in0=Pt[pi][:, q0:q1], in1=eap, op=ALU.mult),
                                      reads=[Ecorr_b], writes=[Pt_b[pi]])

                    def emit_pv(w):
                        kt, q0, q1, c0, mlist = items[w]
                        wd = q1 - q0
                        pi = w % NP_
                        packed = (kind == "C")
                        for idx, mi in enumerate(mlist):
                            mp = maps[mi]
                            pb = idx * wd if packed else q0
                            for (slot, ai) in mp["pv"]:
                                first = ai not in started
                                started.add(ai)
                                kb.op("pe", lambda pe, slot=slot, ai=ai, first=first, pb=pb: mm(
                                    pe, ps[abase + ai][:, q0:q1], Vaug[:, kt, slot, :], Pt[pi][:, pb:pb + wd], first, True, skip=True),
                                    reads=[Pt_b[pi], Vaug_b], writes=[ps_b[abase + ai]])

                    n = len(items)
                    for w in range(n + skew):
                        if w < n:
                            emit_qk(w)
                        if w - skew >= 0:
                            emit_pv(w - skew)
                        drain(1)
                    drain_all()

                    mxi = qb % 2
                    steps = []
                    if kind == "A":
                        for a in range(4):
                            en = "act" if a % 2 == 0 else "dve"
                            if en == "act":
                                kb.op("act", lambda e, a=a: e.copy(out=accS[a][:, :], in_=ps[a][:, :]), reads=[ps_b[a]], writes=[accS_b[a]])
                            else:
                                kb.op("dve", lambda e, a=a: e.tensor_copy(out=accS[a][:, :], in_=ps[a][:, :]), reads=[ps_b[a]],
                                      writes=[accS_b[a]])
                        for a in range(4):
                            c, half = a // 2, a % 2
                            orow = slice(0, 64) if half == 0 else slice(64, 128)
                            lrow = slice(64, 128) if half == 0 else slice(0, 64)
                            steps.append(lambda a=a, c=c, orow=orow, lrow=lrow: kb.op("act", lambda e: e.activation(
                                out=lnt[c][orow, :], in_=accS[a][lrow, :], func=AF.Ln), reads=[accS_b[a]], writes=[lnt_b[c]]))
                        for c in range(2):
                            steps.append(lambda c=c: kb.op("act", lambda e: e.activation(out=rl[c][:, :], in_=lnt[c][:, :], func=AF.Exp,
                                                                                      scale=-1.0), reads=[lnt_b[c]], writes=[rl_b[c]]))
                        for a in range(4):
                            c, half = a // 2, a % 2
                            orow = slice(0, 64) if half == 0 else slice(64, 128)
                            steps.append(lambda a=a, c=c, orow=orow: kb.op("dve", lambda e: e.tensor_tensor(
                                out=on[c][orow, :], in0=accS[a][orow, :], in1=rl[c][orow, :], op=ALU.mult),
                                reads=[accS_b[a], rl_b[c]], writes=[on_b[c]]))
                        steps.append(lambda: kb.op("dve", lambda e: e.scalar_tensor_tensor(
                            out=lnt[0][:, :], in0=on[1][:, :], scalar=der[:, 5:6], in1=on[0][:, :], op0=ALU.mult, op1=ALU.add),
                            reads=[on_b[0], on_b[1], der_b], writes=[lnt_b[0]]))
                        steps.append(lambda: kb.op("act", lambda e: e.activation(out=sqt[0][:, :], in_=lnt[0][:, :], func=AF.Square),
                                                   reads=[lnt_b[0]], writes=[sqt_b[0]]))
                        pbank = sbanks[(n + 1) % nS]
                        def ss_step(pbank=pbank):
                            kb.op("pe", lambda pe: mm(pe, ps[pbank][:, :], aones[:, :], sqt[0][:, :], True, True),
                                  reads=[sqt_b[0], cst_b], writes=[ps_b[pbank]])
                            kb.op("act", lambda e: e.activation(out=lnv[0][:, :], in_=ps[pbank][:, :], func=AF.Ln,
                                                                scale=1.0 / 128, bias=EPS),
                                  reads=[ps_b[pbank]], writes=[lnv_b[0]])
                        steps.append(ss_step)
                        steps.append(lambda: kb.op("act", lambda e: e.activation(out=rstd[0][:, :], in_=lnv[0][:, :], func=AF.Exp, scale=-0.5),
                                                   reads=[lnv_b[0]], writes=[rstd_b[0]]))
                        steps.append(lambda: kb.op("dve", lambda e: e.scalar_tensor_tensor(
                            out=on[0][:, :], in0=lnt[0][:, :], scalar=der[:, 4:5], in1=rstd[0][:, :], op0=ALU.mult, op1=ALU.mult),
                            reads=[lnt_b[0], rstd_b[0], der_b], writes=[on_b[0]]))
                        steps.append(lambda mxi=mxi, Q0=Q0: kb.op("pool", lambda e: e.tensor_tensor(
                            out=mx[mxi][:, :], in0=on[0][:, :], in1=sg[:, Q0:Q0 + 512], op=ALU.mult),
                            reads=[on_b[0], sg_b], writes=[mx_b[mxi]]))
                    else:
                        fi = qb % 2
                        for m in range(2):
                            orow = slice(0, 64) if m == 0 else slice(64, 128)
                            lrow = slice(64, 128) if m == 0 else slice(0, 64)
                            am = abase + m
                            if kind == "C":
                                h = u["heads"][m]
                                steps.append(lambda am=am, orow=orow, lrow=lrow, h=h, fi=fi: kb.op("act", lambda e: e.activation(
                                    out=lnt[fi][orow, :], in_=ps[am][lrow, :], func=AF.Ln, scale=1.0, bias=esink[lrow, h:h + 1]),
                                    reads=[ps_b[am], der_b], writes=[lnt_b[fi]]))
                            else:
                                steps.append(lambda am=am, orow=orow, lrow=lrow, fi=fi: kb.op("act", lambda e: e.activation(
                                    out=lnt[fi][orow, :], in_=ps[am][lrow, :], func=AF.Ln), reads=[ps_b[am]], writes=[lnt_b[fi]]))
                        steps.append(lambda fi=fi: kb.op("act", lambda e: e.activation(out=rl[fi][:, :], in_=lnt[fi][:, :], func=AF.Exp,
                                                                                        scale=-1.0), reads=[lnt_b[fi]], writes=[rl_b[fi]]))
                        steps.append(lambda fi=fi, Q0=Q0: kb.op("pool", lambda e: e.tensor_tensor(
                            out=on[fi][:, :], in0=rl[fi][:, :], in1=sg[:, Q0:Q0 + 512], op=ALU.mult),
                            reads=[rl_b[fi], sg_b], writes=[on_b[fi]]))
                        for m in range(2):
                            orow = slice(0, 64) if m == 0 else slice(64, 128)
                            am = abase + m
                            steps.append(lambda am=am, orow=orow, fi=fi, mxi=mxi: kb.op("dve", lambda e: e.tensor_tensor(
                                out=mx[mxi][orow, :], in0=ps[am][orow, :], in1=on[fi][orow, :], op=ALU.mult),
                                reads=[ps_b[am], on_b[fi]], writes=[mx_b[mxi]]))
                    steps.append(lambda mxi=mxi, Q0=Q0, mix0=u["mix0"], qb=qb: kb.dma(
                        "pool", mixT_d[mix0:mix0 + 128, Q0:Q0 + 512], mx[mxi][:, :], reads=[mx_b[mxi]], writes=[mixblk_b[qb]]))
                    bg.extend(steps)
            drain_all()
            kb.barrier()

    def bprep():
        win = win_d[0].rearrange("(c p) n -> p c n", p=128)
        with contextlib.ExitStack() as s3:
            wst = [sb(s3, "wstb", [128, 8, 8], F32)]
            wst_b = [Buf()]
            wfb = sb(s3, "wfb", [128, 8, 8], BF16)
            wfb_b = Buf()
            l1p = sb(s3, "l1p", [8, S], F32)
            l1p_b = Buf()
            ones8 = sb(s3, "ones8", [8, S], F32)
            ones8_b = Buf()
            cum = sb(s3, "cum", [8, S], F32)
            cum_b = Buf()
            parts = [sb(s3, "part%d" % i, [8, S], BF16) for i in range(3)]
            nparts = [sb(s3, "npart%d" % i, [8, S], BF16) for i in range(3)]
            parts_b = Buf()
            res = sb(s3, "resid", [8, S], F32)
            res_b = Buf()
            kb.dma("sp", wst[0][:, :, 0:8], win[:, :, 4096:4104], writes=[wst_b[0]])
            for dc in range(8):
                kb.op("pool", lambda e, dc=dc: e.tensor_scalar(out=wfb[:, dc, :], in0=wst[0][:, dc, 0:8],
                                                               scalar1=par[:, PC_LNG0 + dc:PC_LNG0 + dc + 1], scalar2=None, op0=ALU.mult),
                      reads=[wst_b[0], par_b], writes=[wfb_b])
            kb.op("pool", lambda e: e.memset(ones8[:, :], 1.0), writes=[ones8_b])
            ones8h = sb(s3, "ones8h", [8, S], BF16)
            kb.op("pool", lambda e: e.memset(ones8h[:, :], 1.0), writes=[ones8_b])
            for tb in range(NBLK):
                t0 = tb * 512
                pf, pf_b = psum_next()

                def f(pe, t0=t0):
                    ins = None
                    for dc in range(8):
                        ins = mm(pe, pf[0:8, :], wfb[:, dc, :], xnT[:, dc, t0:t0 + 512], dc == 0, dc == 7)
                    return ins
                kb.op("pe", f, reads=[wfb_b, xnT_b[tb]], writes=[pf_b])
                kb.op("act", lambda e, t0=t0: e.activation(out=l1p[:, t0:t0 + 512], in_=pf[0:8, :], func=AF.Exp, scale=-1.0,
                                                          bias=der[0:8, 6:7]), reads=[pf_b, der_b], writes=[l1p_b])
                kb.op("act", lambda e, t0=t0: e.activation(out=l1p[:, t0:t0 + 512], in_=l1p[:, t0:t0 + 512], func=AF.Ln, scale=1.0,
                                                          bias=1.0), reads=[], writes=[l1p_b])
            kb.op("dve", lambda e: e.tensor_tensor_scan(out=cum[:, :], data0=ones8[:, :], data1=l1p[:, :], initial=0.0,
                                                        op0=ALU.mult, op1=ALU.add), reads=[ones8_b, l1p_b], writes=[cum_b])
            kb.op("dve", lambda e: e.tensor_copy(out=parts[0][:, :], in_=cum[:, :]), reads=[cum_b], writes=[parts_b])
            kb.op("dve", lambda e: e.tensor_tensor(out=res[:, :], in0=cum[:, :], in1=parts[0][:, :], op=ALU.subtract),
                  reads=[cum_b], writes=[res_b, parts_b])
            kb.op("dve", lambda e: e.tensor_copy(out=parts[1][:, :], in_=res[:, :]), reads=[res_b], writes=[parts_b])
            kb.op("dve", lambda e: e.tensor_tensor(out=res[:, :], in0=res[:, :], in1=parts[1][:, :], op=ALU.subtract),
                  reads=[], writes=[res_b, parts_b])
            kb.op("dve", lambda e: e.tensor_copy(out=parts[2][:, :], in_=res[:, :]), reads=[res_b], writes=[parts_b])
            for i in range(3):
                kb.op("dve", lambda e, i=i: e.tensor_scalar(out=nparts[i][:, :], in0=parts[i][:, :], scalar1=-1.0, scalar2=None,
                                                            op0=ALU.mult), reads=[], writes=[parts_b])
            first = True
            for r in range(3):
                kb.dma("pool", augB_d[:, r, :], ones8h[:, :], reads=[ones8_b], writes=[augB_b], par=not first)
                first = False
                kb.dma("pool", augB_d[:, 9 + r, :], ones8h[:, :], reads=[ones8_b], writes=[augB_b], par=True)
                kb.dma("pool", augB_d[:, 3 + r, :], nparts[r][:, :], reads=[parts_b], writes=[augB_b], par=True)
                kb.dma("pool", augB_d[:, 6 + r, :], parts[r][:, :], reads=[parts_b], writes=[augB_b], par=True)
            kb.barrier()

    def outproj_phase(layer):
        res_src = x_d if layer == 0 else x1_d
        last = (layer == nlayers - 1)
        with contextlib.ExitStack() as s4:
            wo = sb(s4, "wo", [128, 8, DM], BF16)
            wo_b = Buf()
            wos = [sb(s4, "wos%d" % i, [128, DM], F32) for i in range(2)]
            wos_b = [Buf() for _ in range(2)]
            mt = [sb(s4, "mt%d" % i, [128, 8, 512], BF16) for i in range(2)]
            mt_b = [Buf() for _ in range(2)]
            xr = [sb(s4, "xr%d" % i, [128, DM], F32) for i in range(2)]
            xr_b = [Buf() for _ in range(2)]
            xo = [sb(s4, "xo%d" % i, [128, DM], F32) for i in range(3)]
            xo_b = [Buf() for _ in range(3)]
            nb = norm_bufs(s4, "n1") if not last else None
            wod = wout_d[layer].rearrange("(c p) n -> p c n", p=128)
            for mc in range(8):
                i = mc % 2
                kb.dma("sp", wos[i][:, :], wod[:, mc, :], writes=[wos_b[i]])
                kb.op("pool", lambda e, i=i, mc=mc: e.tensor_copy(out=wo[:, mc, :], in_=wos[i][:, :]), reads=[wos_b[i]], writes=[wo_b])
            mixv = mixT_d.rearrange("(c p) t -> p c t", p=128)
            for tb in range(NBLK):
                bi = tb % 2
                kb.dma("sp", mt[bi][:, :, :], mixv[:, :, tb * 512:(tb + 1) * 512], reads=[mixblk_b[tb]], writes=[mt_b[bi]])
                for j in range(4):
                    tt = 4 * tb + j
                    i = tt % 2
                    io = tt % 3
                    rd = [x1_b] if layer == 1 else []
                    kb.dma("sp", xr[i][:, :], res_src[tt * 128:(tt + 1) * 128, :], reads=rd, writes=[xr_b[i]])
                    for half in range(2):
                        po, po_b = psum_next()

                        def f(pe, po=po, half=half, j=j, bi=bi):
                            ins = None
                            for mc in range(8):
                                ins = mm(pe, po[:, :], mt[bi][:, mc, j * 128:(j + 1) * 128], wo[:, mc, half * 512:(half + 1) * 512],
                                         mc == 0, mc == 7)
                            return ins
                        kb.op("pe", f, reads=[mt_b[bi], wo_b], writes=[po_b])
                        kb.op("dve", lambda e, po=po, half=half, i=i, io=io: e.tensor_tensor(
                            out=xo[io][:, half * 512:(half + 1) * 512], in0=po[:, :], in1=xr[i][:, half * 512:(half + 1) * 512], op=ALU.add),
                            reads=[po_b, xr_b[i]], writes=[xo_b[io]])
                    if last:
                        kb.dma("pool", y_d[tt * 128:(tt + 1) * 128, :], xo[io][:, :], reads=[xo_b[io]], writes=[y_b])
                    else:
                        kb.dma("pool", x1_d[tt * 128:(tt + 1) * 128, :], xo[io][:, :], reads=[xo_b[io]], writes=[x1_b])
                        norm_s1(nb, xo[io][:, :], xo_b[io], tt)
                        if tt >= 1:
                            jo = (tt - 1) % 3
                            norm_s2(nb, xo[jo][:, :], xo_b[jo], tt - 1)
            if not last:
                jo = (NT - 1) % 3
                norm_s2(nb, xo[jo][:, :], xo_b[jo], NT - 1)
            kb.barrier()

    for layer in range(nlayers):
        if layer == 0:
            bprep()
        units_phase(layer)
        outproj_phase(layer)
    kb.barrier()
    return nc


def _consts():
    bf = ml_dtypes.bfloat16
    c = {}
    c["ident"] = np.eye(128, dtype=np.float32).astype(bf)
    bo = np.zeros((128, 128), np.float32)
    bo[0:64, 0:64] = 1.0
    bo[64:128, 64:128] = 1.0
    c["bones"] = bo.astype(bf)
    c["aones"] = np.ones((128, 128), np.float32).astype(bf)
    c["onesrow"] = np.ones((8, S), np.float32).astype(bf)
    t = np.arange(S)
    hi = (t // 256) * 256
    lo = t % 256
    augA = np.zeros((4, 12, S), np.float32)
    for h in range(4):
        s = 2.0 ** (-8.0 * (h + 1) / 4)
        augA[h, 0] = 1.0
        augA[h, 1] = 1.0
        augA[h, 3] = -s * hi
        augA[h, 4] = -s * lo
        augA[h, 6] = s * hi
        augA[h, 7] = s * lo
        augA[h, 9] = 1.0
        augA[h, 10] = 1.0
    c["augA"] = augA.astype(bf)
    k = np.arange(128)[:, None]
    i = np.arange(128)[None, :]
    corrA = np.zeros((4, 128, 128), np.float32)
    for h in range(4):
        s = 2.0 ** (-8.0 * (h + 1) / 4)
        m = np.where(k <= i, 0.0, np.where((k // 64) == (i // 64), -2.0 * s * (k - i), NEG))
        corrA[h] = m
    c["corrA"] = corrA
    c["corrB"] = np.where(k <= i, 0.0, NEG).astype(np.float32)
    i2 = np.arange(256)[None, :]
    corrC = np.zeros((8, 128, 256), np.float32)
    for h in range(8):
        s = 2.0 ** (-8.0 * (h + 1) / 8)
        cd = (i2 // 64) - (k // 64)
        corrC[h] = np.where((cd >= 0) & (cd <= 2), -s * np.abs(i2 - k), NEG)
    c["corrC"] = corrC
    i3 = np.arange(640)[None, :]
    cd = (i3 // 64) - (k // 64)
    c["maskD"] = np.where((cd >= 0) & (cd <= 8), 0.0, NEG).astype(np.float32)
    c["ridxD"] = np.clip(i3 - k, -63, 256) + 63
    return c


_C = None


def _host_inputs(inputs):
    global _C
    if _C is None:
        _C = _consts()
    c = _C
    f = lambda a: np.ascontiguousarray(np.asarray(a), dtype=np.float32)
    par = np.zeros((128, NPAR), np.float32)
    par[:, PC_LNG0:PC_LNG0 + 8] = f(inputs["even_ln_g"])[0].reshape(8, 128).T
    par[:, PC_LNG1:PC_LNG1 + 8] = f(inputs["odd_ln_g"])[0].reshape(8, 128).T
    for col, name in ((PC_AQ, "a_q_norm_g"), (PC_AK, "a_k_norm_g"), (PC_BQ, "b_q_norm_g"), (PC_BK, "b_k_norm_g"),
                      (PC_CQ, "c_q_norm_g"), (PC_CK, "c_k_norm_g"), (PC_DQ, "d_q_norm_g"), (PC_DK, "d_k_norm_g")):
        par[:, col] = np.tile(f(inputs[name])[0], 2)
    par[:, PC_SUB] = f(inputs["a_subln_g"])[0]
    par[0:8, PC_FB] = f(inputs["b_forget_bias"])[0]
    par[:, PC_SINK:PC_SINK + 8] = np.broadcast_to(f(inputs["c_sinks"])[0], (128, 8))
    for col, name in ((PC_LQ1, "a_lambda_q1"), (PC_LK1, "a_lambda_k1"), (PC_LQ2, "a_lambda_q2"), (PC_LK2, "a_lambda_k2")):
        par[:, col:col + 64] = np.broadcast_to(f(inputs[name])[0], (128, 64))
    relD = np.ascontiguousarray(f(inputs["d_rel_bias"])[0][:, c["ridxD"]])
    shared = {
        "w_in0": f(inputs["even_w_in"])[0], "w_in1": f(inputs["odd_w_in"])[0],
        "w_out0": f(inputs["even_w_out"])[0], "w_out1": f(inputs["odd_w_out"])[0],
        "params": par, "ident": c["ident"], "bones": c["bones"], "aones": c["aones"], "augA": c["augA"],
        "onesrow": c["onesrow"], "corrA": c["corrA"], "corrB": c["corrB"], "corrC": c["corrC"], "relD": relD,
        "maskD": c["maskD"],
    }
    return shared


def kernel(**inputs):
    x = np.ascontiguousarray(np.asarray(inputs["x"]), dtype=np.float32)
    shared = _host_inputs(inputs)
    nb = x.shape[0]
    in_maps = []
    for b in range(nb):
        m = dict(shared)
        m["x"] = x[b]
        in_maps.append(m)
    nc = build()
    res = run_bass_kernel_spmd(nc, in_maps, core_ids=list(range(nb)))
    return np.stack([np.asarray(r["y"]) for r in res.results], axis=0).astype(np.float32)
```

```python
import contextlib
import numpy as np
import ml_dtypes
import concourse.bass as bass
import concourse.mybir as mybir
from concourse.bass_utils import run_bass_kernel_spmd

F32 = mybir.dt.float32
BF16 = mybir.dt.bfloat16
ALU = mybir.AluOpType
AF = mybir.ActivationFunctionType
AX = mybir.AxisListType

S = 4096
DM = 1024
NT = 32
NBLK = 8
EPS = 1e-6
NEG = -30000.0
P_EVEN = 4104
P_ODD = 3328
NPAR = 290
PC_LNG0, PC_LNG1 = 0, 8
PC_AQ, PC_AK, PC_SUB, PC_BQ, PC_BK, PC_CQ, PC_CK, PC_DQ, PC_DK, PC_FB = 16, 17, 18, 19, 20, 21, 22, 23, 24, 25
PC_SINK = 26
PC_LQ1, PC_LK1, PC_LQ2, PC_LK2 = 34, 98, 162, 226


class Buf:
    __slots__ = ("w", "r", "dsem", "dcnt", "name")

    def __init__(self, name=""):
        self.w = {}
        self.r = {}
        self.dsem = None
        self.dcnt = 0
        self.name = name


def _merge(d, s):
    for k, (sem, v) in s.items():
        if k not in d or d[k][1] < v:
            d[k] = (sem, v)


class KB:
    def __init__(self, nc):
        self.nc = nc
        self.stack = contextlib.ExitStack()
        self.eng = {}
        for name, e in (("pe", nc.tensor), ("act", nc.scalar), ("dve", nc.vector),
                        ("pool", nc.gpsimd), ("sp", nc.sync)):
            sem = self.stack.enter_context(nc.semaphore("s_" + name))
            self.eng[name] = dict(e=e, sem=sem, cnt=0, waited={}, name=name)
        self.dbufs = []
        self.nsem = 5

    def _deps(self, reads, writes):
        d = {}
        for b in reads:
            _merge(d, b.w)
        for b in writes:
            _merge(d, b.w)
            _merge(d, b.r)
        return d

    def _wait(self, E, deps, skip_key=None):
        for key, (sem, val) in deps.items():
            if key == skip_key:
                continue
            if E["name"] == "pe" and key == id(self.eng["pe"]["sem"]):
                continue
            if E["waited"].get(key, 0) < val:
                E["e"].wait_ge(sem, val)
                E["waited"][key] = val

    def op(self, en, fn, reads=(), writes=()):
        E = self.eng[en]
        self._wait(E, self._deps(reads, writes))
        ins = fn(E["e"])
        E["cnt"] += 1
        ins.then_inc(E["sem"], 1)
        key = id(E["sem"])
        tok = (E["sem"], E["cnt"])
        for b in reads:
            b.r[key] = tok
        for b in writes:
            b.w[key] = tok
            b.r = {}
        return tok

    def dma(self, qn, out, in_, reads=(), writes=(), par=False):
        E = self.eng[qn]
        b = writes[0]
        if b.dsem is None:
            b.dsem = self.stack.enter_context(self.nc.semaphore("d%d" % self.nsem))
            self.nsem += 1
            self.dbufs.append(b)
        key = id(b.dsem)
        self._wait(E, self._deps(reads, writes), skip_key=key if par else None)
        ins = E["e"].dma_start(out=out, in_=in_)
        b.dcnt += 16
        ins.then_inc(b.dsem, 16)
        tok = (b.dsem, b.dcnt)
        for rb in reads:
            rb.r[key] = tok
        for wb in writes:
            wb.w[key] = tok
            wb.r = {}
        return tok

    def barrier(self):
        deps = {}
        for F in self.eng.values():
            if F["cnt"] > 0:
                deps[id(F["sem"])] = (F["sem"], F["cnt"])
        for b in self.dbufs:
            deps[id(b.dsem)] = (b.dsem, b.dcnt)
        for E in self.eng.values():
            for key, (sem, val) in deps.items():
                if key == id(E["sem"]):
                    continue
                if E["waited"].get(key, 0) < val:
                    E["e"].wait_ge(sem, val)
                    E["waited"][key] = val


def mm(pe, out, lhsT, rhs, start, stop, skip=False):
    return pe.matmul(out, lhsT=lhsT, rhs=rhs, start=start, stop=stop, skip_group_check=skip)


def units_l0():
    us = []
    for h in range(4):
        us.append(dict(kind="A", qcol=h * 128, kcol=512 + h * 128, kw=128, vcol=1024 + h * 128, vw=128,
                       gcol=1536 + h * 128, mix0=h * 128, heads=(h, h), pq=PC_AQ, pk=PC_AK))
    for p in range(4):
        us.append(dict(kind="B", qcol=2048 + p * 128, kcol=2560 + p * 128, kw=128, vcol=3072 + p * 128, vw=128,
                       gcol=3584 + p * 128, mix0=512 + p * 128, heads=(2 * p, 2 * p + 1), pq=PC_BQ, pk=PC_BK))
    return us


def units_l1():
    us = []
    for j in range(2):
        for p in range(2):
            us.append(dict(kind="C", qcol=j * 256 + p * 128, kcol=512 + j * 64, kw=64, vcol=640 + j * 64, vw=64,
                           gcol=768 + j * 256 + p * 128, mix0=j * 256 + p * 128,
                           heads=(4 * j + 2 * p, 4 * j + 2 * p + 1), pq=PC_CQ, pk=PC_CK))
    for p in range(4):
        us.append(dict(kind="D", qcol=1280 + p * 128, kcol=1792 + p * 128, kw=128, vcol=2304 + p * 128, vw=128,
                       gcol=2816 + p * 128, mix0=512 + p * 128, heads=(2 * p, 2 * p + 1), pq=PC_DQ, pk=PC_DK))
    return us


def tiles_for(kind, qb):
    out = []
    if kind in ("A", "B"):
        for kt in range(4 * qb + 4):
            if kt < 4 * qb:
                out.append((kt, 0, 512, None))
            else:
                j = kt - 4 * qb
                out.append((kt, 128 * j, 512, 0))
        return out
    span = 256 if kind == "C" else 640
    back = 1 if kind == "C" else 4
    kts = [kt for kt in range(4 * qb - back, 4 * qb + 4) if kt >= 0]
    for kt in kts:
        lo = max(128 * kt, 512 * qb)
        hi = min(128 * kt + span, 512 * qb + 512)
        if hi > lo:
            out.append((kt, lo - 512 * qb, hi - 512 * qb, lo - 128 * kt))
    return out


def build(debug=False, nlayers=2, max_units=None):
    nc = bass.Bass("TRN2", target_bir_lowering=False)
    kb = KB(nc)
    st = kb.stack

    def dram(name, shape, dt, kind="ExternalInput"):
        return nc.dram_tensor(name, shape, dt, kind=kind).ap()

    x_d = dram("x", [S, DM], F32)
    win_d = [dram("w_in0", [DM, P_EVEN], F32), dram("w_in1", [DM, P_ODD], F32)]
    wout_d = [dram("w_out0", [DM, DM], F32), dram("w_out1", [DM, DM], F32)]
    par_d = dram("params", [128, NPAR], F32)
    ident_d = dram("ident", [128, 128], BF16)
    bones_d = dram("bones", [128, 128], BF16)
    aones_d = dram("aones", [128, 128], BF16)
    augA_d = dram("augA", [4, 12, S], BF16)
    onesrow_d = dram("onesrow", [8, S], BF16)
    corrA_d = dram("corrA", [4, 128, 128], F32)
    corrB_d = dram("corrB", [128, 128], F32)
    corrC_d = dram("corrC", [8, 128, 256], F32)
    relD_d = dram("relD", [8, 128, 640], F32)
    maskD_d = dram("maskD", [128, 640], F32)
    y_d = dram("y", [S, DM], F32, kind="ExternalOutput")
    skind = "ExternalOutput" if debug else "Internal"
    mixT_d = dram("mixT_scr", [DM, S], BF16, kind=skind)
    x1_d = dram("x1_scr", [S, DM], F32, kind=skind)
    augB_d = dram("augB_scr", [8, 12, S], BF16, kind=skind)

    uniq = [0]

    def sb(stack, name, shape, dt):
        uniq[0] += 1
        return stack.enter_context(nc.sbuf_tensor("%s_%d" % (name, uniq[0]), shape, dt))

    xnT = sb(st, "xnT", [128, 8, S], BF16)
    xnT_b = [Buf("xnT%d" % i) for i in range(NBLK)]
    ident = sb(st, "ident_sb", [128, 128], BF16)
    bones = sb(st, "bones_sb", [128, 128], BF16)
    aones = sb(st, "aones_sb", [128, 128], BF16)
    par = sb(st, "par_sb", [128, NPAR], F32)
    der = sb(st, "der_sb", [128, 32], F32)
    esink = sb(st, "esink_sb", [128, 8], F32)
    cst_b = Buf("consts")
    par_b = Buf("par")
    der_b = Buf("der")
    ps = [st.enter_context(nc.psum_tensor("ps%d" % i, [128, 512], F32)) for i in range(8)]
    ps_b = [Buf("ps%d" % i) for i in range(8)]
    psrr = [0]

    def psum_next(allowed=range(8)):
        allowed = list(allowed)
        i = allowed[psrr[0] % len(allowed)]
        psrr[0] += 1
        return ps[i], ps_b[i]

    mixblk_b = [Buf("mixblk%d" % i) for i in range(NBLK)]
    x1_b = Buf("x1scr")
    y_b = Buf("y")
    augB_b = Buf("augB")

    kb.dma("sp", ident[:], ident_d[:, :], writes=[cst_b])
    kb.dma("sp", bones[:], bones_d[:, :], writes=[cst_b], par=True)
    kb.dma("sp", aones[:], aones_d[:, :], writes=[cst_b], par=True)
    kb.dma("sp", par[:], par_d[:, :], writes=[par_b])

    DQ = {PC_AQ: 0, PC_BQ: 1, PC_CQ: 2, PC_DQ: 3}
    for pc, dc_ in DQ.items():
        kb.op("dve", lambda e, pc=pc, dc_=dc_: e.tensor_scalar(out=der[:, dc_:dc_ + 1], in0=par[:, pc:pc + 1], scalar1=0.125,
                                                             scalar2=None, op0=ALU.mult), reads=[par_b], writes=[der_b])
    kb.op("dve", lambda e: e.tensor_scalar(out=der[:, 4:5], in0=par[:, PC_SUB:PC_SUB + 1], scalar1=0.8, scalar2=None,
                                           op0=ALU.mult), reads=[par_b], writes=[der_b])
    kb.op("dve", lambda e: e.tensor_scalar(out=der[:, 6:7], in0=par[:, PC_FB:PC_FB + 1], scalar1=-1.0, scalar2=None,
                                           op0=ALU.mult), reads=[par_b], writes=[der_b])
    with contextlib.ExitStack() as s0:
        lt = sb(s0, "lamtmp", [128, 128], F32)
        lt_b = Buf("lt")
        kb.op("dve", lambda e: e.tensor_tensor(out=lt[:, 0:64], in0=par[:, PC_LQ1:PC_LQ1 + 64], in1=par[:, PC_LK1:PC_LK1 + 64],
                                               op=ALU.mult), reads=[par_b], writes=[lt_b])
        kb.op("dve", lambda e: e.tensor_tensor(out=lt[:, 64:128], in0=par[:, PC_LQ2:PC_LQ2 + 64], in1=par[:, PC_LK2:PC_LK2 + 64],
                                               op=ALU.mult), reads=[par_b], writes=[lt_b])
        kb.op("dve", lambda e: e.reduce_sum(out=der[:, 8:9], in_=lt[:, 0:64], axis=AX.X), reads=[lt_b], writes=[der_b])
        kb.op("dve", lambda e: e.reduce_sum(out=der[:, 9:10], in_=lt[:, 64:128], axis=AX.X), reads=[lt_b], writes=[der_b])
        kb.op("act", lambda e: e.activation(out=der[:, 10:12], in_=der[:, 8:10], func=AF.Exp), reads=[der_b], writes=[der_b])
        kb.op("dve", lambda e: e.tensor_tensor(out=der[:, 12:13], in0=der[:, 11:12], in1=der[:, 10:11], op=ALU.subtract),
              reads=[der_b], writes=[der_b])
        kb.op("dve", lambda e: e.tensor_scalar(out=der[:, 5:6], in0=der[:, 12:13], scalar1=-0.2, scalar2=None, op0=ALU.add),
              reads=[der_b], writes=[der_b])
        kb.op("act", lambda e: e.activation(out=esink[:, :], in_=par[:, PC_SINK:PC_SINK + 8], func=AF.Exp), reads=[par_b],
              writes=[der_b])
        kb.barrier()

    def norm_s1(nb, xs, xs_b, tt):
        i = tt % 3
        sq, sq_b = nb["sq"][i], nb["sq_b"][i]
        sm, sm_b = nb["sm"][i], nb["sm_b"][i]
        kb.op("act", lambda e: e.activation(out=sq[:, :], in_=xs, func=AF.Square), reads=[xs_b], writes=[sq_b])
        kb.op("dve", lambda e: e.reduce_sum(out=sm[:, 0:1], in_=sq[:, :], axis=AX.X), reads=[sq_b], writes=[sm_b])

    def norm_s2(nb, xs, xs_b, tt):
        i = tt % 3
        sm, sm_b = nb["sm"][i], nb["sm_b"][i]
        xb, xb_b = nb["xb"][i], nb["xb_b"][i]
        kb.op("act", lambda e: e.activation(out=sm[:, 1:2], in_=sm[:, 0:1], func=AF.Ln, scale=1.0 / DM, bias=EPS),
              reads=[sm_b], writes=[sm_b])
        kb.op("act", lambda e: e.activation(out=sm[:, 2:3], in_=sm[:, 1:2], func=AF.Exp, scale=-0.5), reads=[sm_b], writes=[sm_b])
        kb.op("dve", lambda e: e.tensor_scalar(out=xb[:, :], in0=xs, scalar1=sm[:, 2:3], scalar2=None, op0=ALU.mult),
              reads=[xs_b, sm_b], writes=[xb_b])
        pt, pt_b = psum_next()
        ptv = pt[:, :].bitcast(BF16)

        def tr(pe):
            ins = None
            for dc in range(8):
                ins = pe.transpose(ptv[:, dc * 128:(dc + 1) * 128], xb[:, dc * 128:(dc + 1) * 128], ident[:, :])
            return ins
        kb.op("pe", tr, reads=[xb_b, cst_b], writes=[pt_b])
        nb["pt"][i] = (ptv, pt_b)

    def norm_s3(nb, tt):
        ptv, pt_b = nb["pt"][tt % 3]
        tb = tt // 4
        kb.op("act", lambda e: e.copy(out=xnT[:, :, tt * 128:(tt + 1) * 128],
                                      in_=ptv.rearrange("p (c t) -> p c t", c=8)),
              reads=[pt_b], writes=[xnT_b[tb]])

    def norm_bufs(stack, pfx):
        nb = dict(sq=[], sq_b=[], sm=[], sm_b=[], xb=[], xb_b=[], pt=[None, None, None])
        for i in range(3):
            nb["sq"].append(sb(stack, pfx + "sq%d" % i, [128, DM], F32))
            nb["sq_b"].append(Buf())
            nb["sm"].append(sb(stack, pfx + "sm%d" % i, [128, 4], F32))
            nb["sm_b"].append(Buf())
            nb["xb"].append(sb(stack, pfx + "xb%d" % i, [128, DM], BF16))
            nb["xb_b"].append(Buf())
        return nb

    with contextlib.ExitStack() as s1:
        nb = norm_bufs(s1, "n0")
        xst = [sb(s1, "xst%d" % i, [128, DM], F32) for i in range(3)]
        xst_b = [Buf() for _ in range(3)]
        for tt in range(NT):
            i = tt % 3
            kb.dma("sp", xst[i][:, :], x_d[tt * 128:(tt + 1) * 128, :], writes=[xst_b[i]])
            norm_s1(nb, xst[i][:, :], xst_b[i], tt)
            if tt >= 1:
                j = (tt - 1) % 3
                norm_s2(nb, xst[j][:, :], xst_b[j], tt - 1)
            if tt >= 2:
                norm_s3(nb, tt - 2)
        j = (NT - 1) % 3
        norm_s2(nb, xst[j][:, :], xst_b[j], NT - 1)
        norm_s3(nb, NT - 2)
        norm_s3(nb, NT - 1)
        kb.barrier()

    def units_phase(layer):
        units = units_l0() if layer == 0 else units_l1()
        if max_units is not None:
            units = units[:max_units] if isinstance(max_units, int) else [units[i] for i in max_units]
        lng = PC_LNG0 if layer == 0 else PC_LNG1
        win = win_d[layer].rearrange("(c p) n -> p c n", p=128)
        KK = 70 if layer == 0 else 64
        with contextlib.ExitStack() as s2:
            qT = [sb(s2, "qT%d" % m, [128, S], BF16) for m in range(2)]
            kT = [sb(s2, "kT%d" % m, [128, S], BF16) for m in range(2)]
            qT_b = [Buf("qT%d" % m) for m in range(2)]
            kT_b = [Buf("kT%d" % m) for m in range(2)]
            Vaug = sb(s2, "Vaug", [128, NT, 2, 128], BF16)
            Vaug_b = Buf("Vaug")
            sg = sb(s2, "sg", [128, S], F32)
            sg_b = Buf("sg")
            wbf = sb(s2, "wbf", [128, 8, 512], BF16)
            wbf_b = Buf("wbf")
            wst = [sb(s2, "wst%d" % i, [128, 8, 128], F32) for i in range(2)]
            wst_b = [Buf() for _ in range(2)]
            NP_ = 6
            Pt = [sb(s2, "P%d" % i, [128, 512], BF16) for i in range(NP_)]
            Pt_b = [Buf() for _ in range(NP_)]
            ncorr = 640 if layer == 1 else 128
            corr = [sb(s2, "corr%d" % m, [128, ncorr], F32) for m in range(2)]
            corr_b = [Buf() for _ in range(2)]
            if layer == 1:
                maskD = sb(s2, "maskD", [128, 640], F32)
                maskD_b = Buf()
                kb.dma("sp", maskD[:, :], maskD_d[:, :], writes=[maskD_b])
            rl = [sb(s2, "rl%d" % i, [128, 512], F32) for i in range(2)]
            rl_b = [Buf() for _ in range(2)]
            on = [sb(s2, "on%d" % i, [128, 512], F32) for i in range(2)]
            on_b = [Buf() for _ in range(2)]
            lnt = [sb(s2, "lnt%d" % i, [128, 512], F32) for i in range(2)]
            lnt_b = [Buf() for _ in range(2)]
            sqt = [sb(s2, "sqt%d" % i, [128, 512], BF16) for i in range(2)]
            sqt_b = [Buf() for _ in range(2)]
            lnv = [sb(s2, "lnv%d" % i, [128, 512], F32) for i in range(2)]
            lnv_b = [Buf() for _ in range(2)]
            rstd = [sb(s2, "rstd%d" % i, [128, 512], F32) for i in range(2)]
            rstd_b = [Buf() for _ in range(2)]
            nrm_rr = [0]
            mx = [sb(s2, "mx%d" % i, [128, 512], BF16) for i in range(2)]
            mx_b = [Buf() for _ in range(2)]

            kb.op("pool", lambda e: e.memset(Vaug[:, :, 0, 64:128], 1.0), writes=[Vaug_b])
            kb.op("pool", lambda e: e.memset(Vaug[:, :, 1, 0:64], 1.0), writes=[Vaug_b])

            wst4 = [wst[0], wst[1], sb(s2, "wst2", [128, 8, 128], F32), sb(s2, "wst3", [128, 8, 128], F32)]
            wst4_b = [wst_b[0], wst_b[1], Buf(), Buf()]
            accS = [sb(s2, "accS%d" % i, [128, 512], F32) for i in range(4)]
            accS_b = [Buf() for _ in range(4)]
            Ecorr2 = sb(s2, "Ecorr2", [128, 2, ncorr], BF16)
            Ecorr_b = Buf()
            bg = []

            def drain(k):
                for _ in range(k):
                    if bg:
                        bg.pop(0)()

            def drain_all():
                while bg:
                    bg.pop(0)()

            def prep_weights(u, defer):
                groups = [(u["qcol"], 128), (u["kcol"], u["kw"]), (u["vcol"], u["vw"]), (u["gcol"], 128)]
                for gi, (col, wd) in enumerate(groups):
                    kb.dma("sp", wst4[gi][:, :, 0:wd], win[:, :, col:col + wd], writes=[wst4_b[gi]])
                    for dc in range(8):
                        def cast(gi=gi, dc=dc, wd=wd):
                            kb.op("dve", lambda e: e.tensor_scalar(
                                out=wbf[:, dc, gi * 128:gi * 128 + wd], in0=wst4[gi][:, dc, 0:wd],
                                scalar1=par[:, lng + dc:lng + dc + 1], scalar2=None, op0=ALU.mult),
                                reads=[wst4_b[gi], par_b], writes=[wbf_b])
                        if defer:
                            bg.append(cast)
                        else:
                            cast()

            def mm_group(pdst, M, c0, t0):
                def f(pe):
                    ins = None
                    for dc in range(8):
                        ins = mm(pe, pdst[0:M, :], wbf[:, dc, c0:c0 + M], xnT[:, dc, t0:t0 + 512], dc == 0, dc == 7)
                    return ins
                return f

            def emit_projection(u):
                dq = DQ[u["pq"]]
                kw, vw, pk = u["kw"], u["vw"], u["pk"]

                def stageA(tb):
                    p = tb % 2
                    t0 = tb * 512
                    xb_ = xnT_b[tb]
                    kb.op("pe", mm_group(ps[0 + 3 * p], 128, 0, t0), reads=[wbf_b, xb_], writes=[ps_b[0 + 3 * p]])
                    kb.op("pe", mm_group(ps[1 + 3 * p], kw, 128, t0), reads=[wbf_b, xb_], writes=[ps_b[1 + 3 * p]])
                    pv_, pv_b_ = ps[2 + 3 * p], ps_b[2 + 3 * p]

                    def vproj(pe):
                        ins = None
                        for j in range(4):
                            for dc in range(8):
                                ins = mm(pe, pv_[:, j * vw:(j + 1) * vw], xnT[:, dc, t0 + j * 128:t0 + (j + 1) * 128],
                                         wbf[:, dc, 256:256 + vw], dc == 0, dc == 7)
                        return ins
                    kb.op("pe", vproj, reads=[wbf_b, xb_], writes=[pv_b_])
                    pvv = pv_[:, 0:4 * vw].rearrange("p (j c) -> p j c", j=4)
                    hi = pvv[:, :, 64:128] if vw == 128 else pvv[:, :, 0:64]
                    kb.op("dve", lambda e: e.tensor_copy(out=Vaug[:, 4 * tb:4 * tb + 4, 0, 0:64], in_=pvv[:, :, 0:64]),
                          reads=[pv_b_], writes=[Vaug_b])
                    kb.op("dve", lambda e: e.tensor_copy(out=Vaug[:, 4 * tb:4 * tb + 4, 1, 64:128], in_=hi),
                          reads=[pv_b_], writes=[Vaug_b])

                def stageB(tb):
                    p = tb % 2
                    t0 = tb * 512
                    specs = [(ps[0 + 3 * p], ps_b[0 + 3 * p], 128, (lambda rows: der[rows, dq:dq + 1]), qT, qT_b, 6, 0),
                             (ps[1 + 3 * p], ps_b[1 + 3 * p], kw, (lambda rows: par[rows, pk:pk + 1]), kT, kT_b, 7, 1)]
                    for (pq_, pq_b_, R, gcolap, dst, dst_b, sbank, ri) in specs:
                        kb.op("act", lambda e, pq_=pq_, R=R, ri=ri: e.activation(out=sqt[ri][0:R, :], in_=pq_[0:R, :], func=AF.Square),
                              reads=[pq_b_], writes=[sqt_b[ri]])
                    for (pq_, pq_b_, R, gcolap, dst, dst_b, sbank, ri) in specs:
                        kb.op("pe", lambda pe, R=R, ri=ri, sbank=sbank: mm(pe, ps[sbank][0:R, :], bones[0:R, 0:R], sqt[ri][0:R, :], True, True),
                              reads=[sqt_b[ri], cst_b], writes=[ps_b[sbank]])
                    for (pq_, pq_b_, R, gcolap, dst, dst_b, sbank, ri) in specs:
                        kb.op("act", lambda e, R=R, ri=ri, sbank=sbank: e.activation(out=lnv[ri][0:R, :], in_=ps[sbank][0:R, :], func=AF.Ln,
                                                                                       scale=1.0 / 64, bias=EPS),
                              reads=[ps_b[sbank]], writes=[lnv_b[ri]])
                    for (pq_, pq_b_, R, gcolap, dst, dst_b, sbank, ri) in specs:
                        kb.op("act", lambda e, R=R, ri=ri: e.activation(out=rstd[ri][0:R, :], in_=lnv[ri][0:R, :], func=AF.Exp, scale=-0.5),
                              reads=[lnv_b[ri]], writes=[rstd_b[ri]])
                    for (pq_, pq_b_, R, gcolap, dst, dst_b, sbank, ri) in specs:
                        for m in range(R // 64):
                            rows = slice(64 * m, 64 * m + 64)
                            kb.op("dve", lambda e, m=m, rows=rows, pq_=pq_, gcolap=gcolap, dst=dst, ri=ri: e.scalar_tensor_tensor(
                                out=dst[m][0:64, t0:t0 + 512], in0=pq_[rows, :], scalar=gcolap(rows), in1=rstd[ri][rows, :],
                                op0=ALU.mult, op1=ALU.mult), reads=[pq_b_, rstd_b[ri], par_b, der_b], writes=[dst_b[m]])

                stageA(0)
                for tb in range(NBLK):
                    if tb + 1 < NBLK:
                        stageA(tb + 1)
                    stageB(tb)
                for tb in range(NBLK):
                    t0 = tb * 512
                    bi = tb % 6
                    kb.op("pe", mm_group(ps[bi], 128, 384, t0), reads=[wbf_b, xnT_b[tb]], writes=[ps_b[bi]])
                    kb.op("act", lambda e, bi=bi, t0=t0: e.activation(out=sg[:, t0:t0 + 512], in_=ps[bi][:, :], func=AF.Silu),
                          reads=[ps_b[bi]], writes=[sg_b])

            if units:
                prep_weights(units[0], False)
            for ui, u in enumerate(units):
                kind = u["kind"]
                drain_all()
                if layer == 0:
                    for m in range(2):
                        if kind == "A":
                            srcq = augA_d[u["heads"][0], 0:6, :]
                            srck = augA_d[u["heads"][0], 6:12, :]
                            rd = []
                        else:
                            srcq = augB_d[u["heads"][m], 0:6, :]
                            srck = augB_d[u["heads"][m], 6:12, :]
                            rd = [augB_b]
                        kb.dma("sp", qT[m][64:70, :], srcq, reads=rd, writes=[qT_b[m]])
                        kb.dma("sp", kT[m][64:70, :], srck, reads=rd, writes=[kT_b[m]])
                    if kind == "A":
                        kb.dma("sp", corr[0][:, 0:128], corrA_d[u["heads"][0], :, :], writes=[corr_b[0]])
                    else:
                        kb.dma("sp", corr[0][:, 0:128], corrB_d[:, :], writes=[corr_b[0]])
                else:
                    for m in range(2):
                        h = u["heads"][m]
                        if kind == "C":
                            kb.dma("sp", corr[m][:, 0:256], corrC_d[h, :, :], writes=[corr_b[m]])
                            nE = 256
                        else:
                            kb.dma("sp", corr[m][:, 0:640], relD_d[h, :, :], writes=[corr_b[m]])
                            kb.op("dve", lambda e, m=m: e.tensor_tensor(out=corr[m][:, :], in0=corr[m][:, :], in1=maskD[:, :],
                                                                        op=ALU.add), reads=[maskD_b], writes=[corr_b[m]])
                            nE = 640
                        kb.op("act", lambda e, m=m, nE=nE: e.activation(out=Ecorr2[:, m, 0:nE], in_=corr[m][:, 0:nE], func=AF.Exp),
                              reads=[corr_b[m]], writes=[Ecorr_b])
                emit_projection(u)
                if ui + 1 < len(units):
                    prep_weights(units[ui + 1], True)
                if kind == "A":
                    maps = [dict(q=0, k=0, pv=[(0, 0), (1, 1)], corr=0), dict(q=1, k=1, pv=[(0, 2), (1, 3)], corr=0)]
                    nacc = 4
                elif kind == "C":
                    maps = [dict(q=0, k=0, pv=[(0, 0)], corr=0), dict(q=1, k=0, pv=[(1, 1)], corr=1)]
                    nacc = 2
                else:
                    maps = [dict(q=0, k=0, pv=[(0, 0)], corr=0), dict(q=1, k=1, pv=[(1, 1)], corr=0 if layer == 0 else 1)]
                    nacc = 2
                dbl = (nacc == 2)
                sbanks = list(range(4, 8))
                nS = len(sbanks)
                skew = 2 if layer == 0 else 3
                for qb in range(NBLK):
                    Q0 = qb * 512
                    abase = 2 * (qb % 2) if dbl else 0
                    items = []
                    for (kt, q0, q1, c0) in tiles_for(kind, qb):
                        if kind == "C":
                            items.append((kt, q0, q1, c0, [0, 1]))
                        else:
                            for mi in range(len(maps)):
                                items.append((kt, q0, q1, c0, [mi]))
                    started = set()

                    def emit_qk(w):
                        kt, q0, q1, c0, mlist = items[w]
                        wd = q1 - q0
                        bi = sbanks[w % nS]
                        Sx, Sb = ps[bi], ps_b[bi]
                        pi = w % NP_
                        packed = (kind == "C")
                        for idx, mi in enumerate(mlist):
                            mp = maps[mi]
                            pb = idx * wd if packed else q0
                            kb.op("pe", lambda pe, mp=mp, pb=pb: mm(pe, Sx[:, pb:pb + wd], kT[mp["k"]][0:KK, kt * 128:(kt + 1) * 128],
                                                                  qT[mp["q"]][0:KK, Q0 + q0:Q0 + q1], True, True),
                                  reads=[kT_b[mp["k"]], qT_b[mp["q"]]], writes=[Sb])
                        lo = 0 if packed else q0
                        hi = len(mlist) * wd if packed else q1
                        cm = maps[mlist[0]]["corr"]
                        if c0 is not None and layer == 0:
                            a0, a1 = q0, q0 + 128
                            cap = corr[cm][:, 0:128]
                            kb.op("dve", lambda e: e.tensor_tensor(out=Sx[:, a0:a1], in0=Sx[:, a0:a1], in1=cap, op=ALU.add),
                                  reads=[corr_b[cm]], writes=[Sb])
                        kb.op("act", lambda e: e.activation(out=Pt[pi][:, lo:hi], in_=Sx[:, lo:hi], func=AF.Exp),
                              reads=[Sb], writes=[Pt_b[pi]])
                        if c0 is not None and layer == 1:
                            if packed:
                                pv3 = Pt[pi][:, lo:hi].rearrange("p (m c) -> p m c", m=2)
                                eap = Ecorr2[:, :, c0:c0 + wd]
                                kb.op("dve", lambda e: e.tensor_tensor(out=pv3, in0=pv3, in1=eap, op=ALU.mult),
                                      reads=[Ecorr_b], writes=[Pt_b[pi]])
                            else:
                                eap = Ecorr2[:, cm, c0:c0 + wd]
                                kb.op("dve", lambda e: e.tensor_tensor(out=Pt[pi][:, q0:q1], in0=Pt[pi][:, q0:q1], in1=eap, op=ALU.mult),
                                      reads=[Ecorr_b], writes=[Pt_b[pi]])

                    def emit_pv(w):
                        kt, q0, q1, c0, mlist = items[w]
                        wd = q1 - q0
                        pi = w % NP_
                        packed = (kind == "C")
                        for idx, mi in enumerate(mlist):
                            mp = maps[mi]
                            pb = idx * wd if packed else q0
                            for (slot, ai) in mp["pv"]:
                                first = ai not in started
                                started.add(ai)
                                kb.op("pe", lambda pe, slot=slot, ai=ai, first=first, pb=pb: mm(
                                    pe, ps[abase + ai][:, q0:q1], Vaug[:, kt, slot, :], Pt[pi][:, pb:pb + wd], first, True, skip=True),
                                    reads=[Pt_b[pi], Vaug_b], writes=[ps_b[abase + ai]])

                    n = len(items)
                    for w in range(n + skew):
                        if w < n:
                            emit_qk(w)
                        if w - skew >= 0:
                            emit_pv(w - skew)
                        drain(1)
                    drain_all()

                    mxi = qb % 2
                    steps = []
                    if kind == "A":
                        for a in range(4):
                            en = "act" if a % 2 == 0 else "dve"
                            if en == "act":
                                kb.op("act", lambda e, a=a: e.copy(out=accS[a][:, :], in_=ps[a][:, :]), reads=[ps_b[a]], writes=[accS_b[a]])
                            else:
                                kb.op("dve", lambda e, a=a: e.tensor_copy(out=accS[a][:, :], in_=ps[a][:, :]), reads=[ps_b[a]],
                                      writes=[accS_b[a]])
                        for a in range(4):
                            c, half = a // 2, a % 2
                            orow = slice(0, 64) if half == 0 else slice(64, 128)
                            lrow = slice(64, 128) if half == 0 else slice(0, 64)
                            steps.append(lambda a=a, c=c, orow=orow, lrow=lrow: kb.op("act", lambda e: e.activation(
                                out=lnt[c][orow, :], in_=accS[a][lrow, :], func=AF.Ln), reads=[accS_b[a]], writes=[lnt_b[c]]))
                        for c in range(2):
                            steps.append(lambda c=c: kb.op("act", lambda e: e.activation(out=rl[c][:, :], in_=lnt[c][:, :], func=AF.Exp,
                                                                                      scale=-1.0), reads=[lnt_b[c]], writes=[rl_b[c]]))
                        for a in range(4):
                            c, half = a // 2, a % 2
                            orow = slice(0, 64) if half == 0 else slice(64, 128)
                            steps.append(lambda a=a, c=c, orow=orow: kb.op("dve", lambda e: e.tensor_tensor(
                                out=on[c][orow, :], in0=accS[a][orow, :], in1=rl[c][orow, :], op=ALU.mult),
                                reads=[accS_b[a], rl_b[c]], writes=[on_b[c]]))
                        steps.append(lambda: kb.op("dve", lambda e: e.scalar_tensor_tensor(
                            out=lnt[0][:, :], in0=on[1][:, :], scalar=der[:, 5:6], in1=on[0][:, :], op0=ALU.mult, op1=ALU.add),
                            reads=[on_b[0], on_b[1], der_b], writes=[lnt_b[0]]))
                        steps.append(lambda: kb.op("act", lambda e: e.activation(out=sqt[0][:, :], in_=lnt[0][:, :], func=AF.Square),
                                                   reads=[lnt_b[0]], writes=[sqt_b[0]]))
                        pbank = sbanks[(n + 1) % nS]
                        def ss_step(pbank=pbank):
                            kb.op("pe", lambda pe: mm(pe, ps[pbank][:, :], aones[:, :], sqt[0][:, :], True, True),
                                  reads=[sqt_b[0], cst_b], writes=[ps_b[pbank]])
                            kb.op("act", lambda e: e.activation(out=lnv[0][:, :], in_=ps[pbank][:, :], func=AF.Ln,
                                                                scale=1.0 / 128, bias=EPS),
                                  reads=[ps_b[pbank]], writes=[lnv_b[0]])
                        steps.append(ss_step)
                        steps.append(lambda: kb.op("act", lambda e: e.activation(out=rstd[0][:, :], in_=lnv[0][:, :], func=AF.Exp, scale=-0.5),
                                                   reads=[lnv_b[0]], writes=[rstd_b[0]]))
                        steps.append(lambda: kb.op("dve", lambda e: e.scalar_tensor_tensor(
                            out=on[0][:, :], in0=lnt[0][:, :], scalar=der[:, 4:5], in1=rstd[0][:, :], op0=ALU.mult, op1=ALU.mult),
                            reads=[lnt_b[0], rstd_b[0], der_b], writes=[on_b[0]]))
                        steps.append(lambda mxi=mxi, Q0=Q0: kb.op("pool", lambda e: e.tensor_tensor(
                            out=mx[mxi][:, :], in0=on[0][:, :], in1=sg[:, Q0:Q0 + 512], op=ALU.mult),
                            reads=[on_b[0], sg_b], writes=[mx_b[mxi]]))
                    else:
                        fi = qb % 2
                        for m in range(2):
                            orow = slice(0, 64) if m == 0 else slice(64, 128)
                            lrow = slice(64, 128) if m == 0 else slice(0, 64)
                            am = abase + m
                            if kind == "C":
                                h = u["heads"][m]
                                steps.append(lambda am=am, orow=orow, lrow=lrow, h=h, fi=fi: kb.op("act", lambda e: e.activation(
                                    out=lnt[fi][orow, :], in_=ps[am][lrow, :], func=AF.Ln, scale=1.0, bias=esink[lrow, h:h + 1]),
                                    reads=[ps_b[am], der_b], writes=[lnt_b[fi]]))
                            else:
                                steps.append(lambda am=am, orow=orow, lrow=lrow, fi=fi: kb.op("act", lambda e: e.activation(
                                    out=lnt[fi][orow, :], in_=ps[am][lrow, :], func=AF.Ln), reads=[ps_b[am]], writes=[lnt_b[fi]]))
                        steps.append(lambda fi=fi: kb.op("act", lambda e: e.activation(out=rl[fi][:, :], in_=lnt[fi][:, :], func=AF.Exp,
                                                                                        scale=-1.0), reads=[lnt_b[fi]], writes=[rl_b[fi]]))
                        steps.append(lambda fi=fi, Q0=Q0: kb.op("pool", lambda e: e.tensor_tensor(
                            out=on[fi][:, :], in0=rl[fi][:, :], in1=sg[:, Q0:Q0 + 512], op=ALU.mult),
                            reads=[rl_b[fi], sg_b], writes=[on_b[fi]]))
                        for m in range(2):
                            orow = slice(0, 64) if m == 0 else slice(64, 128)
                            am = abase + m
                            steps.append(lambda am=am, orow=orow, fi=fi, mxi=mxi: kb.op("dve", lambda e: e.tensor_tensor(
                                out=mx[mxi][orow, :], in0=ps[am][orow, :], in1=on[fi][orow, :], op=ALU.mult),
                                reads=[ps_b[am], on_b[fi]], writes=[mx_b[mxi]]))
                    steps.append(lambda mxi=mxi, Q0=Q0, mix0=u["mix0"], qb=qb: kb.dma(
                        "pool", mixT_d[mix0:mix0 + 128, Q0:Q0 + 512], mx[mxi][:, :], reads=[mx_b[mxi]], writes=[mixblk_b[qb]]))
                    bg.extend(steps)
            drain_all()
            kb.barrier()

    def bprep():
        win = win_d[0].rearrange("(c p) n -> p c n", p=128)
        with contextlib.ExitStack() as s3:
            wst = [sb(s3, "wstb", [128, 8, 8], F32)]
            wst_b = [Buf()]
            wfb = sb(s3, "wfb", [128, 8, 8], BF16)
            wfb_b = Buf()
            l1p = sb(s3, "l1p", [8, S], F32)
            l1p_b = Buf()
            ones8 = sb(s3, "ones8", [8, S], F32)
            ones8_b = Buf()
            cum = sb(s3, "cum", [8, S], F32)
            cum_b = Buf()
            parts = [sb(s3, "part%d" % i, [8, S], BF16) for i in range(3)]
            nparts = [sb(s3, "npart%d" % i, [8, S], BF16) for i in range(3)]
            parts_b = Buf()
            res = sb(s3, "resid", [8, S], F32)
            res_b = Buf()
            kb.dma("sp", wst[0][:, :, 0:8], win[:, :, 4096:4104], writes=[wst_b[0]])
            for dc in range(8):
                kb.op("pool", lambda e, dc=dc: e.tensor_scalar(out=wfb[:, dc, :], in0=wst[0][:, dc, 0:8],
                                                               scalar1=par[:, PC_LNG0 + dc:PC_LNG0 + dc + 1], scalar2=None, op0=ALU.mult),
                      reads=[wst_b[0], par_b], writes=[wfb_b])
            kb.op("pool", lambda e: e.memset(ones8[:, :], 1.0), writes=[ones8_b])
            ones8h = sb(s3, "ones8h", [8, S], BF16)
            kb.op("pool", lambda e: e.memset(ones8h[:, :], 1.0), writes=[ones8_b])
            for tb in range(NBLK):
                t0 = tb * 512
                pf, pf_b = psum_next()

                def f(pe, t0=t0):
                    ins = None
                    for dc in range(8):
                        ins = mm(pe, pf[0:8, :], wfb[:, dc, :], xnT[:, dc, t0:t0 + 512], dc == 0, dc == 7)
                    return ins
                kb.op("pe", f, reads=[wfb_b, xnT_b[tb]], writes=[pf_b])
                kb.op("act", lambda e, t0=t0: e.activation(out=l1p[:, t0:t0 + 512], in_=pf[0:8, :], func=AF.Exp, scale=-1.0,
                                                          bias=der[0:8, 6:7]), reads=[pf_b, der_b], writes=[l1p_b])
                kb.op("act", lambda e, t0=t0: e.activation(out=l1p[:, t0:t0 + 512], in_=l1p[:, t0:t0 + 512], func=AF.Ln, scale=1.0,
                                                          bias=1.0), reads=[], writes=[l1p_b])
            kb.op("dve", lambda e: e.tensor_tensor_scan(out=cum[:, :], data0=ones8[:, :], data1=l1p[:, :], initial=0.0,
                                                        op0=ALU.mult, op1=ALU.add), reads=[ones8_b, l1p_b], writes=[cum_b])
            kb.op("dve", lambda e: e.tensor_copy(out=parts[0][:, :], in_=cum[:, :]), reads=[cum_b], writes=[parts_b])
            kb.op("dve", lambda e: e.tensor_tensor(out=res[:, :], in0=cum[:, :], in1=parts[0][:, :], op=ALU.subtract),
                  reads=[cum_b], writes=[res_b, parts_b])
            kb.op("dve", lambda e: e.tensor_copy(out=parts[1][:, :], in_=res[:, :]), reads=[res_b], writes=[parts_b])
            kb.op("dve", lambda e: e.tensor_tensor(out=res[:, :], in0=res[:, :], in1=parts[1][:, :], op=ALU.subtract),
                  reads=[], writes=[res_b, parts_b])
            kb.op("dve", lambda e: e.tensor_copy(out=parts[2][:, :], in_=res[:, :]), reads=[res_b], writes=[parts_b])
            for i in range(3):
                kb.op("dve", lambda e, i=i: e.tensor_scalar(out=nparts[i][:, :], in0=parts[i][:, :], scalar1=-1.0, scalar2=None,
                                                            op0=ALU.mult), reads=[], writes=[parts_b])
            first = True
            for r in range(3):
                kb.dma("pool", augB_d[:, r, :], ones8h[:, :], reads=[ones8_b], writes=[augB_b], par=not first)
                first = False
                kb.dma("pool", augB_d[:, 9 + r, :], ones8h[:, :], reads=[ones8_b], writes=[augB_b], par=True)
                kb.dma("pool", augB_d[:, 3 + r, :], nparts[r][:, :], reads=[parts_b], writes=[augB_b], par=True)
                kb.dma("pool", augB_d[:, 6 + r, :], parts[r][:, :], reads=[parts_b], writes=[augB_b], par=True)
            kb.barrier()

    def outproj_phase(layer):
        res_src = x_d if layer == 0 else x1_d
        last = (layer == nlayers - 1)
        with contextlib.ExitStack() as s4:
            wo = sb(s4, "wo", [128, 8, DM], BF16)
            wo_b = Buf()
            wos = [sb(s4, "wos%d" % i, [128, DM], F32) for i in range(2)]
            wos_b = [Buf() for _ in range(2)]
            mt = [sb(s4, "mt%d" % i, [128, 8, 512], BF16) for i in range(2)]
            mt_b = [Buf() for _ in range(2)]
            xr = [sb(s4, "xr%d" % i, [128, DM], F32) for i in range(2)]
            xr_b = [Buf() for _ in range(2)]
            xo = [sb(s4, "xo%d" % i, [128, DM], F32) for i in range(3)]
            xo_b = [Buf() for _ in range(3)]
            nb = norm_bufs(s4, "n1") if not last else None
            wod = wout_d[layer].rearrange("(c p) n -> p c n", p=128)
            for mc in range(8):
                i = mc % 2
                kb.dma("sp", wos[i][:, :], wod[:, mc, :], writes=[wos_b[i]])
                kb.op("pool", lambda e, i=i, mc=mc: e.tensor_copy(out=wo[:, mc, :], in_=wos[i][:, :]), reads=[wos_b[i]], writes=[wo_b])
            mixv = mixT_d.rearrange("(c p) t -> p c t", p=128)
            for tb in range(NBLK):
                bi = tb % 2
                kb.dma("sp", mt[bi][:, :, :], mixv[:, :, tb * 512:(tb + 1) * 512], reads=[mixblk_b[tb]], writes=[mt_b[bi]])
                for j in range(4):
                    tt = 4 * tb + j
                    i = tt % 2
                    io = tt % 3
                    rd = [x1_b] if layer == 1 else []
                    kb.dma("sp", xr[i][:, :], res_src[tt * 128:(tt + 1) * 128, :], reads=rd, writes=[xr_b[i]])
                    for half in range(2):
                        po, po_b = psum_next()

                        def f(pe, po=po, half=half, j=j, bi=bi):
                            ins = None
                            for mc in range(8):
                                ins = mm(pe, po[:, :], mt[bi][:, mc, j * 128:(j + 1) * 128], wo[:, mc, half * 512:(half + 1) * 512],
                                         mc == 0, mc == 7)
                            return ins
                        kb.op("pe", f, reads=[mt_b[bi], wo_b], writes=[po_b])
                        kb.op("dve", lambda e, po=po, half=half, i=i, io=io: e.tensor_tensor(
                            out=xo[io][:, half * 512:(half + 1) * 512], in0=po[:, :], in1=xr[i][:, half * 512:(half + 1) * 512], op=ALU.add),
                            reads=[po_b, xr_b[i]], writes=[xo_b[io]])
                    if last:
                        kb.dma("pool", y_d[tt * 128:(tt + 1) * 128, :], xo[io][:, :], reads=[xo_b[io]], writes=[y_b])
                    else:
                        kb.dma("pool", x1_d[tt * 128:(tt + 1) * 128, :], xo[io][:, :], reads=[xo_b[io]], writes=[x1_b])
                        norm_s1(nb, xo[io][:, :], xo_b[io], tt)
                        if tt >= 1:
                            jo = (tt - 1) % 3
                            norm_s2(nb, xo[jo][:, :], xo_b[jo], tt - 1)
                        if tt >= 2:
                            norm_s3(nb, tt - 2)
            if not last:
                jo = (NT - 1) % 3
                norm_s2(nb, xo[jo][:, :], xo_b[jo], NT - 1)
                norm_s3(nb, NT - 2)
                norm_s3(nb, NT - 1)
            kb.barrier()

    for layer in range(nlayers):
        if layer == 0:
            bprep()
        units_phase(layer)
        outproj_phase(layer)
    kb.barrier()
    return nc


def _consts():
    bf = ml_dtypes.bfloat16
    c = {}
    c["ident"] = np.eye(128, dtype=np.float32).astype(bf)
    bo = np.zeros((128, 128), np.float32)
    bo[0:64, 0:64] = 1.0
    bo[64:128, 64:128] = 1.0
    c["bones"] = bo.astype(bf)
    c["aones"] = np.ones((128, 128), np.float32).astype(bf)
    c["onesrow"] = np.ones((8, S), np.float32).astype(bf)
    t = np.arange(S)
    hi = (t // 256) * 256
    lo = t % 256
    augA = np.zeros((4, 12, S), np.float32)
    for h in range(4):
        s = 2.0 ** (-8.0 * (h + 1) / 4)
        augA[h, 0] = 1.0
        augA[h, 1] = 1.0
        augA[h, 3] = -s * hi
        augA[h, 4] = -s * lo
        augA[h, 6] = s * hi
        augA[h, 7] = s * lo
        augA[h, 9] = 1.0
        augA[h, 10] = 1.0
    c["augA"] = augA.astype(bf)
    k = np.arange(128)[:, None]
    i = np.arange(128)[None, :]
    corrA = np.zeros((4, 128, 128), np.float32)
    for h in range(4):
        s = 2.0 ** (-8.0 * (h + 1) / 4)
        m = np.where(k <= i, 0.0, np.where((k // 64) == (i // 64), -2.0 * s * (k - i), NEG))
        corrA[h] = m
    c["corrA"] = corrA
    c["corrB"] = np.where(k <= i, 0.0, NEG).astype(np.float32)
    i2 = np.arange(256)[None, :]
    corrC = np.zeros((8, 128, 256), np.float32)
    for h in range(8):
        s = 2.0 ** (-8.0 * (h + 1) / 8)
        cd = (i2 // 64) - (k // 64)
        corrC[h] = np.where((cd >= 0) & (cd <= 2), -s * np.abs(i2 - k), NEG)
    c["corrC"] = corrC
    i3 = np.arange(640)[None, :]
    cd = (i3 // 64) - (k // 64)
    c["maskD"] = np.where((cd >= 0) & (cd <= 8), 0.0, NEG).astype(np.float32)
    c["ridxD"] = np.clip(i3 - k, -63, 256) + 63
    return c


_C = None


def _host_inputs(inputs):
    global _C
    if _C is None:
        _C = _consts()
    c = _C
    f = lambda a: np.ascontiguousarray(np.asarray(a), dtype=np.float32)
    par = np.zeros((128, NPAR), np.float32)
    par[:, PC_LNG0:PC_LNG0 + 8] = f(inputs["even_ln_g"])[0].reshape(8, 128).T
    par[:, PC_LNG1:PC_LNG1 + 8] = f(inputs["odd_ln_g"])[0].reshape(8, 128).T
    for col, name in ((PC_AQ, "a_q_norm_g"), (PC_AK, "a_k_norm_g"), (PC_BQ, "b_q_norm_g"), (PC_BK, "b_k_norm_g"),
                      (PC_CQ, "c_q_norm_g"), (PC_CK, "c_k_norm_g"), (PC_DQ, "d_q_norm_g"), (PC_DK, "d_k_norm_g")):
        par[:, col] = np.tile(f(inputs[name])[0], 2)
    par[:, PC_SUB] = f(inputs["a_subln_g"])[0]
    par[0:8, PC_FB] = f(inputs["b_forget_bias"])[0]
    par[:, PC_SINK:PC_SINK + 8] = np.broadcast_to(f(inputs["c_sinks"])[0], (128, 8))
    for col, name in ((PC_LQ1, "a_lambda_q1"), (PC_LK1, "a_lambda_k1"), (PC_LQ2, "a_lambda_q2"), (PC_LK2, "a_lambda_k2")):
        par[:, col:col + 64] = np.broadcast_to(f(inputs[name])[0], (128, 64))
    relD = np.ascontiguousarray(f(inputs["d_rel_bias"])[0][:, c["ridxD"]])
    shared = {
        "w_in0": f(inputs["even_w_in"])[0], "w_in1": f(inputs["odd_w_in"])[0],
        "w_out0": f(inputs["even_w_out"])[0], "w_out1": f(inputs["odd_w_out"])[0],
        "params": par, "ident": c["ident"], "bones": c["bones"], "aones": c["aones"], "augA": c["augA"],
        "onesrow": c["onesrow"], "corrA": c["corrA"], "corrB": c["corrB"], "corrC": c["corrC"], "relD": relD,
        "maskD": c["maskD"],
    }
    return shared


def kernel(**inputs):
    x = np.ascontiguousarray(np.asarray(inputs["x"]), dtype=np.float32)
    shared = _host_inputs(inputs)
    nb = x.shape[0]
    in_maps = []
    for b in range(nb):
        m = dict(shared)
        m["x"] = x[b]
        in_maps.append(m)
    nc = build()
    res = run_bass_kernel_spmd(nc, in_maps, core_ids=list(range(nb)))
    return np.stack([np.asarray(r["y"]) for r in res.results], axis=0).astype(np.float32)
```

```python
import contextlib
import numpy as np
import ml_dtypes
import concourse.bass as bass
import concourse.mybir as mybir
from concourse.bass_utils import run_bass_kernel_spmd

F32 = mybir.dt.float32
BF16 = mybir.dt.bfloat16
ALU = mybir.AluOpType
AF = mybir.ActivationFunctionType
AX = mybir.AxisListType

S = 4096
DM = 1024
NT = 32
NBLK = 8
EPS = 1e-6
NEG = -30000.0
P_EVEN = 4104
P_ODD = 3328
NPAR = 290
PC_LNG0, PC_LNG1 = 0, 8
PC_AQ, PC_AK, PC_SUB, PC_BQ, PC_BK, PC_CQ, PC_CK, PC_DQ, PC_DK, PC_FB = 16, 17, 18, 19, 20, 21, 22, 23, 24, 25
PC_SINK = 26
PC_LQ1, PC_LK1, PC_LQ2, PC_LK2 = 34, 98, 162, 226


class Buf:
    __slots__ = ("w", "r", "dsem", "dcnt", "name")

    def __init__(self, name=""):
        self.w = {}
        self.r = {}
        self.dsem = None
        self.dcnt = 0
        self.name = name


def _merge(d, s):
    for k, (sem, v) in s.items():
        if k not in d or d[k][1] < v:
            d[k] = (sem, v)


class KB:
    def __init__(self, nc):
        self.nc = nc
        self.stack = contextlib.ExitStack()
        self.eng = {}
        for name, e in (("pe", nc.tensor), ("act", nc.scalar), ("dve", nc.vector),
                        ("pool", nc.gpsimd), ("sp", nc.sync)):
            sem = self.stack.enter_context(nc.semaphore("s_" + name))
            self.eng[name] = dict(e=e, sem=sem, cnt=0, waited={}, name=name)
        self.dbufs = []
        self.nsem = 5

    def _deps(self, reads, writes):
        d = {}
        for b in reads:
            _merge(d, b.w)
        for b in writes:
            _merge(d, b.w)
            _merge(d, b.r)
        return d

    def _wait(self, E, deps, skip_key=None):
        for key, (sem, val) in deps.items():
            if key == skip_key:
                continue
            if E["name"] == "pe" and key == id(self.eng["pe"]["sem"]):
                continue
            if E["waited"].get(key, 0) < val:
                E["e"].wait_ge(sem, val)
                E["waited"][key] = val

    def op(self, en, fn, reads=(), writes=()):
        E = self.eng[en]
        self._wait(E, self._deps(reads, writes))
        ins = fn(E["e"])
        E["cnt"] += 1
        ins.then_inc(E["sem"], 1)
        key = id(E["sem"])
        tok = (E["sem"], E["cnt"])
        for b in reads:
            b.r[key] = tok
        for b in writes:
            b.w[key] = tok
            b.r = {}
        return tok

    def dma(self, qn, out, in_, reads=(), writes=(), par=False):
        E = self.eng[qn]
        b = writes[0]
        if b.dsem is None:
            b.dsem = self.stack.enter_context(self.nc.semaphore("d%d" % self.nsem))
            self.nsem += 1
            self.dbufs.append(b)
        key = id(b.dsem)
        self._wait(E, self._deps(reads, writes), skip_key=key if par else None)
        ins = E["e"].dma_start(out=out, in_=in_)
        b.dcnt += 16
        ins.then_inc(b.dsem, 16)
        tok = (b.dsem, b.dcnt)
        for rb in reads:
            rb.r[key] = tok
        for wb in writes:
            wb.w[key] = tok
            wb.r = {}
        return tok

    def barrier(self):
        deps = {}
        for F in self.eng.values():
            if F["cnt"] > 0:
                deps[id(F["sem"])] = (F["sem"], F["cnt"])
        for b in self.dbufs:
            deps[id(b.dsem)] = (b.dsem, b.dcnt)
        for E in self.eng.values():
            for key, (sem, val) in deps.items():
                if key == id(E["sem"]):
                    continue
                if E["waited"].get(key, 0) < val:
                    E["e"].wait_ge(sem, val)
                    E["waited"][key] = val


def mm(pe, out, lhsT, rhs, start, stop, skip=False):
    return pe.matmul(out, lhsT=lhsT, rhs=rhs, start=start, stop=stop, skip_group_check=skip)


def units_l0():
    us = []
    for h in range(4):
        us.append(dict(kind="A", qcol=h * 128, kcol=512 + h * 128, kw=128, vcol=1024 + h * 128, vw=128,
                       gcol=1536 + h * 128, mix0=h * 128, heads=(h, h), pq=PC_AQ, pk=PC_AK))
    for p in range(4):
        us.append(dict(kind="B", qcol=2048 + p * 128, kcol=2560 + p * 128, kw=128, vcol=3072 + p * 128, vw=128,
                       gcol=3584 + p * 128, mix0=512 + p * 128, heads=(2 * p, 2 * p + 1), pq=PC_BQ, pk=PC_BK))
    return us


def units_l1():
    us = []
    for j in range(2):
        for p in range(2):
            us.append(dict(kind="C", qcol=j * 256 + p * 128, kcol=512 + j * 64, kw=64, vcol=640 + j * 64, vw=64,
                           gcol=768 + j * 256 + p * 128, mix0=j * 256 + p * 128,
                           heads=(4 * j + 2 * p, 4 * j + 2 * p + 1), pq=PC_CQ, pk=PC_CK))
    for p in range(4):
        us.append(dict(kind="D", qcol=1280 + p * 128, kcol=1792 + p * 128, kw=128, vcol=2304 + p * 128, vw=128,
                       gcol=2816 + p * 128, mix0=512 + p * 128, heads=(2 * p, 2 * p + 1), pq=PC_DQ, pk=PC_DK))
    return us


def tiles_for(kind, qb):
    out = []
    if kind in ("A", "B"):
        for kt in range(4 * qb + 4):
            if kt < 4 * qb:
                out.append((kt, 0, 512, None))
            else:
                j = kt - 4 * qb
                out.append((kt, 128 * j, 512, 0))
        return out
    span = 256 if kind == "C" else 640
    back = 1 if kind == "C" else 4
    kts = [kt for kt in range(4 * qb - back, 4 * qb + 4) if kt >= 0]
    for kt in kts:
        lo = max(128 * kt, 512 * qb)
        hi = min(128 * kt + span, 512 * qb + 512)
        if hi > lo:
            out.append((kt, lo - 512 * qb, hi - 512 * qb, lo - 128 * kt))
    return out


def build(debug=False, nlayers=2, max_units=None):
    nc = bass.Bass("TRN2", target_bir_lowering=False)
    kb = KB(nc)
    st = kb.stack

    def dram(name, shape, dt, kind="ExternalInput"):
        return nc.dram_tensor(name, shape, dt, kind=kind).ap()

    x_d = dram("x", [S, DM], F32)
    win_d = [dram("w_in0", [DM, P_EVEN], F32), dram("w_in1", [DM, P_ODD], F32)]
    wout_d = [dram("w_out0", [DM, DM], F32), dram("w_out1", [DM, DM], F32)]
    par_d = dram("params", [128, NPAR], F32)
    ident_d = dram("ident", [128, 128], BF16)
    bones_d = dram("bones", [128, 128], BF16)
    aones_d = dram("aones", [128, 128], BF16)
    augA_d = dram("augA", [4, 12, S], BF16)
    onesrow_d = dram("onesrow", [8, S], BF16)
    corrA_d = dram("corrA", [4, 128, 128], F32)
    corrB_d = dram("corrB", [128, 128], F32)
    corrC_d = dram("corrC", [8, 128, 256], F32)
    relD_d = dram("relD", [8, 128, 640], F32)
    maskD_d = dram("maskD", [128, 640], F32)
    y_d = dram("y", [S, DM], F32, kind="ExternalOutput")
    skind = "ExternalOutput" if debug else "Internal"
    mixT_d = dram("mixT_scr", [DM, S], BF16, kind=skind)
    x1_d = dram("x1_scr", [S, DM], F32, kind=skind)
    augB_d = dram("augB_scr", [8, 12, S], BF16, kind=skind)

    uniq = [0]

    def sb(stack, name, shape, dt):
        uniq[0] += 1
        return stack.enter_context(nc.sbuf_tensor("%s_%d" % (name, uniq[0]), shape, dt))

    xnT = sb(st, "xnT", [128, 8, S], BF16)
    xnT_b = [Buf("xnT%d" % i) for i in range(NBLK)]
    ident = sb(st, "ident_sb", [128, 128], BF16)
    bones = sb(st, "bones_sb", [128, 128], BF16)
    aones = sb(st, "aones_sb", [128, 128], BF16)
    par = sb(st, "par_sb", [128, NPAR], F32)
    der = sb(st, "der_sb", [128, 32], F32)
    esink = sb(st, "esink_sb", [128, 8], F32)
    cst_b = Buf("consts")
    par_b = Buf("par")
    der_b = Buf("der")
    ps = [st.enter_context(nc.psum_tensor("ps%d" % i, [128, 512], F32)) for i in range(8)]
    ps_b = [Buf("ps%d" % i) for i in range(8)]
    psrr = [0]

    def psum_next(allowed=range(8)):
        allowed = list(allowed)
        i = allowed[psrr[0] % len(allowed)]
        psrr[0] += 1
        return ps[i], ps_b[i]

    mixblk_b = [Buf("mixblk%d" % i) for i in range(NBLK)]
    x1_b = Buf("x1scr")
    y_b = Buf("y")
    augB_b = Buf("augB")

    kb.dma("sp", ident[:], ident_d[:, :], writes=[cst_b])
    kb.dma("sp", bones[:], bones_d[:, :], writes=[cst_b], par=True)
    kb.dma("sp", aones[:], aones_d[:, :], writes=[cst_b], par=True)
    kb.dma("sp", par[:], par_d[:, :], writes=[par_b])

    DQ = {PC_AQ: 0, PC_BQ: 1, PC_CQ: 2, PC_DQ: 3}
    for pc, dc_ in DQ.items():
        kb.op("dve", lambda e, pc=pc, dc_=dc_: e.tensor_scalar(out=der[:, dc_:dc_ + 1], in0=par[:, pc:pc + 1], scalar1=0.125,
                                                             scalar2=None, op0=ALU.mult), reads=[par_b], writes=[der_b])
    kb.op("dve", lambda e: e.tensor_scalar(out=der[:, 4:5], in0=par[:, PC_SUB:PC_SUB + 1], scalar1=0.8, scalar2=None,
                                           op0=ALU.mult), reads=[par_b], writes=[der_b])
    kb.op("dve", lambda e: e.tensor_scalar(out=der[:, 6:7], in0=par[:, PC_FB:PC_FB + 1], scalar1=-1.0, scalar2=None,
                                           op0=ALU.mult), reads=[par_b], writes=[der_b])
    with contextlib.ExitStack() as s0:
        lt = sb(s0, "lamtmp", [128, 128], F32)
        lt_b = Buf("lt")
        kb.op("dve", lambda e: e.tensor_tensor(out=lt[:, 0:64], in0=par[:, PC_LQ1:PC_LQ1 + 64], in1=par[:, PC_LK1:PC_LK1 + 64],
                                               op=ALU.mult), reads=[par_b], writes=[lt_b])
        kb.op("dve", lambda e: e.tensor_tensor(out=lt[:, 64:128], in0=par[:, PC_LQ2:PC_LQ2 + 64], in1=par[:, PC_LK2:PC_LK2 + 64],
                                               op=ALU.mult), reads=[par_b], writes=[lt_b])
        kb.op("dve", lambda e: e.reduce_sum(out=der[:, 8:9], in_=lt[:, 0:64], axis=AX.X), reads=[lt_b], writes=[der_b])
        kb.op("dve", lambda e: e.reduce_sum(out=der[:, 9:10], in_=lt[:, 64:128], axis=AX.X), reads=[lt_b], writes=[der_b])
        kb.op("act", lambda e: e.activation(out=der[:, 10:12], in_=der[:, 8:10], func=AF.Exp), reads=[der_b], writes=[der_b])
        kb.op("dve", lambda e: e.tensor_tensor(out=der[:, 12:13], in0=der[:, 11:12], in1=der[:, 10:11], op=ALU.subtract),
              reads=[der_b], writes=[der_b])
        kb.op("dve", lambda e: e.tensor_scalar(out=der[:, 5:6], in0=der[:, 12:13], scalar1=-0.2, scalar2=None, op0=ALU.add),
              reads=[der_b], writes=[der_b])
        kb.op("act", lambda e: e.activation(out=esink[:, :], in_=par[:, PC_SINK:PC_SINK + 8], func=AF.Exp), reads=[par_b],
              writes=[der_b])
        kb.barrier()

    def norm_s1(nb, xs, xs_b, tt):
        i = tt % 3
        sq, sq_b = nb["sq"][i], nb["sq_b"][i]
        sm, sm_b = nb["sm"][i], nb["sm_b"][i]
        kb.op("act", lambda e: e.activation(out=sq[:, :], in_=xs, func=AF.Square), reads=[xs_b], writes=[sq_b])
        kb.op("dve", lambda e: e.reduce_sum(out=sm[:, 0:1], in_=sq[:, :], axis=AX.X), reads=[sq_b], writes=[sm_b])

    def norm_s2(nb, xs, xs_b, tt):
        i = tt % 3
        sm, sm_b = nb["sm"][i], nb["sm_b"][i]
        xb, xb_b = nb["xb"][i], nb["xb_b"][i]
        kb.op("act", lambda e: e.activation(out=sm[:, 1:2], in_=sm[:, 0:1], func=AF.Ln, scale=1.0 / DM, bias=EPS),
              reads=[sm_b], writes=[sm_b])
        kb.op("act", lambda e: e.activation(out=sm[:, 2:3], in_=sm[:, 1:2], func=AF.Exp, scale=-0.5), reads=[sm_b], writes=[sm_b])
        kb.op("dve", lambda e: e.tensor_scalar(out=xb[:, :], in0=xs, scalar1=sm[:, 2:3], scalar2=None, op0=ALU.mult),
              reads=[xs_b, sm_b], writes=[xb_b])
        pt, pt_b = psum_next()
        ptv = pt[:, :].bitcast(BF16)

        def tr(pe):
            ins = None
            for dc in range(8):
                ins = pe.transpose(ptv[:, dc * 128:(dc + 1) * 128], xb[:, dc * 128:(dc + 1) * 128], ident[:, :])
            return ins
        kb.op("pe", tr, reads=[xb_b, cst_b], writes=[pt_b])
        nb["pt"][i] = (ptv, pt_b)

    def norm_s3(nb, tt):
        ptv, pt_b = nb["pt"][tt % 3]
        tb = tt // 4
        kb.op("act", lambda e: e.copy(out=xnT[:, :, tt * 128:(tt + 1) * 128],
                                      in_=ptv.rearrange("p (c t) -> p c t", c=8)),
              reads=[pt_b], writes=[xnT_b[tb]])

    def norm_bufs(stack, pfx):
        nb = dict(sq=[], sq_b=[], sm=[], sm_b=[], xb=[], xb_b=[], pt=[None, None, None])
        for i in range(3):
            nb["sq"].append(sb(stack, pfx + "sq%d" % i, [128, DM], F32))
            nb["sq_b"].append(Buf())
            nb["sm"].append(sb(stack, pfx + "sm%d" % i, [128, 4], F32))
            nb["sm_b"].append(Buf())
            nb["xb"].append(sb(stack, pfx + "xb%d" % i, [128, DM], BF16))
            nb["xb_b"].append(Buf())
        return nb

    with contextlib.ExitStack() as s1:
        nb = norm_bufs(s1, "n0")
        xst = [sb(s1, "xst%d" % i, [128, DM], F32) for i in range(3)]
        xst_b = [Buf() for _ in range(3)]
        for tt in range(NT):
            i = tt % 3
            kb.dma("sp", xst[i][:, :], x_d[tt * 128:(tt + 1) * 128, :], writes=[xst_b[i]])
            norm_s1(nb, xst[i][:, :], xst_b[i], tt)
            if tt >= 1:
                j = (tt - 1) % 3
                norm_s2(nb, xst[j][:, :], xst_b[j], tt - 1)
            if tt >= 2:
                norm_s3(nb, tt - 2)
        j = (NT - 1) % 3
        norm_s2(nb, xst[j][:, :], xst_b[j], NT - 1)
        norm_s3(nb, NT - 2)
        norm_s3(nb, NT - 1)
        kb.barrier()

    def units_phase(layer):
        units = units_l0() if layer == 0 else units_l1()
        if max_units is not None:
            units = units[:max_units] if isinstance(max_units, int) else [units[i] for i in max_units]
        lng = PC_LNG0 if layer == 0 else PC_LNG1
        win = win_d[layer].rearrange("(c p) n -> p c n", p=128)
        KK = 70 if layer == 0 else 64
        with contextlib.ExitStack() as s2:
            qT = [sb(s2, "qT%d" % m, [128, S], BF16) for m in range(2)]
            kT = [sb(s2, "kT%d" % m, [128, S], BF16) for m in range(2)]
            qT_b = [Buf("qT%d" % m) for m in range(2)]
            kT_b = [Buf("kT%d" % m) for m in range(2)]
            Vaug = sb(s2, "Vaug", [128, NT, 2, 128], BF16)
            Vaug_b = Buf("Vaug")
            sg = sb(s2, "sg", [128, S], F32)
            sg_b = Buf("sg")
            wbf = sb(s2, "wbf", [128, 8, 512], BF16)
            wbf_b = Buf("wbf")
            wst = [sb(s2, "wst%d" % i, [128, 8, 128], F32) for i in range(2)]
            wst_b = [Buf() for _ in range(2)]
            NP_ = 8
            Pt = [sb(s2, "P%d" % i, [128, 512], BF16) for i in range(NP_)]
            Pt_b = [Buf() for _ in range(NP_)]
            ncorr = 640 if layer == 1 else 128
            corr = [sb(s2, "corr%d" % m, [128, ncorr], F32) for m in range(2)]
            corr_b = [Buf() for _ in range(2)]
            if layer == 1:
                maskD = sb(s2, "maskD", [128, 640], F32)
                maskD_b = Buf()
                kb.dma("sp", maskD[:, :], maskD_d[:, :], writes=[maskD_b])
            rl = [sb(s2, "rl%d" % i, [128, 512], F32) for i in range(2)]
            rl_b = [Buf() for _ in range(2)]
            on = [sb(s2, "on%d" % i, [128, 512], F32) for i in range(2)]
            on_b = [Buf() for _ in range(2)]
            lnt = [sb(s2, "lnt%d" % i, [128, 512], F32) for i in range(2)]
            lnt_b = [Buf() for _ in range(2)]
            sqt = [sb(s2, "sqt%d" % i, [128, 512], BF16) for i in range(2)]
            sqt_b = [Buf() for _ in range(2)]
            lnv = [sb(s2, "lnv%d" % i, [128, 512], F32) for i in range(2)]
            lnv_b = [Buf() for _ in range(2)]
            rstd = [sb(s2, "rstd%d" % i, [128, 512], F32) for i in range(2)]
            rstd_b = [Buf() for _ in range(2)]
            nrm_rr = [0]
            mx = [sb(s2, "mx%d" % i, [128, 512], BF16) for i in range(2)]
            mx_b = [Buf() for _ in range(2)]

            kb.op("pool", lambda e: e.memset(Vaug[:, :, 0, 64:128], 1.0), writes=[Vaug_b])
            kb.op("pool", lambda e: e.memset(Vaug[:, :, 1, 0:64], 1.0), writes=[Vaug_b])

            wst4 = [wst[0], wst[1], sb(s2, "wst2", [128, 8, 128], F32), sb(s2, "wst3", [128, 8, 128], F32)]
            wst4_b = [wst_b[0], wst_b[1], Buf(), Buf()]
            accS = [sb(s2, "accS%d" % i, [128, 512], F32) for i in range(4)]
            accS_b = [Buf() for _ in range(4)]
            Ecorr2 = sb(s2, "Ecorr2", [128, 2, ncorr], BF16)
            Ecorr_b = Buf()
            bg = []

            def drain(k):
                for _ in range(k):
                    if bg:
                        bg.pop(0)()

            def drain_all():
                while bg:
                    bg.pop(0)()

            def prep_weights(u, defer):
                groups = [(u["qcol"], 128), (u["kcol"], u["kw"]), (u["vcol"], u["vw"]), (u["gcol"], 128)]
                for gi, (col, wd) in enumerate(groups):
                    kb.dma("sp", wst4[gi][:, :, 0:wd], win[:, :, col:col + wd], writes=[wst4_b[gi]])
                    for dc in range(8):
                        def cast(gi=gi, dc=dc, wd=wd):
                            kb.op("dve", lambda e: e.tensor_scalar(
                                out=wbf[:, dc, gi * 128:gi * 128 + wd], in0=wst4[gi][:, dc, 0:wd],
                                scalar1=par[:, lng + dc:lng + dc + 1], scalar2=None, op0=ALU.mult),
                                reads=[wst4_b[gi], par_b], writes=[wbf_b])
                        if defer:
                            bg.append(cast)
                        else:
                            cast()

            def mm_group(pdst, M, c0, t0):
                def f(pe):
                    ins = None
                    for dc in range(8):
                        ins = mm(pe, pdst[0:M, :], wbf[:, dc, c0:c0 + M], xnT[:, dc, t0:t0 + 512], dc == 0, dc == 7)
                    return ins
                return f

            def emit_projection(u):
                dq = DQ[u["pq"]]
                kw, vw, pk = u["kw"], u["vw"], u["pk"]

                def stageA(tb):
                    p = tb % 2
                    t0 = tb * 512
                    xb_ = xnT_b[tb]
                    kb.op("pe", mm_group(ps[0 + 3 * p], 128, 0, t0), reads=[wbf_b, xb_], writes=[ps_b[0 + 3 * p]])
                    kb.op("pe", mm_group(ps[1 + 3 * p], kw, 128, t0), reads=[wbf_b, xb_], writes=[ps_b[1 + 3 * p]])
                    pv_, pv_b_ = ps[2 + 3 * p], ps_b[2 + 3 * p]

                    def vproj(pe):
                        ins = None
                        for j in range(4):
                            for dc in range(8):
                                ins = mm(pe, pv_[:, j * vw:(j + 1) * vw], xnT[:, dc, t0 + j * 128:t0 + (j + 1) * 128],
                                         wbf[:, dc, 256:256 + vw], dc == 0, dc == 7)
                        return ins
                    kb.op("pe", vproj, reads=[wbf_b, xb_], writes=[pv_b_])
                    pvv = pv_[:, 0:4 * vw].rearrange("p (j c) -> p j c", j=4)
                    hi = pvv[:, :, 64:128] if vw == 128 else pvv[:, :, 0:64]
                    kb.op("dve", lambda e: e.tensor_copy(out=Vaug[:, 4 * tb:4 * tb + 4, 0, 0:64], in_=pvv[:, :, 0:64]),
                          reads=[pv_b_], writes=[Vaug_b])
                    kb.op("dve", lambda e: e.tensor_copy(out=Vaug[:, 4 * tb:4 * tb + 4, 1, 64:128], in_=hi),
                          reads=[pv_b_], writes=[Vaug_b])

                def stageB(tb):
                    p = tb % 2
                    t0 = tb * 512
                    specs = [(ps[0 + 3 * p], ps_b[0 + 3 * p], 128, (lambda rows: der[rows, dq:dq + 1]), qT, qT_b, 6, 0),
                             (ps[1 + 3 * p], ps_b[1 + 3 * p], kw, (lambda rows: par[rows, pk:pk + 1]), kT, kT_b, 7, 1)]
                    for (pq_, pq_b_, R, gcolap, dst, dst_b, sbank, ri) in specs:
                        kb.op("act", lambda e, pq_=pq_, R=R, ri=ri: e.activation(out=sqt[ri][0:R, :], in_=pq_[0:R, :], func=AF.Square),
                              reads=[pq_b_], writes=[sqt_b[ri]])
                    for (pq_, pq_b_, R, gcolap, dst, dst_b, sbank, ri) in specs:
                        kb.op("pe", lambda pe, R=R, ri=ri, sbank=sbank: mm(pe, ps[sbank][0:R, :], bones[0:R, 0:R], sqt[ri][0:R, :], True, True),
                              reads=[sqt_b[ri], cst_b], writes=[ps_b[sbank]])
                    for (pq_, pq_b_, R, gcolap, dst, dst_b, sbank, ri) in specs:
                        kb.op("act", lambda e, R=R, ri=ri, sbank=sbank: e.activation(out=lnv[ri][0:R, :], in_=ps[sbank][0:R, :], func=AF.Ln,
                                                                                       scale=1.0 / 64, bias=EPS),
                              reads=[ps_b[sbank]], writes=[lnv_b[ri]])
                    for (pq_, pq_b_, R, gcolap, dst, dst_b, sbank, ri) in specs:
                        kb.op("act", lambda e, R=R, ri=ri: e.activation(out=rstd[ri][0:R, :], in_=lnv[ri][0:R, :], func=AF.Exp, scale=-0.5),
                              reads=[lnv_b[ri]], writes=[rstd_b[ri]])
                    for (pq_, pq_b_, R, gcolap, dst, dst_b, sbank, ri) in specs:
                        for m in range(R // 64):
                            rows = slice(64 * m, 64 * m + 64)
                            kb.op("dve", lambda e, m=m, rows=rows, pq_=pq_, gcolap=gcolap, dst=dst, ri=ri: e.scalar_tensor_tensor(
                                out=dst[m][0:64, t0:t0 + 512], in0=pq_[rows, :], scalar=gcolap(rows), in1=rstd[ri][rows, :],
                                op0=ALU.mult, op1=ALU.mult), reads=[pq_b_, rstd_b[ri], par_b, der_b], writes=[dst_b[m]])

                stageA(0)
                for tb in range(NBLK):
                    if tb + 1 < NBLK:
                        stageA(tb + 1)
                    stageB(tb)
                for tb in range(NBLK):
                    t0 = tb * 512
                    bi = tb % 6
                    kb.op("pe", mm_group(ps[bi], 128, 384, t0), reads=[wbf_b, xnT_b[tb]], writes=[ps_b[bi]])
                    kb.op("act", lambda e, bi=bi, t0=t0: e.activation(out=sg[:, t0:t0 + 512], in_=ps[bi][:, :], func=AF.Silu),
                          reads=[ps_b[bi]], writes=[sg_b])

            if units:
                prep_weights(units[0], False)
            for ui, u in enumerate(units):
                kind = u["kind"]
                drain_all()
                if layer == 0:
                    for m in range(2):
                        if kind == "A":
                            srcq = augA_d[u["heads"][0], 0:6, :]
                            srck = augA_d[u["heads"][0], 6:12, :]
                            rd = []
                        else:
                            srcq = augB_d[u["heads"][m], 0:6, :]
                            srck = augB_d[u["heads"][m], 6:12, :]
                            rd = [augB_b]
                        kb.dma("sp", qT[m][64:70, :], srcq, reads=rd, writes=[qT_b[m]])
                        kb.dma("sp", kT[m][64:70, :], srck, reads=rd, writes=[kT_b[m]])
                    if kind == "A":
                        kb.dma("sp", corr[0][:, 0:128], corrA_d[u["heads"][0], :, :], writes=[corr_b[0]])
                    else:
                        kb.dma("sp", corr[0][:, 0:128], corrB_d[:, :], writes=[corr_b[0]])
                else:
                    for m in range(2):
                        h = u["heads"][m]
                        if kind == "C":
                            kb.dma("sp", corr[m][:, 0:256], corrC_d[h, :, :], writes=[corr_b[m]])
                            nE = 256
                        else:
                            kb.dma("sp", corr[m][:, 0:640], relD_d[h, :, :], writes=[corr_b[m]])
                            kb.op("dve", lambda e, m=m: e.tensor_tensor(out=corr[m][:, :], in0=corr[m][:, :], in1=maskD[:, :],
                                                                        op=ALU.add), reads=[maskD_b], writes=[corr_b[m]])
                            nE = 640
                        kb.op("act", lambda e, m=m, nE=nE: e.activation(out=Ecorr2[:, m, 0:nE], in_=corr[m][:, 0:nE], func=AF.Exp),
                              reads=[corr_b[m]], writes=[Ecorr_b])
                emit_projection(u)
                if ui + 1 < len(units):
                    prep_weights(units[ui + 1], True)
                if kind == "A":
                    maps = [dict(q=0, k=0, pv=[(0, 0), (1, 1)], corr=0), dict(q=1, k=1, pv=[(0, 2), (1, 3)], corr=0)]
                    nacc = 4
                elif kind == "C":
                    maps = [dict(q=0, k=0, pv=[(0, 0)], corr=0), dict(q=1, k=0, pv=[(1, 1)], corr=1)]
                    nacc = 2
                else:
                    maps = [dict(q=0, k=0, pv=[(0, 0)], corr=0), dict(q=1, k=1, pv=[(1, 1)], corr=0 if layer == 0 else 1)]
                    nacc = 2
                dbl = (nacc == 2 and layer == 0)
                evac = (nacc == 2 and layer == 1)
                sbanks = list(range(2, 8)) if evac else list(range(4, 8))
                nS = len(sbanks)
                skew = 3 if layer == 0 else 5
                for qb in range(NBLK):
                    Q0 = qb * 512
                    abase = 2 * (qb % 2) if dbl else 0
                    items = []
                    for (kt, q0, q1, c0) in tiles_for(kind, qb):
                        if kind == "C":
                            items.append((kt, q0, q1, c0, [0, 1]))
                        else:
                            for mi in range(len(maps)):
                                items.append((kt, q0, q1, c0, [mi]))
                    started = set()

                    def emit_qk(w):
                        kt, q0, q1, c0, mlist = items[w]
                        wd = q1 - q0
                        bi = sbanks[w % nS]
                        Sx, Sb = ps[bi], ps_b[bi]
                        pi = w % NP_
                        packed = (kind == "C")
                        for idx, mi in enumerate(mlist):
                            mp = maps[mi]
                            pb = idx * wd if packed else q0
                            kb.op("pe", lambda pe, mp=mp, pb=pb: mm(pe, Sx[:, pb:pb + wd], kT[mp["k"]][0:KK, kt * 128:(kt + 1) * 128],
                                                                  qT[mp["q"]][0:KK, Q0 + q0:Q0 + q1], True, True),
                                  reads=[kT_b[mp["k"]], qT_b[mp["q"]]], writes=[Sb])
                        lo = 0 if packed else q0
                        hi = len(mlist) * wd if packed else q1
                        cm = maps[mlist[0]]["corr"]
                        if c0 is not None and layer == 0:
                            a0, a1 = q0, q0 + 128
                            cap = corr[cm][:, 0:128]
                            kb.op("dve", lambda e: e.tensor_tensor(out=Sx[:, a0:a1], in0=Sx[:, a0:a1], in1=cap, op=ALU.add),
                                  reads=[corr_b[cm]], writes=[Sb])
                        kb.op("act", lambda e: e.activation(out=Pt[pi][:, lo:hi], in_=Sx[:, lo:hi], func=AF.Exp),
                              reads=[Sb], writes=[Pt_b[pi]])
                        if c0 is not None and layer == 1:
                            if packed:
                                pv3 = Pt[pi][:, lo:hi].rearrange("p (m c) -> p m c", m=2)
                                eap = Ecorr2[:, :, c0:c0 + wd]
                                kb.op("dve", lambda e: e.tensor_tensor(out=pv3, in0=pv3, in1=eap, op=ALU.mult),
                                      reads=[Ecorr_b], writes=[Pt_b[pi]])
                            else:
                                eap = Ecorr2[:, cm, c0:c0 + wd]
                                kb.op("dve", lambda e: e.tensor_tensor(out=Pt[pi][:, q0:q1], in0=Pt[pi][:, q0:q1], in1=eap, op=ALU.mult),
                                      reads=[Ecorr_b], writes=[Pt_b[pi]])

                    def emit_pv(w):
                        kt, q0, q1, c0, mlist = items[w]
                        wd = q1 - q0
                        pi = w % NP_
                        packed = (kind == "C")
                        for idx, mi in enumerate(mlist):
                            mp = maps[mi]
                            pb = idx * wd if packed else q0
                            for (slot, ai) in mp["pv"]:
                                first = ai not in started
                                started.add(ai)
                                kb.op("pe", lambda pe, slot=slot, ai=ai, first=first, pb=pb: mm(
                                    pe, ps[abase + ai][:, q0:q1], Vaug[:, kt, slot, :], Pt[pi][:, pb:pb + wd], first, True, skip=True),
                                    reads=[Pt_b[pi], Vaug_b], writes=[ps_b[abase + ai]])

                    n = len(items)
                    for w in range(n + skew):
                        if w < n:
                            emit_qk(w)
                        if w - skew >= 0:
                            emit_pv(w - skew)
                        drain(1)
                    drain_all()

                    mxi = qb % 2
                    steps = []
                    if kind == "A":
                        for a in range(4):
                            en = "act" if a % 2 == 0 else "dve"
                            if en == "act":
                                kb.op("act", lambda e, a=a: e.copy(out=accS[a][:, :], in_=ps[a][:, :]), reads=[ps_b[a]], writes=[accS_b[a]])
                            else:
                                kb.op("dve", lambda e, a=a: e.tensor_copy(out=accS[a][:, :], in_=ps[a][:, :]), reads=[ps_b[a]],
                                      writes=[accS_b[a]])
                        for a in range(4):
                            c, half = a // 2, a % 2
                            orow = slice(0, 64) if half == 0 else slice(64, 128)
                            lrow = slice(64, 128) if half == 0 else slice(0, 64)
                            steps.append(lambda a=a, c=c, orow=orow, lrow=lrow: kb.op("act", lambda e: e.activation(
                                out=lnt[c][orow, :], in_=accS[a][lrow, :], func=AF.Ln), reads=[accS_b[a]], writes=[lnt_b[c]]))
                        for c in range(2):
                            steps.append(lambda c=c: kb.op("act", lambda e: e.activation(out=rl[c][:, :], in_=lnt[c][:, :], func=AF.Exp,
                                                                                      scale=-1.0), reads=[lnt_b[c]], writes=[rl_b[c]]))
                        for a in range(4):
                            c, half = a // 2, a % 2
                            orow = slice(0, 64) if half == 0 else slice(64, 128)
                            steps.append(lambda a=a, c=c, orow=orow: kb.op("dve", lambda e: e.tensor_tensor(
                                out=on[c][orow, :], in0=accS[a][orow, :], in1=rl[c][orow, :], op=ALU.mult),
                                reads=[accS_b[a], rl_b[c]], writes=[on_b[c]]))
                        steps.append(lambda: kb.op("dve", lambda e: e.scalar_tensor_tensor(
                            out=lnt[0][:, :], in0=on[1][:, :], scalar=der[:, 5:6], in1=on[0][:, :], op0=ALU.mult, op1=ALU.add),
                            reads=[on_b[0], on_b[1], der_b], writes=[lnt_b[0]]))
                        steps.append(lambda: kb.op("act", lambda e: e.activation(out=sqt[0][:, :], in_=lnt[0][:, :], func=AF.Square),
                                                   reads=[lnt_b[0]], writes=[sqt_b[0]]))
                        pbank = sbanks[(n + 1) % nS]
                        def ss_step(pbank=pbank):
                            kb.op("pe", lambda pe: mm(pe, ps[pbank][:, :], aones[:, :], sqt[0][:, :], True, True),
                                  reads=[sqt_b[0], cst_b], writes=[ps_b[pbank]])
                            kb.op("act", lambda e: e.activation(out=lnv[0][:, :], in_=ps[pbank][:, :], func=AF.Ln,
                                                                scale=1.0 / 128, bias=EPS),
                                  reads=[ps_b[pbank]], writes=[lnv_b[0]])
                        steps.append(ss_step)
                        steps.append(lambda: kb.op("act", lambda e: e.activation(out=rstd[0][:, :], in_=lnv[0][:, :], func=AF.Exp, scale=-0.5),
                                                   reads=[lnv_b[0]], writes=[rstd_b[0]]))
                        steps.append(lambda: kb.op("dve", lambda e: e.scalar_tensor_tensor(
                            out=on[0][:, :], in0=lnt[0][:, :], scalar=der[:, 4:5], in1=rstd[0][:, :], op0=ALU.mult, op1=ALU.mult),
                            reads=[lnt_b[0], rstd_b[0], der_b], writes=[on_b[0]]))
                        steps.append(lambda mxi=mxi, Q0=Q0: kb.op("pool", lambda e: e.tensor_tensor(
                            out=mx[mxi][:, :], in0=on[0][:, :], in1=sg[:, Q0:Q0 + 512], op=ALU.mult),
                            reads=[on_b[0], sg_b], writes=[mx_b[mxi]]))
                    else:
                        fi = qb % 2
                        if evac:
                            kb.op("act", lambda e: e.copy(out=accS[0][:, :], in_=ps[0][:, :]), reads=[ps_b[0]], writes=[accS_b[0]])
                            kb.op("dve", lambda e: e.tensor_copy(out=accS[1][:, :], in_=ps[1][:, :]), reads=[ps_b[1]], writes=[accS_b[1]])
                            srcs, srcs_b = accS, accS_b
                        else:
                            srcs, srcs_b = ps, ps_b
                        for m in range(2):
                            orow = slice(0, 64) if m == 0 else slice(64, 128)
                            lrow = slice(64, 128) if m == 0 else slice(0, 64)
                            am = abase + m
                            if kind == "C":
                                h = u["heads"][m]
                                steps.append(lambda am=am, orow=orow, lrow=lrow, h=h, fi=fi, srcs=srcs, srcs_b=srcs_b: kb.op("act", lambda e: e.activation(
                                    out=lnt[fi][orow, :], in_=srcs[am][lrow, :], func=AF.Ln, scale=1.0, bias=esink[lrow, h:h + 1]),
                                    reads=[srcs_b[am], der_b], writes=[lnt_b[fi]]))
                            else:
                                steps.append(lambda am=am, orow=orow, lrow=lrow, fi=fi, srcs=srcs, srcs_b=srcs_b: kb.op("act", lambda e: e.activation(
                                    out=lnt[fi][orow, :], in_=srcs[am][lrow, :], func=AF.Ln), reads=[srcs_b[am]], writes=[lnt_b[fi]]))
                        steps.append(lambda fi=fi: kb.op("act", lambda e: e.activation(out=rl[fi][:, :], in_=lnt[fi][:, :], func=AF.Exp,
                                                                                        scale=-1.0), reads=[lnt_b[fi]], writes=[rl_b[fi]]))
                        steps.append(lambda fi=fi, Q0=Q0: kb.op("pool", lambda e: e.tensor_tensor(
                            out=on[fi][:, :], in0=rl[fi][:, :], in1=sg[:, Q0:Q0 + 512], op=ALU.mult),
                            reads=[rl_b[fi], sg_b], writes=[on_b[fi]]))
                        for m in range(2):
                            orow = slice(0, 64) if m == 0 else slice(64, 128)
                            am = abase + m
                            steps.append(lambda am=am, orow=orow, fi=fi, mxi=mxi, srcs=srcs, srcs_b=srcs_b: kb.op("dve", lambda e: e.tensor_tensor(
                                out=mx[mxi][orow, :], in0=srcs[am][orow, :], in1=on[fi][orow, :], op=ALU.mult),
                                reads=[srcs_b[am], on_b[fi]], writes=[mx_b[mxi]]))
                    steps.append(lambda mxi=mxi, Q0=Q0, mix0=u["mix0"], qb=qb: kb.dma(
                        "pool", mixT_d[mix0:mix0 + 128, Q0:Q0 + 512], mx[mxi][:, :], reads=[mx_b[mxi]], writes=[mixblk_b[qb]]))
                    bg.extend(steps)
            drain_all()
            kb.barrier()

    def bprep():
        win = win_d[0].rearrange("(c p) n -> p c n", p=128)
        with contextlib.ExitStack() as s3:
            wst = [sb(s3, "wstb", [128, 8, 8], F32)]
            wst_b = [Buf()]
            wfb = sb(s3, "wfb", [128, 8, 8], BF16)
            wfb_b = Buf()
            l1p = sb(s3, "l1p", [8, S], F32)
            l1p_b = Buf()
            ones8 = sb(s3, "ones8", [8, S], F32)
            ones8_b = Buf()
            cum = sb(s3, "cum", [8, S], F32)
            cum_b = Buf()
            parts = [sb(s3, "part%d" % i, [8, S], BF16) for i in range(3)]
            nparts = [sb(s3, "npart%d" % i, [8, S], BF16) for i in range(3)]
            parts_b = Buf()
            res = sb(s3, "resid", [8, S], F32)
            res_b = Buf()
            kb.dma("sp", wst[0][:, :, 0:8], win[:, :, 4096:4104], writes=[wst_b[0]])
            for dc in range(8):
                kb.op("pool", lambda e, dc=dc: e.tensor_scalar(out=wfb[:, dc, :], in0=wst[0][:, dc, 0:8],
                                                               scalar1=par[:, PC_LNG0 + dc:PC_LNG0 + dc + 1], scalar2=None, op0=ALU.mult),
                      reads=[wst_b[0], par_b], writes=[wfb_b])
            kb.op("pool", lambda e: e.memset(ones8[:, :], 1.0), writes=[ones8_b])
            ones8h = sb(s3, "ones8h", [8, S], BF16)
            kb.op("pool", lambda e: e.memset(ones8h[:, :], 1.0), writes=[ones8_b])
            for tb in range(NBLK):
                t0 = tb * 512
                pf, pf_b = psum_next()

                def f(pe, t0=t0):
                    ins = None
                    for dc in range(8):
                        ins = mm(pe, pf[0:8, :], wfb[:, dc, :], xnT[:, dc, t0:t0 + 512], dc == 0, dc == 7)
                    return ins
                kb.op("pe", f, reads=[wfb_b, xnT_b[tb]], writes=[pf_b])
                kb.op("act", lambda e, t0=t0: e.activation(out=l1p[:, t0:t0 + 512], in_=pf[0:8, :], func=AF.Exp, scale=-1.0,
                                                          bias=der[0:8, 6:7]), reads=[pf_b, der_b], writes=[l1p_b])
                kb.op("act", lambda e, t0=t0: e.activation(out=l1p[:, t0:t0 + 512], in_=l1p[:, t0:t0 + 512], func=AF.Ln, scale=1.0,
                                                          bias=1.0), reads=[], writes=[l1p_b])
            kb.op("dve", lambda e: e.tensor_tensor_scan(out=cum[:, :], data0=ones8[:, :], data1=l1p[:, :], initial=0.0,
                                                        op0=ALU.mult, op1=ALU.add), reads=[ones8_b, l1p_b], writes=[cum_b])
            kb.op("dve", lambda e: e.tensor_copy(out=parts[0][:, :], in_=cum[:, :]), reads=[cum_b], writes=[parts_b])
            kb.op("dve", lambda e: e.tensor_tensor(out=res[:, :], in0=cum[:, :], in1=parts[0][:, :], op=ALU.subtract),
                  reads=[cum_b], writes=[res_b, parts_b])
            kb.op("dve", lambda e: e.tensor_copy(out=parts[1][:, :], in_=res[:, :]), reads=[res_b], writes=[parts_b])
            kb.op("dve", lambda e: e.tensor_tensor(out=res[:, :], in0=res[:, :], in1=parts[1][:, :], op=ALU.subtract),
                  reads=[], writes=[res_b, parts_b])
            kb.op("dve", lambda e: e.tensor_copy(out=parts[2][:, :], in_=res[:, :]), reads=[res_b], writes=[parts_b])
            for i in range(3):
                kb.op("dve", lambda e, i=i: e.tensor_scalar(out=nparts[i][:, :], in0=parts[i][:, :], scalar1=-1.0, scalar2=None,
                                                            op0=ALU.mult), reads=[], writes=[parts_b])
            first = True
            for r in range(3):
                kb.dma("pool", augB_d[:, r, :], ones8h[:, :], reads=[ones8_b], writes=[augB_b], par=not first)
                first = False
                kb.dma("pool", augB_d[:, 9 + r, :], ones8h[:, :], reads=[ones8_b], writes=[augB_b], par=True)
                kb.dma("pool", augB_d[:, 3 + r, :], nparts[r][:, :], reads=[parts_b], writes=[augB_b], par=True)
                kb.dma("pool", augB_d[:, 6 + r, :], parts[r][:, :], reads=[parts_b], writes=[augB_b], par=True)
            kb.barrier()

    def outproj_phase(layer):
        res_src = x_d if layer == 0 else x1_d
        last = (layer == nlayers - 1)
        with contextlib.ExitStack() as s4:
            wo = sb(s4, "wo", [128, 8, DM], BF16)
            wo_b = Buf()
            wos = [sb(s4, "wos%d" % i, [128, DM], F32) for i in range(2)]
            wos_b = [Buf() for _ in range(2)]
            mt = [sb(s4, "mt%d" % i, [128, 8, 512], BF16) for i in range(2)]
            mt_b = [Buf() for _ in range(2)]
            xr = [sb(s4, "xr%d" % i, [128, DM], F32) for i in range(2)]
            xr_b = [Buf() for _ in range(2)]
            xo = [sb(s4, "xo%d" % i, [128, DM], F32) for i in range(3)]
            xo_b = [Buf() for _ in range(3)]
            nb = norm_bufs(s4, "n1") if not last else None
            wod = wout_d[layer].rearrange("(c p) n -> p c n", p=128)
            for mc in range(8):
                i = mc % 2
                kb.dma("sp", wos[i][:, :], wod[:, mc, :], writes=[wos_b[i]])
                kb.op("pool", lambda e, i=i, mc=mc: e.tensor_copy(out=wo[:, mc, :], in_=wos[i][:, :]), reads=[wos_b[i]], writes=[wo_b])
            mixv = mixT_d.rearrange("(c p) t -> p c t", p=128)
            for tb in range(NBLK):
                bi = tb % 2
                kb.dma("sp", mt[bi][:, :, :], mixv[:, :, tb * 512:(tb + 1) * 512], reads=[mixblk_b[tb]], writes=[mt_b[bi]])
                for j in range(4):
                    tt = 4 * tb + j
                    i = tt % 2
                    io = tt % 3
                    rd = [x1_b] if layer == 1 else []
                    kb.dma("sp", xr[i][:, :], res_src[tt * 128:(tt + 1) * 128, :], reads=rd, writes=[xr_b[i]])
                    for half in range(2):
                        po, po_b = psum_next()

                        def f(pe, po=po, half=half, j=j, bi=bi):
                            ins = None
                            for mc in range(8):
                                ins = mm(pe, po[:, :], mt[bi][:, mc, j * 128:(j + 1) * 128], wo[:, mc, half * 512:(half + 1) * 512],
                                         mc == 0, mc == 7)
                            return ins
                        kb.op("pe", f, reads=[mt_b[bi], wo_b], writes=[po_b])
                        kb.op("dve", lambda e, po=po, half=half, i=i, io=io: e.tensor_tensor(
                            out=xo[io][:, half * 512:(half + 1) * 512], in0=po[:, :], in1=xr[i][:, half * 512:(half + 1) * 512], op=ALU.add),
                            reads=[po_b, xr_b[i]], writes=[xo_b[io]])
                    if last:
                        kb.dma("pool", y_d[tt * 128:(tt + 1) * 128, :], xo[io][:, :], reads=[xo_b[io]], writes=[y_b])
                    else:
                        kb.dma("pool", x1_d[tt * 128:(tt + 1) * 128, :], xo[io][:, :], reads=[xo_b[io]], writes=[x1_b])
                        norm_s1(nb, xo[io][:, :], xo_b[io], tt)
                        if tt >= 1:
                            jo = (tt - 1) % 3
                            norm_s2(nb, xo[jo][:, :], xo_b[jo], tt - 1)
                        if tt >= 2:
                            norm_s3(nb, tt - 2)
            if not last:
                jo = (NT - 1) % 3
                norm_s2(nb, xo[jo][:, :], xo_b[jo], NT - 1)
                norm_s3(nb, NT - 2)
                norm_s3(nb, NT - 1)
            kb.barrier()

    for layer in range(nlayers):
        if layer == 0:
            bprep()
        units_phase(layer)
        outproj_phase(layer)
    kb.barrier()
    return nc


def _consts():
    bf = ml_dtypes.bfloat16
    c = {}
    c["ident"] = np.eye(128, dtype=np.float32).astype(bf)
    bo = np.zeros((128, 128), np.float32)
    bo[0:64, 0:64] = 1.0
    bo[64:128, 64:128] = 1.0
    c["bones"] = bo.astype(bf)
    c["aones"] = np.ones((128, 128), np.float32).astype(bf)
    c["onesrow"] = np.ones((8, S), np.float32).astype(bf)
    t = np.arange(S)
    hi = (t // 256) * 256
    lo = t % 256
    augA = np.zeros((4, 12, S), np.float32)
    for h in range(4):
        s = 2.0 ** (-8.0 * (h + 1) / 4)
        augA[h, 0] = 1.0
        augA[h, 1] = 1.0
        augA[h, 3] = -s * hi
        augA[h, 4] = -s * lo
        augA[h, 6] = s * hi
        augA[h, 7] = s * lo
        augA[h, 9] = 1.0
        augA[h, 10] = 1.0
    c["augA"] = augA.astype(bf)
    k = np.arange(128)[:, None]
    i = np.arange(128)[None, :]
    corrA = np.zeros((4, 128, 128), np.float32)
    for h in range(4):
        s = 2.0 ** (-8.0 * (h + 1) / 4)
        m = np.where(k <= i, 0.0, np.where((k // 64) == (i // 64), -2.0 * s * (k - i), NEG))
        corrA[h] = m
    c["corrA"] = corrA
    c["corrB"] = np.where(k <= i, 0.0, NEG).astype(np.float32)
    i2 = np.arange(256)[None, :]
    corrC = np.zeros((8, 128, 256), np.float32)
    for h in range(8):
        s = 2.0 ** (-8.0 * (h + 1) / 8)
        cd = (i2 // 64) - (k // 64)
        corrC[h] = np.where((cd >= 0) & (cd <= 2), -s * np.abs(i2 - k), NEG)
    c["corrC"] = corrC
    i3 = np.arange(640)[None, :]
    cd = (i3 // 64) - (k // 64)
    c["maskD"] = np.where((cd >= 0) & (cd <= 8), 0.0, NEG).astype(np.float32)
    c["ridxD"] = np.clip(i3 - k, -63, 256) + 63
    return c


_C = None


def _host_inputs(inputs):
    global _C
    if _C is None:
        _C = _consts()
    c = _C
    f = lambda a: np.ascontiguousarray(np.asarray(a), dtype=np.float32)
    par = np.zeros((128, NPAR), np.float32)
    par[:, PC_LNG0:PC_LNG0 + 8] = f(inputs["even_ln_g"])[0].reshape(8, 128).T
    par[:, PC_LNG1:PC_LNG1 + 8] = f(inputs["odd_ln_g"])[0].reshape(8, 128).T
    for col, name in ((PC_AQ, "a_q_norm_g"), (PC_AK, "a_k_norm_g"), (PC_BQ, "b_q_norm_g"), (PC_BK, "b_k_norm_g"),
                      (PC_CQ, "c_q_norm_g"), (PC_CK, "c_k_norm_g"), (PC_DQ, "d_q_norm_g"), (PC_DK, "d_k_norm_g")):
        par[:, col] = np.tile(f(inputs[name])[0], 2)
    par[:, PC_SUB] = f(inputs["a_subln_g"])[0]
    par[0:8, PC_FB] = f(inputs["b_forget_bias"])[0]
    par[:, PC_SINK:PC_SINK + 8] = np.broadcast_to(f(inputs["c_sinks"])[0], (128, 8))
    for col, name in ((PC_LQ1, "a_lambda_q1"), (PC_LK1, "a_lambda_k1"), (PC_LQ2, "a_lambda_q2"), (PC_LK2, "a_lambda_k2")):
        par[:, col:col + 64] = np.broadcast_to(f(inputs[name])[0], (128, 64))
    relD = np.ascontiguousarray(f(inputs["d_rel_bias"])[0][:, c["ridxD"]])
    shared = {
        "w_in0": f(inputs["even_w_in"])[0], "w_in1": f(inputs["odd_w_in"])[0],
        "w_out0": f(inputs["even_w_out"])[0], "w_out1": f(inputs["odd_w_out"])[0],
        "params": par, "ident": c["ident"], "bones": c["bones"], "aones": c["aones"], "augA": c["augA"],
        "onesrow": c["onesrow"], "corrA": c["corrA"], "corrB": c["corrB"], "corrC": c["corrC"], "relD": relD,
        "maskD": c["maskD"],
    }
    return shared


def kernel(**inputs):
    x = np.ascontiguousarray(np.asarray(inputs["x"]), dtype=np.float32)
    shared = _host_inputs(inputs)
    nb = x.shape[0]
    in_maps = []
    for b in range(nb):
        m = dict(shared)
        m["x"] = x[b]
        in_maps.append(m)
    nc = build()
    res = run_bass_kernel_spmd(nc, in_maps, core_ids=list(range(nb)))
    return np.stack([np.asarray(r["y"]) for r in res.results], axis=0).astype(np.float32)
```
